# Optimizing a Trainium2 kernel written in Bass

```python
import math
import jax
import jax.numpy as jnp
from jax import lax
import numpy as np

D_MODEL = 1024
BATCH = 16
SEQ = 2048
DEPTH = 2

CTX_LEN = 256
GRID_W = 64
D_MIX = D_MODEL
W_A = D_MIX // 4
W_B = D_MIX // 4
W_C = D_MIX // 4
W_D = D_MIX - W_A - W_B - W_C
MLSTM_HEADS = 4
MLSTM_HD = W_A // MLSTM_HEADS
MLSTM_CHUNK = 64
SGU_GROUPS = 4
SGU_GD = W_B // SGU_GROUPS
SGU_CHUNK = 128
CONV_WIDTH = 31
CONV_PAD = CONV_WIDTH // 2
S5_P = 16
S5_GROUPS = W_D // S5_P
S5_N = 64
D_FF = -(-8 * D_MODEL // (3 * 256)) * 256
ALPHA = (2 * DEPTH) ** 0.25
BETA = (8 * DEPTH) ** -0.25
LN_EPS = 1e-5
OFF_B = 4 * W_A + 4 * MLSTM_HEADS
OFF_C = OFF_B + 2 * W_B
OFF_D = OFF_C + 2 * W_C
P_IN = OFF_D + W_D

kernel_name = 'hybrid_parallel_mlstm_sgu_conv_s5_dit'


def layer_norm(x, w, b):
    xf = x.astype(jnp.float32)
    mu = jnp.mean(xf, -1, keepdims=True)
    var = jnp.mean(jnp.square(xf - mu), -1, keepdims=True)
    return ((xf - mu) * lax.rsqrt(var + LN_EPS) * w + b).astype(x.dtype)


def swiglu(h, w_in, w_out):
    g, u = jnp.split(h @ w_in, 2, axis=-1)
    return (jax.nn.silu(g) * u) @ w_out


def mlstm_split(z, gate_bias):
    bsz, t, _ = z.shape
    heads = lambda a: a.reshape(bsz, t, MLSTM_HEADS, MLSTM_HD).transpose(0, 2, 1, 3)
    q, k, v, o = (z[..., i * W_A:(i + 1) * W_A] for i in range(4))
    g = (z[..., 4 * W_A:] + gate_bias).astype(jnp.float32)
    g = g.reshape(bsz, t, 4, MLSTM_HEADS).transpose(2, 0, 3, 1)
    gates = (g[0], jax.nn.log_sigmoid(g[1]), g[2], jax.nn.log_sigmoid(g[3]))
    return heads(q), heads(k) * MLSTM_HD ** -0.5, heads(v), o, gates


def mlstm_zero_state(bsz):
    return (jnp.zeros((bsz, MLSTM_HEADS, MLSTM_HD, MLSTM_HD), jnp.float32),
            jnp.zeros((bsz, MLSTM_HEADS, MLSTM_HD), jnp.float32),
            jnp.zeros((bsz, MLSTM_HEADS), jnp.float32))


def mlstm_context_state(k, v, ig, lf):
    k, v = k.astype(jnp.float32), v.astype(jnp.float32)
    cum = jnp.cumsum(lf, axis=-1)
    logw = cum[..., -1:] - cum + ig
    m = jnp.max(logw, axis=-1)
    e = jnp.exp(logw - m[..., None])
    C = jnp.einsum('bht,bhtv,bhtk->bhvk', e, v, k)
    n = jnp.einsum('bht,bhtk->bhk', e, k)
    return (C, n, m)


def mlstm_chunkwise(q, k, v, ig, lf, state):
    bsz, nh, t, d = q.shape
    nc = t // MLSTM_CHUNK
    chunks = lambda a: jnp.moveaxis(a.astype(jnp.float32).reshape(bsz, nh, nc, MLSTM_CHUNK, *a.shape[3:]), 2, 0)
    xs = tuple(chunks(a) for a in (q, k, v, ig, lf))
    lower = jnp.tril(jnp.ones((MLSTM_CHUNK, MLSTM_CHUNK), bool))

    def step(carry, inp):
        C, n, m = carry
        qc, kc, vc, ic, fc = inp
        b = jnp.cumsum(fc, axis=-1)
        logw = jnp.where(lower, b[..., :, None] - b[..., None, :] + ic[..., None, :], -jnp.inf)
        inter = b + m[..., None]
        m_t = jnp.maximum(inter, jnp.max(logw, axis=-1))
        w_inter = jnp.exp(inter - m_t)
        s = jnp.einsum('bhtd,bhsd->bhts', qc, kc) * jnp.exp(logw - m_t[..., None])
        num = jnp.einsum('bhts,bhsd->bhtd', s, vc) + w_inter[..., None] * jnp.einsum('bhvk,bhtk->bhtv', C, qc)
        den = jnp.sum(s, axis=-1) + w_inter * jnp.einsum('bhk,bhtk->bht', n, qc)
        h = num / jnp.maximum(jnp.abs(den), jnp.exp(-m_t))[..., None]
        b_end = b[..., -1]
        logu = b_end[..., None] - b + ic
        m_new = jnp.maximum(b_end + m, jnp.max(logu, axis=-1))
        e = jnp.exp(logu - m_new[..., None])
        decay = jnp.exp(b_end + m - m_new)
        C = decay[..., None, None] * C + jnp.einsum('bhs,bhsv,bhsk->bhvk', e, vc, kc)
        n = decay[..., None] * n + jnp.einsum('bhs,bhsk->bhk', e, kc)
        return (C, n, m_new), h

    state, h = lax.scan(step, state, xs)
    return jnp.moveaxis(h, 0, 2).reshape(bsz, nh, t, d), state


def mlstm_readout(h, o, norm_w):
    bsz, _, t, _ = h.shape
    h = h.transpose(0, 2, 1, 3)
    mu = jnp.mean(h, -1, keepdims=True)
    var = jnp.mean(jnp.square(h - mu), -1, keepdims=True)
    hn = (h - mu) * lax.rsqrt(var + LN_EPS) * norm_w.reshape(MLSTM_HEADS, MLSTM_HD)
    return (hn.reshape(bsz, t, W_A) * jax.nn.sigmoid(o.astype(jnp.float32))).astype(o.dtype)


def mlstm_mixer(zl, zc, gate_bias, norm_w, need_ctx):
    ql, kl, vl, ol, gl = mlstm_split(zl, gate_bias)
    qc, kc, vc, oc, gc = mlstm_split(zc, gate_bias)
    hl_dirs, hc_dirs = [], []
    for d in range(2):
        orient = (lambda a: jnp.flip(a, axis=2)) if d else (lambda a: a)
        if need_ctx:
            hc, state = mlstm_chunkwise(orient(qc), orient(kc), orient(vc), orient(gc[2 * d]),
                                        orient(gc[2 * d + 1]), mlstm_zero_state(zc.shape[0]))
            hc_dirs.append(orient(hc))
        else:
            state = mlstm_context_state(orient(kc), orient(vc), orient(gc[2 * d]), orient(gc[2 * d + 1]))
        hl, _ = mlstm_chunkwise(orient(ql), orient(kl), orient(vl), orient(gl[2 * d]), orient(gl[2 * d + 1]), state)
        hl_dirs.append(orient(hl))
    yl = mlstm_readout(hl_dirs[0] + hl_dirs[1], ol, norm_w)
    yc = mlstm_readout(hc_dirs[0] + hc_dirs[1], oc, norm_w) if need_ctx else None
    return yl, yc


def sgu_mixer(z, ln_w, ln_b, w_s, b_s):
    bsz, t, _ = z.shape
    u = jax.nn.gelu(z[..., :W_B])
    v = layer_norm(jax.nn.gelu(z[..., W_B:]), ln_w, ln_b)
    v = v.reshape(bsz, t // SGU_CHUNK, SGU_CHUNK, SGU_GROUPS, SGU_GD)
    v = jnp.einsum('gts,bnsgc->bntgc', w_s, v) + b_s.T[:, :, None]
    return u * v.reshape(bsz, t, W_B)


def conv_mixer(z, w, b, ln_w, ln_b, rows):
    a = z[..., :W_C] * jax.nn.sigmoid(z[..., W_C:])
    bsz, t, ch = a.shape
    a = a.reshape(bsz * rows, t // rows, ch)
    a = lax.conv_general_dilated(a, w[:, None, :], window_strides=(1,), padding=[(CONV_PAD, CONV_PAD)],
                                 dimension_numbers=('NWC', 'WIO', 'NWC'), feature_group_count=ch)
    a = a.reshape(bsz, t, ch) + b
    return jax.nn.silu(layer_norm(a, ln_w, ln_b))


def s5_discretize(a_re, a_im, log_dt, b_re, b_im):
    lam = lax.complex(a_re.astype(jnp.float32), a_im.astype(jnp.float32))
    lam_dt = lam * jnp.exp(log_dt.astype(jnp.float32))[:, None]
    a_bar = jnp.exp(lam_dt)
    b_bar = ((a_bar - 1.0) / lam)[..., None] * lax.complex(b_re.astype(jnp.float32), b_im.astype(jnp.float32))
    return lam_dt, a_bar, b_bar


def s5_scan(bu, a_bar):
    def combine(left, right):
        a_l, b_l = left
        a_r, b_r = right
        return a_r * a_l, a_r * b_l + b_r
    a = jnp.broadcast_to(a_bar, (1,) + bu.shape[1:])
    return lax.associative_scan(combine, (a, bu), axis=1)[1]


def s5_final_state(bu, lam_dt):
    t = bu.shape[1]
    lags = jnp.arange(t - 1, -1, -1, dtype=jnp.float32)
    return jnp.einsum('tgn,btgn->bgn', jnp.exp(lags[:, None, None] * lam_dt), bu)


def s5_mixer(zl, zc, a_re, a_im, log_dt, b_re, b_im, c_re, c_im, d_skip, glu_w, glu_b, need_ctx):
    grouped = lambda z: z.astype(jnp.float32).reshape(z.shape[0], z.shape[1], S5_GROUPS, S5_P)
    ul, uc = grouped(zl), grouped(zc)
    xl_dirs, xc_dirs = [], []
    for d in range(2):
        orient = (lambda a: jnp.flip(a, axis=1)) if d else (lambda a: a)
        lam_dt, a_bar, b_bar = s5_discretize(a_re[d], a_im[d], log_dt[d], b_re, b_im)
        bul = orient(jnp.einsum('btgp,gnp->btgn', ul, b_bar))
        buc = orient(jnp.einsum('btgp,gnp->btgn', uc, b_bar))
        if need_ctx:
            sc = s5_scan(buc, a_bar)
            h0 = sc[:, -1]
            xc_dirs.append(orient(sc))
        else:
            h0 = s5_final_state(buc, lam_dt)
        sl = s5_scan(bul.at[:, 0].add(a_bar * h0), a_bar)
        xl_dirs.append(orient(sl))
    c_mat = lax.complex(c_re.astype(jnp.float32), c_im.astype(jnp.float32))
    d_mat = d_skip.astype(jnp.float32).reshape(S5_GROUPS, S5_P)

    def readout(states, u, dtype):
        y = jnp.real(jnp.einsum('gpn,btgn->btgp', c_mat, states)) + d_mat * u
        y = jax.nn.gelu(y.reshape(u.shape[0], u.shape[1], W_D))
        return (y * jax.nn.sigmoid(y @ glu_w.astype(jnp.float32) + glu_b.astype(jnp.float32))).astype(dtype)

    yl = readout(xl_dirs[0] + xl_dirs[1], ul, zl.dtype)
    yc = readout(xc_dirs[0] + xc_dirs[1], uc, zc.dtype) if need_ctx else None
    return yl, yc


def setup_inputs(seed: int = 0) -> dict:
    key = jax.random.key(seed)
    ks = iter(jax.random.split(key, 48))
    f32 = jnp.float32
    nrm = lambda shape, scale: jax.random.normal(next(ks), shape, f32) * scale
    L, D = DEPTH, D_MODEL
    x = nrm((BATCH, SEQ, D), 1.0)
    c = nrm((BATCH, D), 1.0)
    ctx = nrm((BATCH, CTX_LEN, D), 1.0)
    c_ctx = nrm((D,), 1.0)
    w_mod = nrm((L, D, 6 * D), 0.5 * D ** -0.5)
    b_mod = nrm((L, 6 * D), 0.02)
    w_in = nrm((L, D, P_IN), D ** -0.5)
    ig_b = nrm((L, 2, 1, MLSTM_HEADS), 0.1)
    fg_b = jnp.linspace(3.0, 6.0, MLSTM_HEADS, dtype=f32) + nrm((L, 2, 1, MLSTM_HEADS), 0.1)
    mlstm_gate_bias = jnp.concatenate([ig_b, fg_b], axis=2).reshape(L, 4 * MLSTM_HEADS)
    mlstm_norm_w = 1.0 + nrm((L, W_A), 0.02)
    sgu_ln_w = 1.0 + nrm((L, W_B), 0.02)
    sgu_ln_b = nrm((L, W_B), 0.02)
    sgu_w = nrm((L, SGU_GROUPS, SGU_CHUNK, SGU_CHUNK), SGU_CHUNK ** -0.5)
    sgu_b = 1.0 + nrm((L, SGU_GROUPS, SGU_CHUNK), 0.02)
    conv_w = nrm((L, CONV_WIDTH, W_C), CONV_WIDTH ** -0.5)
    conv_b = nrm((L, W_C), 0.02)
    conv_ln_w = 1.0 + nrm((L, W_C), 0.02)
    conv_ln_b = nrm((L, W_C), 0.02)
    s5_a_re = -0.5 + nrm((L, 2, S5_GROUPS, S5_N), 0.01)
    s5_a_im = math.pi * jnp.arange(S5_N, dtype=f32) + nrm((L, 2, S5_GROUPS, S5_N), 0.01)
    s5_log_dt = jax.random.uniform(next(ks), (L, 2, S5_GROUPS), f32, math.log(1e-3), math.log(1e-1))
    s5_b_re = nrm((L, S5_GROUPS, S5_N, S5_P), (2 * S5_P) ** -0.5)
    s5_b_im = nrm((L, S5_GROUPS, S5_N, S5_P), (2 * S5_P) ** -0.5)
    s5_c_re = nrm((L, S5_GROUPS, S5_P, S5_N), (2 * S5_N) ** -0.5)
    s5_c_im = nrm((L, S5_GROUPS, S5_P, S5_N), (2 * S5_N) ** -0.5)
    s5_d = nrm((L, W_D), 1.0)
    s5_glu_w = nrm((L, W_D, W_D), W_D ** -0.5)
    s5_glu_b = nrm((L, W_D), 0.02)
    w_out = nrm((L, D_MIX, D), BETA * D_MIX ** -0.5)
    ln1_w = 1.0 + nrm((L, D), 0.02)
    ln1_b = nrm((L, D), 0.02)
    w_ffn_in = nrm((L, D, 2 * D_FF), D ** -0.5)
    w_ffn_out = nrm((L, D_FF, D), BETA * D_FF ** -0.5)
    ln2_w = 1.0 + nrm((L, D), 0.02)
    ln2_b = nrm((L, D), 0.02)
    return {'x': x, 'c': c, 'ctx': ctx, 'c_ctx': c_ctx, 'w_mod': w_mod, 'b_mod': b_mod, 'w_in': w_in,
            'mlstm_gate_bias': mlstm_gate_bias, 'mlstm_norm_w': mlstm_norm_w,
            'sgu_ln_w': sgu_ln_w, 'sgu_ln_b': sgu_ln_b, 'sgu_w': sgu_w, 'sgu_b': sgu_b,
            'conv_w': conv_w, 'conv_b': conv_b, 'conv_ln_w': conv_ln_w, 'conv_ln_b': conv_ln_b,
            's5_a_re': s5_a_re, 's5_a_im': s5_a_im, 's5_log_dt': s5_log_dt, 's5_b_re': s5_b_re,
            's5_b_im': s5_b_im, 's5_c_re': s5_c_re, 's5_c_im': s5_c_im, 's5_d': s5_d,
            's5_glu_w': s5_glu_w, 's5_glu_b': s5_glu_b, 'w_out': w_out, 'ln1_w': ln1_w, 'ln1_b': ln1_b,
            'w_ffn_in': w_ffn_in, 'w_ffn_out': w_ffn_out, 'ln2_w': ln2_w, 'ln2_b': ln2_b}


def reference(x, c, ctx, c_ctx, w_mod, b_mod, w_in, mlstm_gate_bias, mlstm_norm_w,
              sgu_ln_w, sgu_ln_b, sgu_w, sgu_b, conv_w, conv_b, conv_ln_w, conv_ln_b,
              s5_a_re, s5_a_im, s5_log_dt, s5_b_re, s5_b_im, s5_c_re, s5_c_im, s5_d,
              s5_glu_w, s5_glu_b, w_out, ln1_w, ln1_b, w_ffn_in, w_ffn_out, ln2_w, ln2_b):
    rows = x.shape[1] // GRID_W
    xc = ctx
    for l in range(DEPTH):
        need_ctx = l < DEPTH - 1
        w_in_l = w_in[l]
        mod = jax.nn.silu(c) @ w_mod[l] + b_mod[l]
        mod_c = jax.nn.silu(c_ctx) @ w_mod[l] + b_mod[l]
        sh1, sc1, g1, sh2, sc2, g2 = jnp.split(mod[:, None, :], 6, axis=-1)
        sh1c, sc1c, g1c, sh2c, sc2c, g2c = jnp.split(mod_c, 6, axis=-1)
        h = x * (1.0 + sc1) + sh1
        hc = xc * (1.0 + sc1c) + sh1c
        z = h @ w_in_l
        zc_a = hc @ w_in_l[:, :OFF_B]
        zc_d = hc @ w_in_l[:, OFF_D:]
        y_a, yc_a = mlstm_mixer(z[..., :OFF_B], zc_a, mlstm_gate_bias[l], mlstm_norm_w[l], need_ctx)
        y_b = sgu_mixer(z[..., OFF_B:OFF_C], sgu_ln_w[l], sgu_ln_b[l], sgu_w[l], sgu_b[l])
        y_c = conv_mixer(z[..., OFF_C:OFF_D], conv_w[l], conv_b[l], conv_ln_w[l], conv_ln_b[l], rows)
        y_d, yc_d = s5_mixer(z[..., OFF_D:], zc_d, s5_a_re[l], s5_a_im[l], s5_log_dt[l], s5_b_re[l], s5_b_im[l],
                             s5_c_re[l], s5_c_im[l], s5_d[l], s5_glu_w[l], s5_glu_b[l], need_ctx)
        y = jnp.concatenate([y_a, y_b, y_c, y_d], axis=-1) @ w_out[l]
        x = layer_norm(ALPHA * x + g1 * y, ln1_w[l], ln1_b[l])
        x = layer_norm(ALPHA * x + g2 * swiglu(x * (1.0 + sc2) + sh2, w_ffn_in[l], w_ffn_out[l]), ln2_w[l], ln2_b[l])
        if need_ctx:
            yc_b = sgu_mixer(hc @ w_in_l[:, OFF_B:OFF_C], sgu_ln_w[l], sgu_ln_b[l], sgu_w[l], sgu_b[l])
            yc_c = conv_mixer(hc @ w_in_l[:, OFF_C:OFF_D], conv_w[l], conv_b[l], conv_ln_w[l], conv_ln_b[l], 1)
            yc = jnp.concatenate([yc_a, yc_b, yc_c, yc_d], axis=-1) @ w_out[l]
            xc = layer_norm(ALPHA * xc + g1c * yc, ln1_w[l], ln1_b[l])
            xc = layer_norm(ALPHA * xc + g2c * swiglu(xc * (1.0 + sc2c) + sh2c, w_ffn_in[l], w_ffn_out[l]),
                            ln2_w[l], ln2_b[l])
    return x
```

```python
import math
from contextlib import ExitStack
import numpy as np
import concourse.bass as bass
import concourse.mybir as mybir
from concourse.bass_utils import run_bass_kernel_spmd

F32 = mybir.dt.float32
BF16 = mybir.dt.bfloat16
AF = mybir.ActivationFunctionType
ALU = mybir.AluOpType

D = 1024
T = 2304
NB = 256
NBLK = 9
NCH = 18
DFF = 2816
NJ = 22
ALPHA = 4.0 ** 0.25
EPS = 1e-5
PI = math.pi
NPP = 104
NBC = 784
SF = 16
SJ = T // SF


class Sync:
    NDMA = 40

    def __init__(self, nc, stack):
        self.nc = nc
        self.eng = {"pe": nc.tensor, "dve": nc.vector, "act": nc.scalar, "pool": nc.gpsimd, "sp": nc.sync}
        self.sem = {k: stack.enter_context(nc.semaphore("s_" + k)) for k in self.eng}
        self.cnt = {k: 0 for k in self.eng}
        self.seen = {k: {} for k in self.eng}
        self.dsem = [stack.enter_context(nc.semaphore("d%d" % i)) for i in range(self.NDMA)]
        self.dcnt = [0] * self.NDMA
        self.dnext = 0
        self.dnext_sw = 0
        self.lw = {}
        self.rd = {}
        self.semobj = {}
        for k in self.eng:
            self.semobj[("e", k)] = self.sem[k]
        for i in range(self.NDMA):
            self.semobj[("d", i)] = self.dsem[i]

    def _wait(self, e, sid, val):
        if self.seen[e].get(sid, 0) >= val:
            return
        if sid[0] == "e" and val > self.cnt[sid[1]]:
            if sid[1] == e:
                return
            raise RuntimeError("wait on unsignalled instruction: %s waits %s >= %d (cnt %d)" % (e, sid, val, self.cnt[sid[1]]))
        self.eng[e].wait_ge(self.semobj[sid], val)
        self.seen[e][sid] = val

    def _deps(self, e, reads, writes):
        for k in reads:
            if k in self.lw:
                self._wait(e, *self.lw[k])
        for k in writes:
            if k in self.lw:
                self._wait(e, *self.lw[k])
            for sid, val in self.rd.get(k, {}).items():
                self._wait(e, sid, val)

    def _record(self, sid, val, reads, writes):
        for k in reads:
            d = self.rd.setdefault(k, {})
            d[sid] = max(d.get(sid, 0), val)
        for k in writes:
            self.lw[k] = (sid, val)
            self.rd[k] = {}

    def op(self, e, fn, reads=(), writes=(), inc=True):
        self._deps(e, reads, writes)
        inst = fn(self.eng[e])
        if inc:
            self.cnt[e] += 1
            inst.then_inc(self.sem[e], 1)
            val = self.cnt[e]
        else:
            val = self.cnt[e] + 1
        self._record(("e", e), val, reads, writes)

    def dma(self, e, out, in_, reads=(), writes=(), **kw):
        half = self.NDMA // 2
        if e == "pool":
            s = half + self.dnext_sw
            self.dnext_sw = (self.dnext_sw + 1) % half
        else:
            s = self.dnext
            self.dnext = (self.dnext + 1) % half
        sid = ("d", s)
        if self.dcnt[s] > 0:
            self._wait(e, sid, self.dcnt[s])
        self._deps(e, reads, writes)
        self.dcnt[s] += 16
        self.eng[e].dma_start(out=out, in_=in_, **kw).then_inc(self.dsem[s], 16)
        self._record(sid, self.dcnt[s], reads, writes)

    def barrier(self):
        for e in self.eng:
            for k in self.eng:
                if k != e and self.cnt[k] > 0:
                    self._wait(e, ("e", k), self.cnt[k])
            for i in range(self.NDMA):
                if self.dcnt[i] > 0:
                    self._wait(e, ("d", i), self.dcnt[i])
        self.lw = {}
        self.rd = {}


def build_nc(nbatch=2, nlayer=2, debug=False):
    nc = bass.Bass("TRN2", target_bir_lowering=False)

    def din(name, shape):
        return nc.dram_tensor(name, list(shape), F32, kind="ExternalInput").ap()

    xin = din("xin", [2, T, D])
    cT_d = din("cT", [128, 8, 3])
    wmod_d = din("w_mod", [2, D, 6 * D])
    bmod_d = din("bmodT", [128, 2, 48])
    winA_d = din("w_inA", [2, 128, 8, 1296])
    winB_d = din("w_inB", [2, 128, 8, 1024])
    wout_d = din("w_outh", [2, 128, 8, 1024])
    w1_d = din("w1h", [2, NJ, 128, 8, 256])
    w2_d = din("w2h", [2, 8, 128, NJ, 128])
    pp_d = din("pp", [2, 128, NPP])
    bc_d = din("bc", [2, NBC])
    sguw_d = din("sguwT", [2, 128, 4, 128])
    sgub_d = din("sgub", [2, 128, 2, 128])
    s5B_d = din("s5B", [2, 128, 2, 2, 128])
    s5A_d = din("s5A", [2, 128, 3, 2, 2, 128])
    s5C_d = din("s5C", [2, 128, 2, 8, 32])
    s5Bn_d = din("s5Bn", [2, 128, 2, 8, 32])
    s5pp_d = din("s5pp", [2, 128, 3, 16])
    glu_d = din("gluh", [2, 128, 2, 256])
    consts_d = din("consts", [128, 3, 128])
    out_d = nc.dram_tensor("out", [2, 2048, D], F32, kind="ExternalOutput").ap()
    w1s = nc.dram_tensor("w1s", [2, NJ, 128, 8 * 256], BF16, kind="Internal").ap()
    w2s = nc.dram_tensor("w2s", [2, 8, 128, NJ * 128], BF16, kind="Internal").ap()
    cWE = nc.dram_tensor("cWE", [2, 128, 16 * 2 * 512], BF16, kind="Internal").ap()
    cG = nc.dram_tensor("cG", [2, 128, 2 * 2 * 16 * 32], BF16, kind="Internal").ap()
    cG0 = nc.dram_tensor("cG0", [2, 128, 64], BF16, kind="Internal").ap()
    cPW = nc.dram_tensor("cPW", [2, 128, 2 * 17 * 16], F32, kind="Internal").ap()
    cPW2 = nc.dram_tensor("cPW2", [2, 128, 2 * 9 * 16], F32, kind="Internal").ap()
    cNPW2 = nc.dram_tensor("cNPW2", [2, 128, 9 * 16], F32, kind="Internal").ap()
    if debug:
        dbg_mix = nc.dram_tensor("dbg_mix", [128, 8, T], F32, kind="ExternalOutput").ap()
        dbg_x = nc.dram_tensor("dbg_x", [128, 8, T], F32, kind="ExternalOutput").ap()
        dbg_mod = nc.dram_tensor("dbg_mod", [128, 2, 48, 3], F32, kind="ExternalOutput").ap()
        dbg_h = nc.dram_tensor("dbg_h", [128, 8, NB], F32, kind="ExternalOutput").ap()
        dbg_qk = nc.dram_tensor("dbg_qk", [128, 4, T], F32, kind="ExternalOutput").ap()
        dbg_s5u = nc.dram_tensor("dbg_s5u", [128, 2, T], F32, kind="ExternalOutput").ap()
        dbg_xT0 = nc.dram_tensor("dbg_xT0", [128, 8, T], F32, kind="ExternalOutput").ap()

    with ExitStack() as G:
        S = Sync(nc, G)

        uid = [0]

        def tile(st, name, shape, dt=F32):
            uid[0] += 1
            return st.enter_context(nc.sbuf_tensor("t%d_%s" % (uid[0], name), list(shape), dt))

        ps = [G.enter_context(nc.psum_tensor("ps%d" % i, [128, 512], F32)) for i in range(8)]
        PK = ["ps%d" % i for i in range(8)]

        xT = tile(G, "xT", [128, 8, T])
        consts = tile(G, "consts", [128, 3, 128])
        ident = consts[:, 0, :]
        tri_le = consts[:, 1, :]
        tri_ge = consts[:, 2, :]
        ones_f = tile(G, "ones_f", [128, 128])
        avgD = tile(G, "avgD", [128, 128])
        avgC = tile(G, "avgC", [128, 128])
        modT = tile(G, "modT", [128, 2, 48, 3])
        ops1 = tile(G, "ops1", [128, 2, 8, 3])
        ops2 = tile(G, "ops2", [128, 2, 8, 3])

        S.dma("sp", consts[:], consts_d, writes=["consts"])
        epsc = tile(G, "epsc", [128, 1])
        S.op("dve", lambda e: e.memset(epsc[:], EPS), writes=["epsc"])
        S.op("dve", lambda e: e.memset(ones_f[:], 1.0), writes=["ones_f"])
        S.op("dve", lambda e: e.memset(avgD[:], 1.0 / D), writes=["avgD"])
        S.op("dve", lambda e: e.memset(avgC[:], 1.0 / 256.0), writes=["avgC"])

        for l in range(nlayer):
            for j in range(NJ):
                S.dma("pool", w1s[l, j], w1_d[l, j].rearrange("p a b -> p (a b)"), writes=["w1s_%d_%d" % (l, j)])
            for oc in range(8):
                S.dma("pool", w2s[l, oc], w2_d[l, oc].rearrange("p a b -> p (a b)"), writes=["w2s_%d_%d" % (l, oc)])

        with ExitStack() as P0:
            cTt = tile(P0, "cTt", [128, 8, 3])
            scT = tile(P0, "scT", [128, 8, 3])
            bmod = tile(P0, "bmod", [128, 2, 48])
            wm = [tile(P0, "wm%d" % i, [128, 6 * D]) for i in range(2)]
            S.dma("sp", cTt[:], cT_d, writes=["cTt"])
            S.dma("sp", bmod[:], bmod_d, writes=["bmod"])
            S.op("act", lambda e: e.activation(out=scT[:], in_=cTt[:], func=AF.Silu), reads=["cTt"], writes=["scT"])
            it = 0
            for l in range(nlayer):
                for kc in range(8):
                    w = wm[it % 2]
                    wk = "wm%d" % (it % 2)
                    it += 1
                    S.dma("sp", w[:], wmod_d[l, kc * 128:(kc + 1) * 128, :], writes=[wk])
                    pz = kc % 2
                    for j in range(48):
                        S.op("pe", lambda e: e.matmul(ps[pz][:, j * 3:(j + 1) * 3], lhsT=w[:, j * 128:(j + 1) * 128],
                                                      rhs=scT[:, kc, :], start=True, stop=True),
                             reads=[wk, "scT"], writes=[PK[pz]], inc=(j == 47))
                    S.op("dve", lambda e: e.tensor_tensor(
                        out=modT[:, l, :, :], in0=ps[pz][:, 0:144].rearrange("p (a b) -> p a b", b=3),
                        in1=(bmod[:, l, :].unsqueeze(2).to_broadcast([128, 48, 3]) if kc == 0 else modT[:, l, :, :]), op=ALU.add),
                        reads=[PK[pz], "bmod", "modT"], writes=["modT"])
            S.op("dve", lambda e: e.tensor_scalar_add(out=ops1[:], in0=modT[:, :, 8:16, :], scalar1=1.0), reads=["modT"], writes=["ops1"])
            S.op("dve", lambda e: e.tensor_scalar_add(out=ops2[:], in0=modT[:, :, 32:40, :], scalar1=1.0), reads=["modT"], writes=["ops2"])
            if debug:
                S.dma("sp", dbg_mod, modT[:], reads=["modT"])
            S.barrier()

        def ln_stats(st, src_fn, nchunks, avg, sq_tile, sqk, pA, pB, width, srckeys, tag):
            mean = st["mean"]
            m2 = st["m2"]
            rstd = st["rstd"]
            kmean, km2, krstd = st.get("keys", ("mean", "m2", "rstd"))
            for c in range(nchunks):
                S.op("act", lambda e: e.activation(out=sq_tile[:, c, 0:width], in_=src_fn(c), func=AF.Square),
                     reads=srckeys, writes=[sqk])
            for c in range(nchunks):
                S.op("pe", lambda e: e.matmul(ps[pA][:, 0:width], lhsT=avg[:], rhs=src_fn(c), start=(c == 0), stop=(c == nchunks - 1)),
                     reads=srckeys, writes=[PK[pA]], inc=(c == nchunks - 1))
            for c in range(nchunks):
                S.op("pe", lambda e: e.matmul(ps[pB][:, 0:width], lhsT=avg[:], rhs=sq_tile[:, c, 0:width], start=(c == 0), stop=(c == nchunks - 1)),
                     reads=[sqk], writes=[PK[pB]], inc=(c == nchunks - 1))
            S.op("act", lambda e: e.activation(out=mean[:, 0:width], in_=ps[pA][:, 0:width], func=AF.Identity), reads=[PK[pA]], writes=[kmean])
            S.op("dve", lambda e: e.tensor_tensor(out=m2[:, 0:width], in0=mean[:, 0:width], in1=mean[:, 0:width], op=ALU.mult), reads=[kmean], writes=[km2])
            S.op("dve", lambda e: e.tensor_tensor(out=m2[:, 0:width], in0=ps[pB][:, 0:width], in1=m2[:, 0:width], op=ALU.subtract), reads=[PK[pB], km2], writes=[km2])
            S.op("act", lambda e: e.activation(out=rstd[:, 0:width], in_=m2[:, 0:width], func=AF.Ln, bias=epsc[:, 0:1]), reads=[km2, "epsc"], writes=[krstd])
            S.op("act", lambda e: e.activation(out=rstd[:, 0:width], in_=rstd[:, 0:width], func=AF.Exp, scale=-0.5), reads=[krstd], writes=[krstd])

        for b in range(nbatch):
            with ExitStack() as PL:
                xtok = [tile(PL, "xtok%d" % i, [128, D]) for i in range(2)]
                for ch in range(NCH):
                    xt = xtok[ch % 2]
                    xk = "xtok%d" % (ch % 2)
                    S.dma("sp", xt[:], xin[b, ch * 128:(ch + 1) * 128, :], writes=[xk])
                    for half in range(2):
                        pi = (ch * 2 + half) % 4
                        for q in range(4):
                            fc = half * 4 + q
                            S.op("pe", lambda e: e.transpose(out=ps[pi][:, q * 128:(q + 1) * 128], in_=xt[:, fc * 128:(fc + 1) * 128], identity=ident),
                                 reads=[xk, "consts"], writes=[PK[pi]], inc=(q == 3))
                        S.op("dve" if half == 0 else "act",
                             (lambda e: e.tensor_copy(out=xT[:, half * 4:half * 4 + 4, ch * 128:(ch + 1) * 128],
                                                      in_=ps[pi][:, :].rearrange("p (a b) -> p a b", b=128))) if half == 0 else
                             (lambda e: e.activation(out=xT[:, half * 4:half * 4 + 4, ch * 128:(ch + 1) * 128],
                                                     in_=ps[pi][:, :].rearrange("p (a b) -> p a b", b=128), func=AF.Identity)),
                             reads=[PK[pi]], writes=["xT%d" % (ch // 2)])
                S.barrier()

            for l in range(nlayer):
                last = (l == nlayer - 1) and (nlayer == 2)
                with ExitStack() as L:
                    pp = tile(L, "pp", [128, NPP])
                    bcr = tile(L, "bcr", [128, NBC])
                    yad = tile(L, "yad", [128, 4, T], BF16)
                    S.dma("sp", pp[:], pp_d[l], writes=["pp"])
                    S.dma("sp", bcr[:], bc_d[l:l + 1, :].partition_broadcast(128) if False else bc_d[l:l + 1, :].to_broadcast([128, NBC]), writes=["bcr"])
                    ln1w, ln1b, ln2w, ln2b = pp[:, 0:8], pp[:, 8:16], pp[:, 16:24], pp[:, 24:32]
                    convw = pp[:, 32:94].rearrange("p (c k) -> p c k", k=31)
                    convb, convlnw, convlnb = pp[:, 94:96], pp[:, 96:98], pp[:, 98:100]
                    s5d, glub = pp[:, 100:102], pp[:, 102:104]
                    gbias = bcr[:, 0:16]
                    normw = bcr[:, 16:272]
                    sgulnw = bcr[:, 272:528]
                    sgulnb = bcr[:, 528:784]

                    def mcol(blk):
                        return 2 if blk == 0 else b

                    with ExitStack() as P12:
                        s5u = tile(P12, "s5u", [128, 2, SF, SJ], BF16)
                        PM = ExitStack()
                        qkT = tile(PM, "qkT", [128, 4, T], BF16)
                        ktok = tile(PM, "ktok", [128, NCH, 256], BF16)
                        vaug = tile(PM, "vaug", [128, NCH, 4, 65], BF16)
                        sigo = tile(PM, "sigo", [128, NCH, 256], BF16)
                        gates = tile(PM, "gates", [128, NCH, 16])
                        with ExitStack() as P1:
                            wA = tile(P1, "wA", [128, 8, 1296], BF16)
                            hT = [tile(P1, "hT%d" % i, [128, 8, NB], BF16) for i in range(2)]
                            S.dma("pool", wA[:], winA_d[l], writes=["wA"])
                            S.op("dve", lambda e: e.memset(vaug[:], 1.0), writes=["vaug"])
                            for blk in range(NBLK):
                                c0 = blk * NB
                                mc = mcol(blk)
                                h = hT[blk % 2]
                                hk = "hT%d" % (blk % 2)
                                for fc in range(8):
                                    S.op("act", lambda e: e.activation(out=h[:, fc, :], in_=xT[:, fc, c0:c0 + NB], func=AF.Identity,
                                                                       scale=ops1[:, l, fc, mc:mc + 1], bias=modT[:, l, fc, mc:mc + 1]),
                                         reads=["xT%d" % blk, "ops1", "modT"], writes=[hk])
                                if debug and b == 0 and l == 0 and blk == 1:
                                    S.dma("pool", dbg_h, h[:], reads=[hk])
                                fm = [(0, qkT, 0, 1.0), (128, qkT, 1, 1.0), (256, qkT, 2, 0.125), (384, qkT, 3, 0.125),
                                      (1040, s5u, 0, 1.0), (1168, s5u, 1, 1.0)]
                                for fi, (wc, dst, dc, scl) in enumerate(fm):
                                    pi = fi % 2
                                    for kc in range(8):
                                        S.op("pe", lambda e: e.matmul(ps[pi][:, 0:NB], lhsT=wA[:, kc, wc:wc + 128], rhs=h[:, kc, :],
                                                                      start=(kc == 0), stop=(kc == 7)),
                                             reads=["wA", hk], writes=[PK[pi]], inc=(kc == 7))
                                    dk = ("qkT%d" % blk) if dst is qkT else ("s5u%d" % blk)
                                    if dst is s5u:
                                        S.op("act", lambda e: e.activation(out=s5u[:, dc, :, blk * 16:(blk + 1) * 16], in_=ps[pi][:, 0:NB].rearrange("p (j i) -> p i j", i=SF), func=AF.Identity),
                                             reads=[PK[pi]], writes=[dk])
                                    elif fi % 2 == 0:
                                        S.op("act", lambda e: e.activation(out=dst[:, dc, c0:c0 + NB], in_=ps[pi][:, 0:NB], func=AF.Identity, scale=scl),
                                             reads=[PK[pi]], writes=[dk])
                                    else:
                                        S.op("dve", lambda e: e.tensor_scalar_mul(out=dst[:, dc, c0:c0 + NB], in0=ps[pi][:, 0:NB], scalar1=scl),
                                             reads=[PK[pi]], writes=[dk])
                                for cc in range(2):
                                    ch = blk * 2 + cc
                                    pa, pb = 2 + cc, 4 + cc
                                    for kc in range(8):
                                        S.op("pe", lambda e: e.matmul(ps[pa][:, 0:512], lhsT=h[:, kc, cc * 128:(cc + 1) * 128], rhs=wA[:, kc, 256:768],
                                                                      start=(kc == 0), stop=(kc == 7)), reads=["wA", hk], writes=[PK[pa]], inc=(kc == 7))
                                    for kc in range(8):
                                        S.op("pe", lambda e: e.matmul(ps[pb][:, 0:272], lhsT=h[:, kc, cc * 128:(cc + 1) * 128], rhs=wA[:, kc, 768:1040],
                                                                      start=(kc == 0), stop=(kc == 7)), reads=["wA", hk], writes=[PK[pb]], inc=(kc == 7))
                                    ck = "tm%d" % ch
                                    S.op("act", lambda e: e.activation(out=ktok[:, ch, :], in_=ps[pa][:, 0:256], func=AF.Identity, scale=0.125),
                                         reads=[PK[pa]], writes=[ck])
                                    S.op("dve", lambda e: e.tensor_copy(out=vaug[:, ch, :, 0:64], in_=ps[pa][:, 256:512].rearrange("p (a b) -> p a b", b=64)),
                                         reads=[PK[pa], "vaug"], writes=[ck])
                                    S.op("act", lambda e: e.activation(out=sigo[:, ch, :], in_=ps[pb][:, 0:256], func=AF.Sigmoid),
                                         reads=[PK[pb]], writes=[ck])
                                    S.op("dve", lambda e: e.tensor_tensor(out=gates[:, ch, :], in0=ps[pb][:, 256:272], in1=gbias, op=ALU.add),
                                         reads=[PK[pb], "bcr"], writes=[ck])
                            if debug and b == 0 and l == 0:
                                S.dma("pool", dbg_qk, qkT[:], reads=["qkT%d" % i for i in range(NBLK)])
                                S.dma("sp", dbg_xT0, xT[:], reads=["xT%d" % i for i in range(NBLK)])
                            S.barrier()

                        with ExitStack() as P2:
                            gt = tile(P2, "gt", [128, NCH, 4, 8])
                            Hs = tile(P2, "Hs", [128, NCH, 256])
                            Cst = tile(P2, "Cst", [128, 4, 65])
                            Cbf = tile(P2, "Cbf", [128, 4, 65], BF16)
                            PT = [tile(P2, "PT%d" % i, [128, 128], BF16) for i in range(2)]
                            Kpp = [tile(P2, "Kpp%d" % i, [128, 64], BF16) for i in range(2)]
                            sm = [tile(P2, "sm%d" % i, [128, 4]) for i in range(2)]
                            gtmp = [tile(P2, "gtmp%d" % i, [128, 40]) for i in range(2)]
                            S.op("dve", lambda e: e.memset(Cst[:], 0.0), writes=["Cst"])
                            S.op("dve", lambda e: e.memset(Cbf[:], 0.0), writes=["Cbf"])
                            for ch in range(NCH):
                                g = gates[:, ch, :].rearrange("p (a b) -> p a b", b=4)
                                tm = gtmp[ch % 2]
                                tk = "gtmp%d" % (ch % 2)
                                sp_ = tm[:, 0:8]
                                S.op("act", lambda e: e.activation(out=sp_.rearrange("p (a b) -> p a b", b=4), in_=g[:, 1::2, :], func=AF.Exp, scale=-1.0),
                                     reads=["tm%d" % ch], writes=[tk])
                                S.op("act", lambda e: e.activation(out=sp_, in_=sp_, func=AF.Ln, bias=1.0), reads=[tk], writes=[tk])
                                S.op("pe", lambda e: e.matmul(ps[6][:, 0:4], lhsT=tri_le, rhs=sp_[:, 0:4], start=True, stop=True), reads=[tk, "consts"], writes=[PK[6]], inc=False)
                                S.op("pe", lambda e: e.matmul(ps[6][:, 4:8], lhsT=tri_ge, rhs=sp_[:, 4:8], start=True, stop=True), reads=[tk, "consts"], writes=[PK[6]], inc=False)
                                S.op("pe", lambda e: e.matmul(ps[6][:, 8:16], lhsT=ones_f[:], rhs=sp_, start=True, stop=True), reads=[tk, "ones_f"], writes=[PK[6]], inc=True)
                                gk = "gt%d" % ch
                                S.op("act", lambda e: e.activation(out=gt[:, ch, 0, :], in_=ps[6][:, 0:8], func=AF.Exp, scale=-1.0), reads=[PK[6]], writes=[gk])
                                S.op("dve", lambda e: e.tensor_tensor(out=tm[:, 8:16].rearrange("p (a b) -> p a b", b=4),
                                                                      in0=ps[6][:, 0:8].rearrange("p (a b) -> p a b", b=4), in1=g[:, 0::2, :], op=ALU.add),
                                     reads=[PK[6], "tm%d" % ch], writes=[tk])
                                S.op("act", lambda e: e.activation(out=gt[:, ch, 1, :], in_=tm[:, 8:16], func=AF.Exp), reads=[tk], writes=[gk])
                                S.op("dve", lambda e: e.tensor_tensor(out=tm[:, 16:24], in0=tm[:, 8:16], in1=ps[6][:, 8:16], op=ALU.subtract), reads=[PK[6], tk], writes=[tk])
                                S.op("act", lambda e: e.activation(out=gt[:, ch, 2, :], in_=tm[:, 16:24], func=AF.Exp), reads=[tk], writes=[gk])
                                S.op("act", lambda e: e.activation(out=gt[:, ch, 3, :], in_=ps[6][:, 8:16], func=AF.Exp, scale=-1.0), reads=[PK[6]], writes=[gk])
                            order = [list(range(NCH)), [1, 0] + list(range(NCH - 1, 1, -1))]
                            written = set()
                            PTs = [tile(P2, "PTs%d" % i, [128, 128], BF16) for i in range(2)]
                            maskb = tile(P2, "maskb", [128, 2, 128], BF16)
                            S.op("dve", lambda e: e.tensor_copy(out=maskb[:], in_=consts[:, 1:3, :]), reads=["consts"], writes=["maskb"])
                            flat = []
                            for step in range(NCH):
                                for d in range(2):
                                    for hh in range(4):
                                        it = len(flat)
                                        ch = order[d][step]
                                        flat.append(dict(it=it, step=step, d=d, hh=hh, ch=ch, hd=d * 4 + hh, qc=hh // 2, po=(hh % 2) * 64, ci=d * 2 + hh // 2,
                                                         cs=slice(ch * 128, (ch + 1) * 128), qk="qkT%d" % (ch // 2), ck="tm%d" % ch, gk="gt%d" % ch,
                                                         stk="C%d_%d" % (d * 2 + hh // 2, hh % 2), mask=(tri_le if d == 0 else tri_ge)))

                            def stage1(q):
                                it, ch, hh, hd, po, qc, cs = q["it"], q["ch"], q["hh"], q["hd"], q["po"], q["qc"], q["cs"]
                                pS = it % 2
                                ptile, kp = PT[it % 2], Kpp[it % 2]
                                ptk, kpk = "PT%d" % (it % 2), "Kpp%d" % (it % 2)
                                S.op("pe", lambda e: e.matmul(ps[pS][:, 0:128], lhsT=qkT[po:po + 64, 2 + qc, cs], rhs=qkT[po:po + 64, qc, cs], start=True, stop=True),
                                     reads=[q["qk"]], writes=[PK[pS]])
                                pts = PTs[it % 2]
                                ptsk = "PTs%d" % (it % 2)
                                S.op("act", lambda e: e.activation(out=pts[:], in_=ps[pS][:, 0:128], func=AF.Identity, scale=gt[:, ch, 1, hd:hd + 1]),
                                     reads=[PK[pS], q["gk"]], writes=[ptsk])
                                S.op("pool", lambda e: e.tensor_tensor(out=ptile[:], in0=pts[:], in1=(maskb[:, 0, :] if q["d"] == 0 else maskb[:, 1, :]), op=ALU.mult),
                                     reads=[ptsk, "maskb"], writes=[ptk])
                                if q["step"] < NCH - 1:
                                    S.op("act", lambda e: e.activation(out=kp[:], in_=ktok[:, ch, hh * 64:(hh + 1) * 64], func=AF.Identity, scale=gt[:, ch, 2, hd:hd + 1]),
                                         reads=[q["ck"], q["gk"]], writes=[kpk])

                            def stage2(q):
                                it, ch, hh, hd, po, qc, cs, ci = q["it"], q["ch"], q["hh"], q["hd"], q["po"], q["qc"], q["cs"], q["ci"]
                                pA, pC = 2 + it % 2, 4 + it % 2
                                ptile, kp, smt = PT[it % 2], Kpp[it % 2], sm[it % 2]
                                ptk, kpk, smk = "PT%d" % (it % 2), "Kpp%d" % (it % 2), "sm%d" % (it % 2)
                                qk, ck, gk, stk = q["qk"], q["ck"], q["gk"], q["stk"]
                                S.op("pe", lambda e: e.matmul(ps[pA][:, 0:65], lhsT=ptile[:], rhs=vaug[:, ch, hh, :], start=True, stop=False),
                                     reads=[ptk, ck], writes=[PK[pA]], inc=False)
                                S.op("pe", lambda e: e.matmul(ps[pA][:, 0:65], lhsT=qkT[po:po + 64, qc, cs], rhs=Cbf[po:po + 64, ci, :], start=False, stop=True),
                                     reads=[qk, stk + "b", "Cbf"], writes=[PK[pA]])
                                if q["step"] < NCH - 1:
                                    S.op("pe", lambda e: e.matmul(ps[pC][po:po + 64, 0:65], lhsT=kp[:], rhs=vaug[:, ch, hh, :], start=True, stop=True, tile_position=(0, po)),
                                         reads=[kpk, ck], writes=[PK[pC]])
                                S.op("act", lambda e: e.activation(out=smt[:, 0:1], in_=ps[pA][:, 64:65], func=AF.Abs, scale=gt[:, ch, 0, hd:hd + 1]),
                                     reads=[PK[pA], gk], writes=[smk])
                                S.op("dve", lambda e: e.tensor_scalar_max(out=smt[:, 0:1], in0=smt[:, 0:1], scalar1=1.0), reads=[smk], writes=[smk])
                                S.op("dve", lambda e: e.reciprocal(out=smt[:, 2:3], in_=smt[:, 0:1]), reads=[smk], writes=[smk])
                                S.op("dve", lambda e: e.tensor_tensor(out=smt[:, 1:2], in0=gt[:, ch, 0, hd:hd + 1], in1=smt[:, 2:3], op=ALU.mult),
                                     reads=[smk, gk], writes=[smk])
                                hk_ = "Hs%d_%d" % (ch, hh)
                                if (ch, hh) not in written:
                                    written.add((ch, hh))
                                    S.op("dve", lambda e: e.tensor_scalar_mul(out=Hs[:, ch, hh * 64:(hh + 1) * 64], in0=ps[pA][:, 0:64], scalar1=smt[:, 1:2]),
                                         reads=[PK[pA], smk], writes=[hk_])
                                else:
                                    S.op("dve", lambda e: e.scalar_tensor_tensor(out=Hs[:, ch, hh * 64:(hh + 1) * 64], in0=ps[pA][:, 0:64], scalar=smt[:, 1:2],
                                                                                 in1=Hs[:, ch, hh * 64:(hh + 1) * 64], op0=ALU.mult, op1=ALU.add),
                                         reads=[PK[pA], smk, hk_], writes=[hk_])
                                if q["step"] < NCH - 1:
                                    S.op("dve", lambda e: e.scalar_tensor_tensor(out=Cst[po:po + 64, ci, :], in0=Cst[po:po + 64, ci, :], scalar=gt[po:po + 64, ch, 3, hd:hd + 1],
                                                                                 in1=ps[pC][po:po + 64, 0:65], op0=ALU.mult, op1=ALU.add),
                                         reads=[PK[pC], gk, stk, "Cst"], writes=[stk])
                                    S.op("pool", lambda e: e.tensor_copy(out=Cbf[po:po + 64, ci, :], in_=Cst[po:po + 64, ci, :]),
                                         reads=[stk, "Cbf"], writes=[stk + "b"])

                            for i in range(len(flat) + 1):
                                if i < len(flat):
                                    stage1(flat[i])
                                if i >= 1:
                                    stage2(flat[i - 1])
                            with ExitStack() as P2r:
                                st6 = [tile(P2r, "st6_%d" % i, [128, 4, 6]) for i in range(2)]
                                mv = [tile(P2r, "mv%d" % i, [128, 4, 2]) for i in range(2)]
                                rs = [tile(P2r, "rs%d" % i, [128, 4]) for i in range(2)]
                                ya = [tile(P2r, "ya%d" % i, [128, 256]) for i in range(2)]
                                for ch in range(NCH):
                                    i2 = ch % 2
                                    hkeys = ["Hs%d_%d" % (ch, hh) for hh in range(4)]
                                    for hh in range(4):
                                        S.op("dve", lambda e: e.bn_stats(out=st6[i2][:, hh, :], in_=Hs[:, ch, hh * 64:(hh + 1) * 64]), reads=hkeys, writes=["st6_%d" % i2])
                                        S.op("dve", lambda e: e.bn_aggr(out=mv[i2][:, hh, :], in_=st6[i2][:, hh, :]), reads=["st6_%d" % i2], writes=["mv%d" % i2])
                                    S.op("act", lambda e: e.activation(out=rs[i2][:], in_=mv[i2][:, :, 1], func=AF.Sqrt, bias=epsc[:, 0:1]),
                                         reads=["mv%d" % i2, "epsc"], writes=["rs%d" % i2])
                                    S.op("dve", lambda e: e.reciprocal(out=rs[i2][:], in_=rs[i2][:]), reads=["rs%d" % i2], writes=["rs%d" % i2])
                                    for hh in range(4):
                                        S.op("dve", lambda e: e.tensor_scalar(out=ya[i2][:, hh * 64:(hh + 1) * 64], in0=Hs[:, ch, hh * 64:(hh + 1) * 64],
                                                                              scalar1=mv[i2][:, hh, 0:1], scalar2=rs[i2][:, hh:hh + 1], op0=ALU.subtract, op1=ALU.mult),
                                             reads=hkeys + ["mv%d" % i2, "rs%d" % i2], writes=["ya%d" % i2])
                                    S.op("dve", lambda e: e.tensor_tensor(out=ya[i2][:], in0=ya[i2][:], in1=normw, op=ALU.mult), reads=["ya%d" % i2, "bcr"], writes=["ya%d" % i2])
                                    S.op("dve", lambda e: e.tensor_tensor(out=ya[i2][:], in0=ya[i2][:], in1=sigo[:, ch, :], op=ALU.mult), reads=["ya%d" % i2, "tm%d" % ch], writes=["ya%d" % i2])
                                    pi = 6 + i2
                                    for j in range(2):
                                        S.op("pe", lambda e: e.transpose(out=ps[pi][:, j * 128:(j + 1) * 128], in_=ya[i2][:, j * 128:(j + 1) * 128], identity=ident),
                                             reads=["ya%d" % i2, "consts"], writes=[PK[pi]], inc=(j == 1))
                                    S.op("act", lambda e: e.activation(out=yad[:, 0:2, ch * 128:(ch + 1) * 128], in_=ps[pi][:, 0:256].rearrange("p (a b) -> p a b", b=128), func=AF.Identity),
                                         reads=[PK[pi]], writes=["yad_a%d" % ch])
                                S.barrier()
                        PM.close()

                        with ExitStack() as P3:
                            s5A = tile(P3, "s5A", [128, 3, 512])
                            s5B = tile(P3, "s5B", [128, 2, 256])
                            s5C = tile(P3, "s5C", [128, 2, 256])
                            s5Cb = tile(P3, "s5Cb", [128, 2, 256], BF16)
                            s5Bn = tile(P3, "s5Bn", [128, 2, 256])
                            s5p = tile(P3, "s5p", [128, 3, 16])
                            tp = [tile(P3, "tp%d" % i, [128, 16]) for i in range(10)]
                            pw = tile(P3, "pw", [128, 2, 17, 16])
                            pw2 = tile(P3, "pw2", [128, 2, 9, 16])
                            npw2 = tile(P3, "npw2", [128, 9, 16])
                            sT = tile(P3, "sT", [128, 2, 16, 16])
                            gluw = tile(P3, "gluw", [128, 2, 256], BF16)
                            WEb = tile(P3, "WEb", [128, 16, 2, 512], BF16)
                            Gb = tile(P3, "Gb", [128, 2, 2, 16, 32], BF16)
                            G0 = tile(P3, "G0", [128, 2, 32], BF16)
                            S.dma("sp", s5C[:].rearrange("p a (b c) -> p a b c", c=32), s5C_d[l], writes=["s5C"])
                            S.dma("pool", gluw[:], glu_d[l], writes=["gluw"])
                            S.op("dve", lambda e: e.tensor_copy(out=s5Cb[:, 0, :], in_=s5C[:, 0, :]), reads=["s5C"], writes=["s5Cb"])
                            S.op("dve", lambda e: e.tensor_scalar_mul(out=s5Cb[:, 1, :], in0=s5C[:, 1, :], scalar1=-1.0), reads=["s5C", "s5Cb"], writes=["s5Cb"])
                            if b == 0:
                                PTA = ExitStack()
                                Wnb = [tile(PTA, "Wnb%d" % i, [128, 2, 512], BF16) for i in range(2)]
                                tA = [tile(PTA, "tA%d" % i, [128, 512]) for i in range(8)]
                                S.dma("sp", s5A[:].rearrange("p a (b c) -> p a b c", c=128), s5A_d[l].rearrange("p a d c n -> p a (d c) n"), writes=["s5A"])
                                S.dma("sp", s5B[:].rearrange("p a (b c) -> p a b c", c=128), s5B_d[l], writes=["s5B"])
                                S.dma("sp", s5Bn[:].rearrange("p a (b c) -> p a b c", c=32), s5Bn_d[l], writes=["s5Bn"])
                                S.dma("sp", s5p[:], s5pp_d[l], writes=["s5p"])

                                def tt(out, in0, in1, op, reads, writes, en="dve"):
                                    S.op(en, lambda e: e.tensor_tensor(out=out, in0=in0, in1=in1, op=op), reads=reads, writes=writes)

                                def cplx_abar(are, aim, ldt, t, width, key, tkeys):
                                    w = width
                                    dt_, lr, li, mag, ph, kk, ar, ai = [x[:, 0:w] for x in t[0:8]]
                                    S.op("act", lambda e: e.activation(out=dt_, in_=ldt, func=AF.Exp), reads=[key], writes=[tkeys[0]])
                                    tt(lr, are, dt_, ALU.mult, [key, tkeys[0]], [tkeys[1]])
                                    tt(li, aim, dt_, ALU.mult, [key, tkeys[0]], [tkeys[2]])
                                    S.op("act", lambda e: e.activation(out=mag, in_=lr, func=AF.Exp), reads=[tkeys[1]], writes=[tkeys[3]])
                                    for (shift, dst, dk) in ((0.0, ai, tkeys[7]), (PI / 2, ar, tkeys[6])):
                                        S.op("dve", lambda e: e.tensor_scalar_add(out=ph, in0=li, scalar1=shift), reads=[tkeys[2]], writes=[tkeys[4]])
                                        S.op("dve", lambda e: e.tensor_copy(out=dst, in_=ph), reads=[tkeys[4]], writes=[dk])
                                        for m in range(6):
                                            thr = (2 * m + 1) * PI
                                            S.op("dve", lambda e: e.tensor_scalar(out=kk, in0=ph, scalar1=thr, scalar2=-2 * PI, op0=ALU.is_gt, op1=ALU.mult),
                                                 reads=[tkeys[4]], writes=[tkeys[5]])
                                            tt(dst, dst, kk, ALU.add, [tkeys[5], dk], [dk])
                                        S.op("act", lambda e: e.activation(out=dst, in_=dst, func=AF.Sin), reads=[dk], writes=[dk])
                                        tt(dst, dst, mag, ALU.mult, [dk, tkeys[3]], [dk])
                                    return ar, ai

                                def kappa(kr, ki, den, arm1, ar, ai, lr, li, t0, keys):
                                    K = keys
                                    tt(den, lr, lr, ALU.mult, [K["lam"]], [K["den"]])
                                    tt(t0, li, li, ALU.mult, [K["lam"]], [K["t0"]])
                                    tt(den, den, t0, ALU.add, [K["den"], K["t0"]], [K["den"]])
                                    S.op("dve", lambda e: e.reciprocal(out=den, in_=den), reads=[K["den"]], writes=[K["den"]])
                                    S.op("dve", lambda e: e.tensor_scalar_add(out=arm1, in0=ar, scalar1=-1.0), reads=[K["ar"]], writes=[K["arm1"]])
                                    tt(kr, arm1, lr, ALU.mult, [K["arm1"], K["lam"]], [K["kr"]])
                                    tt(t0, ai, li, ALU.mult, [K["ai"], K["lam"]], [K["t0"]])
                                    tt(kr, kr, t0, ALU.add, [K["kr"], K["t0"]], [K["kr"]])
                                    tt(kr, kr, den, ALU.mult, [K["kr"], K["den"]], [K["kr"]])
                                    tt(ki, ai, lr, ALU.mult, [K["ai"], K["lam"]], [K["ki"]])
                                    tt(t0, arm1, li, ALU.mult, [K["arm1"], K["lam"]], [K["t0"]])
                                    tt(ki, ki, t0, ALU.subtract, [K["ki"], K["t0"]], [K["ki"]])
                                    tt(ki, ki, den, ALU.mult, [K["ki"], K["den"]], [K["ki"]])

                                tAk = ["tA%d" % i for i in range(8)]
                                tpk = ["tp%d" % i for i in range(10)]
                                ar, ai = cplx_abar(s5A[:, 0, :], s5A[:, 1, :], s5A[:, 2, :], tA, 512, "s5A", tAk)
                                den, kr, ki, t0, t1, arm1 = tA[0][:, :], tA[1][:, :], tA[2][:, :], tA[3][:, :], tA[4][:, :], tA[5][:, :]
                                kappa(kr, ki, den, arm1, ar, ai, s5A[:, 0, :], s5A[:, 1, :], t0,
                                      dict(lam="s5A", den=tAk[0], kr=tAk[1], ki=tAk[2], t0=tAk[3], arm1=tAk[5], ar=tAk[6], ai=tAk[7]))
                                v3 = lambda x: x.rearrange("p (a b) -> p a b", b=256)
                                Bre = s5B[:, 0, :].unsqueeze(1).to_broadcast([128, 2, 256])
                                Bim = s5B[:, 1, :].unsqueeze(1).to_broadcast([128, 2, 256])
                                Wr, Wi, Wr2, Wi2 = tA[5][:, :], tA[0][:, :], tA[1][:, :], tA[2][:, :]
                                Wrk, Wik, Wr2k, Wi2k = tAk[5], tAk[0], tAk[1], tAk[2]
                                tt(v3(t0), v3(kr), Bre, ALU.mult, [tAk[1], "s5B"], [tAk[3]])
                                tt(v3(t1), v3(ki), Bim, ALU.mult, [tAk[2], "s5B"], [tAk[4]])
                                tt(Wr, t0, t1, ALU.subtract, [tAk[3], tAk[4], tAk[5]], [Wrk])
                                tt(v3(t0), v3(kr), Bim, ALU.mult, [tAk[1], "s5B"], [tAk[3]])
                                tt(v3(t1), v3(ki), Bre, ALU.mult, [tAk[2], "s5B"], [tAk[4]])
                                tt(Wi, t0, t1, ALU.add, [tAk[3], tAk[4], tAk[0]], [Wik])
                                for e_ in range(16):
                                    S.op("act", lambda e: e.activation(out=WEb[:, e_, 0, :], in_=Wr, func=AF.Identity), reads=[Wrk], writes=["WEb"])
                                    S.op("act", lambda e: e.activation(out=WEb[:, e_, 1, :], in_=Wi, func=AF.Identity), reads=[Wik], writes=["WEb"])
                                    if e_ == 15:
                                        break
                                    tt(t0, Wr, ar, ALU.mult, [Wrk, tAk[6]], [tAk[3]])
                                    tt(t1, Wi, ai, ALU.mult, [Wik, tAk[7]], [tAk[4]])
                                    tt(Wr2, t0, t1, ALU.subtract, [tAk[3], tAk[4], Wr2k], [Wr2k])
                                    tt(t0, Wr, ai, ALU.mult, [Wrk, tAk[7]], [tAk[3]])
                                    tt(t1, Wi, ar, ALU.mult, [Wik, tAk[6]], [tAk[4]])
                                    tt(Wi2, t0, t1, ALU.add, [tAk[3], tAk[4], Wi2k], [Wi2k])
                                    Wr, Wi, Wr2, Wi2 = Wr2, Wi2, Wr, Wi
                                    Wrk, Wik, Wr2k, Wi2k = Wr2k, Wi2k, Wrk, Wik
                                par, pai = cplx_abar(s5p[:, 0, :], s5p[:, 1, :], s5p[:, 2, :], tp, 16, "s5p", tpk)
                                S.op("dve", lambda e: e.memset(pw[:, 0, 0, :], 1.0), writes=["pw"])
                                S.op("dve", lambda e: e.memset(pw[:, 1, 0, :], 0.0), reads=["pw"], writes=["pw"])
                                S.op("dve", lambda e: e.tensor_copy(out=pw[:, 0, 1, :], in_=par), reads=[tpk[6], "pw"], writes=["pw"])
                                S.op("dve", lambda e: e.tensor_copy(out=pw[:, 1, 1, :], in_=pai), reads=[tpk[7], "pw"], writes=["pw"])
                                kappa(tp[1][:, :], tp[2][:, :], tp[0][:, :], tp[5][:, :], par, pai, s5p[:, 0, :], s5p[:, 1, :], tp[3][:, :],
                                      dict(lam="s5p", den=tpk[0], kr=tpk[1], ki=tpk[2], t0=tpk[3], arm1=tpk[5], ar=tpk[6], ai=tpk[7]))
                                S.op("dve", lambda e: e.tensor_copy(out=sT[:, 0, 0, :], in_=tp[1][:, :]), reads=[tpk[1]], writes=["sT"])
                                S.op("dve", lambda e: e.tensor_copy(out=sT[:, 1, 0, :], in_=tp[2][:, :]), reads=[tpk[2], "sT"], writes=["sT"])

                                def cmul(dr, di, xr, xi, yr, yi, rk, wk):
                                    u0, u1, u2, u3 = tp[3][:, :], tp[4][:, :], tp[8][:, :], tp[9][:, :]
                                    tt(u0, xr, yr, ALU.mult, rk, [tpk[3]])
                                    tt(u1, xi, yi, ALU.mult, rk, [tpk[4]])
                                    tt(u2, xr, yi, ALU.mult, rk, [tpk[8]])
                                    tt(u3, xi, yr, ALU.mult, rk, [tpk[9]])
                                    tt(dr, u0, u1, ALU.subtract, [tpk[3], tpk[4]] + wk, wk)
                                    tt(di, u2, u3, ALU.add, [tpk[8], tpk[9]] + wk, wk)

                                for k in range(2, 17):
                                    cmul(pw[:, 0, k, :], pw[:, 1, k, :], pw[:, 0, k - 1, :], pw[:, 1, k - 1, :], pw[:, 0, 1, :], pw[:, 1, 1, :], ["pw"], ["pw"])
                                S.op("dve", lambda e: e.tensor_copy(out=pw2[:, :, 0, :], in_=pw[:, :, 16, :]), reads=["pw"], writes=["pw2"])
                                for m in range(1, 9):
                                    cmul(pw2[:, 0, m, :], pw2[:, 1, m, :], pw2[:, 0, m - 1, :], pw2[:, 1, m - 1, :], pw2[:, 0, m - 1, :], pw2[:, 1, m - 1, :], ["pw2"], ["pw2"])
                                S.op("dve", lambda e: e.tensor_scalar_mul(out=npw2[:], in0=pw2[:, 1, :, :], scalar1=-1.0), reads=["pw2"], writes=["npw2"])
                                for tau in range(1, 16):
                                    cmul(sT[:, 0, tau, :], sT[:, 1, tau, :], pw[:, 0, tau, :], pw[:, 1, tau, :], sT[:, 0, 0, :], sT[:, 1, 0, :], ["pw", "sT"], ["sT"])
                                v4 = lambda x: x.rearrange("p (a b c) -> p a b c", a=2, b=8)
                                Bnr = s5Bn[:, 0, :].rearrange("p (b c) -> p b c", c=32).unsqueeze(1).to_broadcast([128, 2, 8, 32])
                                Bni = s5Bn[:, 1, :].rearrange("p (b c) -> p b c", c=32).unsqueeze(1).to_broadcast([128, 2, 8, 32])
                                for tau in range(16):
                                    wn = Wnb[tau % 2]
                                    wnk = "Wnb%d" % (tau % 2)
                                    sr = sT[:, 0, tau, :].rearrange("p (a b) -> p a b", b=8).unsqueeze(3).to_broadcast([128, 2, 8, 32])
                                    si = sT[:, 1, tau, :].rearrange("p (a b) -> p a b", b=8).unsqueeze(3).to_broadcast([128, 2, 8, 32])
                                    tt(v4(t0), sr, Bnr, ALU.mult, ["sT", "s5Bn"], [tAk[3]])
                                    tt(v4(t1), si, Bni, ALU.mult, ["sT", "s5Bn"], [tAk[4]])
                                    tt(wn[:, 0, :], t0, t1, ALU.subtract, [tAk[3], tAk[4]], [wnk])
                                    tt(v4(t0), sr, Bni, ALU.mult, ["sT", "s5Bn"], [tAk[3]])
                                    tt(v4(t1), si, Bnr, ALU.mult, ["sT", "s5Bn"], [tAk[4]])
                                    tt(wn[:, 1, :], t0, t1, ALU.add, [tAk[3], tAk[4], wnk], [wnk])
                                    pi = tau % 2
                                    for d in range(2):
                                        for pair in range(8):
                                            chn, ppi = pair // 4, pair % 4
                                            o = ps[pi][ppi * 32:(ppi + 1) * 32, (chn * 2 + d) * 32:(chn * 2 + d + 1) * 32]
                                            for c in range(2):
                                                S.op("pe", lambda e: e.matmul(o, lhsT=wn[:, c, (d * 8 + pair) * 32:(d * 8 + pair + 1) * 32], rhs=s5Cb[:, c, pair * 32:(pair + 1) * 32],
                                                                              start=(c == 0), stop=(c == 1), tile_position=(0, ppi * 32)),
                                                     reads=[wnk, "s5Cb"], writes=[PK[pi]], inc=(c == 1 and d == 1 and pair == 7))
                                    S.op("act", lambda e: e.activation(out=Gb[:, :, :, tau, :], in_=ps[pi][:, 0:128].rearrange("p (a b c) -> p a b c", a=2, b=2), func=AF.Identity),
                                         reads=[PK[pi]], writes=["Gb"])
                                tt(G0[:], Gb[:, :, 0, 0, :], Gb[:, :, 1, 0, :], ALU.add, ["Gb"], ["G0"])
                                S.barrier()
                                PTA.close()
                                fl = lambda x: x
                                S.dma("sp", cWE[l], WEb[:].rearrange("p a b c -> p (a b c)"), reads=["WEb"])
                                S.dma("sp", cG[l], Gb[:].rearrange("p a b c d -> p (a b c d)"), reads=["Gb"])
                                S.dma("sp", cG0[l], G0[:].rearrange("p a b -> p (a b)"), reads=["G0"])
                                S.dma("sp", cPW[l], pw[:].rearrange("p a b c -> p (a b c)"), reads=["pw"])
                                S.dma("sp", cPW2[l], pw2[:].rearrange("p a b c -> p (a b c)"), reads=["pw2"])
                                S.dma("sp", cNPW2[l], npw2[:].rearrange("p a b -> p (a b)"), reads=["npw2"])
                            else:
                                S.dma("sp", WEb[:].rearrange("p a b c -> p (a b c)"), cWE[l], writes=["WEb"])
                                S.dma("sp", Gb[:].rearrange("p a b c d -> p (a b c d)"), cG[l], writes=["Gb"])
                                S.dma("sp", G0[:].rearrange("p a b -> p (a b)"), cG0[l], writes=["G0"])
                                S.dma("sp", pw[:].rearrange("p a b c -> p (a b c)"), cPW[l], writes=["pw"])
                                S.dma("sp", pw2[:].rearrange("p a b c -> p (a b c)"), cPW2[l], writes=["pw2"])
                                S.dma("sp", npw2[:].rearrange("p a b -> p (a b)"), cNPW2[l], writes=["npw2"])
                                S.barrier()
                            EA = [[[tile(P3, "EA%d%d%d" % (par, d, c), [128, SJ + 1]) for c in range(2)] for d in range(2)] for par in range(2)]
                            EB = [[[tile(P3, "EB%d%d%d" % (par, d, c), [128, SJ + 1]) for c in range(2)] for d in range(2)] for par in range(2)]
                            CWt = [tile(P3, "CW%d" % i, [128, 17, 2, 2, 32], BF16) for i in range(2)]
                            cu = [tile(P3, "cu%d" % i, [128, 16, 32]) for i in range(3)]
                            ypre = tile(P3, "ypre", [128, 2, T], BF16)
                            en = "dve"

                            def stt(out, in0, scal, in1, reads, writes):
                                S.op(en, lambda e: e.scalar_tensor_tensor(out=out, in0=in0, scalar=scal, in1=in1, op0=ALU.mult, op1=ALU.add), reads=reads, writes=writes)

                            uk = ["s5u%d" % i for i in range(NBLK)]
                            W = SJ + 1

                            def s5_head(pair):
                                chn, ppi, par = pair // 4, pair % 4, pair % 2
                                rows = slice(ppi * 32, ppi * 32 + 32)
                                cw = CWt[par]
                                cwk = "CW%d" % par
                                DD = []
                                for d in range(2):
                                    DD.append(dict(d=d, sc=d * 8 + pair, ea=EA[par][d], eb=EB[par][d],
                                                   eak=["EA%d%d0" % (par, d), "EA%d%d1" % (par, d)], ebk=["EB%d%d0" % (par, d), "EB%d%d1" % (par, d)],
                                                   lo=(1 if d == 0 else 0), zc=(0 if d == 0 else SJ)))
                                pend = []
                                Cr = s5C[:, 0, pair * 32:(pair + 1) * 32].unsqueeze(1).to_broadcast([128, 16, 32])
                                Ci = s5C[:, 1, pair * 32:(pair + 1) * 32].unsqueeze(1).to_broadcast([128, 16, 32])
                                for d in range(2):
                                    sc = d * 8 + pair
                                    pr = pw[:, 0, 1:17, sc:sc + 1].to_broadcast([128, 16, 32])
                                    pim = pw[:, 1, 1:17, sc:sc + 1].to_broadcast([128, 16, 32])
                                    pend.append(lambda pr=pr: tt(cu[0][:], Cr, pr, ALU.mult, ["s5C", "pw"], ["cu0"]))
                                    pend.append(lambda pim=pim: tt(cu[1][:], Ci, pim, ALU.mult, ["s5C", "pw"], ["cu1"]))
                                    pend.append(lambda d=d: tt(cw[:, 1:17, d, 0, :], cu[0][:], cu[1][:], ALU.subtract, ["cu0", "cu1"], [cwk]))
                                    pend.append(lambda pim=pim: tt(cu[0][:], Cr, pim, ALU.mult, ["s5C", "pw"], ["cu0"]))
                                    pend.append(lambda pr=pr: tt(cu[1][:], Ci, pr, ALU.mult, ["s5C", "pw"], ["cu1"]))
                                    pend.append(lambda d=d: S.op("dve", lambda e: e.scalar_tensor_tensor(out=cw[:, 1:17, d, 1, :], in0=cu[0][:], scalar=-1.0, in1=cu[1][:], op0=ALU.mult, op1=ALU.subtract),
                                                                 reads=["cu0", "cu1", cwk], writes=[cwk]))
                                for q in DD:
                                    d = q["d"]
                                    for c in range(2):
                                        pi = d * 2 + c
                                        for i in range(SF):
                                            lag = SF - 1 - i if d == 0 else i
                                            S.op("pe", lambda e: e.matmul(ps[pi][:, 0:SJ], lhsT=WEb[rows, lag, c, (d * 2 + chn) * 128:(d * 2 + chn + 1) * 128],
                                                                          rhs=s5u[rows, chn, i, :], start=(i == 0), stop=(i == SF - 1), tile_position=(ppi * 32, 0)),
                                                 reads=["WEb"] + uk, writes=[PK[pi]], inc=(i == SF - 1))
                                        zc = q["zc"]
                                        S.op("act", lambda e: e.activation(out=q["ea"][c][:, zc:zc + 1], in_=zcol[:, 0:1], func=AF.Identity), reads=["zcol"], writes=[q["eak"][c]])
                                        S.op("act", lambda e: e.activation(out=q["eb"][c][:, zc:zc + 1], in_=zcol[:, 0:1], func=AF.Identity), reads=["zcol"], writes=[q["ebk"][c]])
                                        if d == 0:
                                            S.op("act", lambda e: e.activation(out=q["ea"][c][:, 1:SJ + 1], in_=ps[pi][:, 0:SJ], func=AF.Identity), reads=[PK[pi]], writes=[q["eak"][c]])
                                        else:
                                            S.op("act", lambda e: e.activation(out=q["ea"][c][:, 0:128], in_=ps[pi][:, 16:SJ], func=AF.Identity), reads=[PK[pi]], writes=[q["eak"][c]])
                                            S.op("act", lambda e: e.activation(out=q["ea"][c][:, 128:SJ], in_=ps[pi][:, 0:16], func=AF.Identity), reads=[PK[pi]], writes=[q["eak"][c]])
                                    q["cur"], q["nxt"], q["curk"], q["nxtk"] = q["ea"], q["eb"], q["eak"], q["ebk"]
                                for m in range(8):
                                    dd = 1 << m
                                    for phase in range(3):
                                        for q in DD:
                                            d, sc = q["d"], q["sc"]
                                            cur, nxt, curk, nxtk = q["cur"], q["nxt"], q["curk"], q["nxtk"]
                                            p_r, p_i, np_i = pw2[:, 0, m, sc:sc + 1], pw2[:, 1, m, sc:sc + 1], npw2[:, m, sc:sc + 1]
                                            if d == 0:
                                                o_s, i_s, k_s = slice(dd, W), slice(0, W - dd), slice(0, dd)
                                            else:
                                                o_s, i_s, k_s = slice(0, W - dd), slice(dd, W), slice(W - dd, W)
                                            if phase == 0:
                                                pend.append(lambda nxt=nxt, cur=cur, o_s=o_s, i_s=i_s, p_r=p_r, curk=curk, nxtk=nxtk: stt(nxt[0][:, o_s], cur[0][:, i_s], p_r, cur[0][:, o_s], curk + ["pw2"], [nxtk[0]]))
                                                pend.append(lambda nxt=nxt, cur=cur, o_s=o_s, i_s=i_s, p_r=p_r, curk=curk, nxtk=nxtk: stt(nxt[1][:, o_s], cur[1][:, i_s], p_r, cur[1][:, o_s], curk + ["pw2"], [nxtk[1]]))
                                            elif phase == 1:
                                                pend.append(lambda nxt=nxt, cur=cur, o_s=o_s, i_s=i_s, np_i=np_i, curk=curk, nxtk=nxtk: stt(nxt[0][:, o_s], cur[1][:, i_s], np_i, nxt[0][:, o_s], curk + ["npw2", nxtk[0]], [nxtk[0]]))
                                                pend.append(lambda nxt=nxt, cur=cur, o_s=o_s, i_s=i_s, p_i=p_i, curk=curk, nxtk=nxtk: stt(nxt[1][:, o_s], cur[0][:, i_s], p_i, nxt[1][:, o_s], curk + ["pw2", nxtk[1]], [nxtk[1]]))
                                            else:
                                                for c in range(2):
                                                    pend.append(lambda nxt=nxt, cur=cur, k_s=k_s, c=c, curk=curk, nxtk=nxtk: S.op(en, lambda e: e.tensor_copy(out=nxt[c][:, k_s], in_=cur[c][:, k_s]), reads=[curk[c]], writes=[nxtk[c]]))
                                    for q in DD:
                                        q["cur"], q["nxt"], q["curk"], q["nxtk"] = q["nxt"], q["cur"], q["nxtk"], q["curk"]
                                Ef = []
                                for q in DD:
                                    for c in range(2):
                                        pend.append(lambda q=q, c=c, nxt=q["nxt"], cur=q["cur"], curk=q["curk"], nxtk=q["nxtk"]:
                                                    S.op("act", lambda e: e.activation(out=nxt[c][:, :].bitcast(BF16)[:, 0:W], in_=cur[c][:, :], func=AF.Identity),
                                                         reads=[curk[c]], writes=[nxtk[c]]))
                                    Ef.append(([q["nxt"][c][:, :].bitcast(BF16) for c in range(2)], list(q["nxtk"])))
                                return dict(pair=pair, chn=chn, ppi=ppi, rows=rows, cw=cw, cwk=cwk, Ef=Ef), pend

                            def s5_tail(ctx, pend):
                                pair, chn, ppi, rows, cw, cwk, Ef = ctx["pair"], ctx["chn"], ctx["ppi"], ctx["rows"], ctx["cw"], ctx["cwk"], ctx["Ef"]
                                npend = (len(pend) + SF - 1) // SF
                                for i in range(SF):
                                    pi = 4 + i % 4
                                    o = ps[pi][rows, 0:SJ]
                                    rk = ["Gb", "G0"] + uk
                                    for i2 in range(SF):
                                        if i2 < i:
                                            lt = Gb[rows, chn, 0, i - i2, :]
                                        elif i2 > i:
                                            lt = Gb[rows, chn, 1, i2 - i, :]
                                        else:
                                            lt = G0[rows, chn, :]
                                        S.op("pe", lambda e: e.matmul(o, lhsT=lt, rhs=s5u[rows, chn, i2, :], start=(i2 == 0), stop=False, tile_position=(ppi * 32, ppi * 32)),
                                             reads=rk, writes=[PK[pi]], inc=False)
                                    (ef, efk), (eb_, ebk_) = Ef
                                    for c in range(2):
                                        S.op("pe", lambda e: e.matmul(o, lhsT=cw[:, i + 1, 0, c, :], rhs=ef[c][:, 0:SJ], start=False, stop=False, tile_position=(0, ppi * 32)),
                                             reads=[cwk] + efk, writes=[PK[pi]], inc=False)
                                    for c in range(2):
                                        S.op("pe", lambda e: e.matmul(ps[pi][rows, 0:16], lhsT=cw[:, SF - i, 1, c, :], rhs=eb_[c][:, 129:145], start=False, stop=False, tile_position=(0, ppi * 32)),
                                             reads=[cwk] + ebk_, writes=[PK[pi]], inc=False)
                                        S.op("pe", lambda e: e.matmul(ps[pi][rows, 16:SJ], lhsT=cw[:, SF - i, 1, c, :], rhs=eb_[c][:, 1:129], start=False, stop=(c == 1), tile_position=(0, ppi * 32)),
                                             reads=[cwk] + ebk_, writes=[PK[pi]], inc=(c == 1))
                                    S.op("dve", lambda e: e.scalar_tensor_tensor(out=ypre[rows, chn, i::SF], in0=s5u[rows, chn, i, :], scalar=s5d[rows, chn:chn + 1], in1=o,
                                                                                 op0=ALU.mult, op1=ALU.add), reads=[PK[pi], "pp"] + uk, writes=["ypre%d" % pair])
                                    for _ in range(npend):
                                        if pend:
                                            pend.pop(0)()
                                while pend:
                                    pend.pop(0)()

                            zcol = tile(P3, "zcol", [128, 1])
                            S.op("dve", lambda e: e.memset(zcol[:], 0.0), writes=["zcol"])
                            ctx, pend0 = s5_head(0)
                            for f_ in pend0:
                                f_()
                            for pair in range(8):
                                if pair < 7:
                                    nctx, npnd = s5_head(pair + 1)
                                else:
                                    nctx, npnd = None, []
                                s5_tail(ctx, npnd)
                                ctx = nctx
                            yg = [ypre[:, 0, :], ypre[:, 1, :]]
                            ypk = ["ypre%d" % p for p in range(8)]
                            for c in range(2):
                                for s0 in range(0, T, 768):
                                    S.op("act", lambda e: e.activation(out=yg[c][:, s0:s0 + 768], in_=ypre[:, c, s0:s0 + 768], func=AF.Gelu_apprx_tanh), reads=ypk, writes=["yg"] + ypk)
                            sgt = [tile(P3, "sgt%d" % i, [128, 512]) for i in range(2)]
                            it = 0
                            for s0 in list(range(0, 2048, 512)) + [2048]:
                                n = min(512, T - s0)
                                for oc in range(2):
                                    pi = it % 2
                                    sg = sgt[it % 2]
                                    sgk = "sgt%d" % (it % 2)
                                    it += 1
                                    for kc in range(2):
                                        S.op("pe", lambda e: e.matmul(ps[pi][:, 0:n], lhsT=gluw[:, kc, oc * 128:(oc + 1) * 128], rhs=yg[kc][:, s0:s0 + n], start=(kc == 0), stop=(kc == 1)),
                                             reads=["gluw", "yg"], writes=[PK[pi]], inc=(kc == 1))
                                    S.op("act", lambda e: e.activation(out=sg[:, 0:n], in_=ps[pi][:, 0:n], func=AF.Sigmoid, bias=glub[:, oc:oc + 1]), reads=[PK[pi], "pp"], writes=[sgk])
                                    S.op("dve", lambda e: e.tensor_tensor(out=yad[:, 2 + oc, s0:s0 + n], in0=yg[oc][:, s0:s0 + n], in1=sg[:, 0:n], op=ALU.mult), reads=["yg", sgk], writes=["yad_d"])
                            S.barrier()

                    with ExitStack() as P4:
                        wB = tile(P4, "wB", [128, 8, 1024], BF16)
                        wo = tile(P4, "wo", [128, 8, 1024], BF16)
                        swT = tile(P4, "swT", [128, 4, 128], BF16)
                        sbias = tile(P4, "sbias", [128, 2, 128])
                        hT = [tile(P4, "h3T%d" % i, [128, 8, NB], BF16) for i in range(1)]
                        ybc = [tile(P4, "ybc%d" % i, [128, 4, NB], BF16) for i in range(2)]
                        uT = tile(P4, "uT", [128, 2, NB], BF16)
                        gv = tile(P4, "gv", [128, 256])
                        vtok = tile(P4, "vtok", [128, 256], BF16)
                        st6 = tile(P4, "s_st6", [128, 6])
                        mv = tile(P4, "s_mv", [128, 2])
                        rs1 = tile(P4, "s_rs", [128, 1])
                        sgtmp = tile(P4, "sgtmp", [128, NB])
                        sgtmp2 = tile(P4, "sgtmp2", [128, 128])
                        padl = tile(P4, "padl", [128, 2, 4, 94])
                        padc = tile(P4, "padc", [128, 2, 286])
                        cacc = tile(P4, "cacc", [128, 2, NB])
                        sq8 = tile(P4, "sq8", [128, 8, NB])
                        st = {"mean": tile(P4, "mean", [128, NB]), "m2": tile(P4, "m2", [128, NB]), "rstd": tile(P4, "rstd", [128, NB])}
                        tnorm = cacc
                        tmp8 = tile(P4, "tmp8", [128, NB])
                        h2T = tile(P4, "h2T", [128, 8, NB], BF16)
                        actT = tile(P4, "actT", [128, NJ, NB], BF16)
                        sgf = [tile(P4, "sgf%d" % i, [128, NB]) for i in range(2)]
                        w1p = [tile(P4, "w1p%d" % i, [128, 8, 256], BF16) for i in range(3)]
                        w2p = [tile(P4, "w2p%d" % i, [128, NJ, 128], BF16) for i in range(2)]
                        A2 = tile(P4, "A2", [128, 8, 3])
                        B2 = tile(P4, "B2", [128, 8, 3])
                        S.op("dve", lambda e: e.tensor_tensor(out=A2[:], in0=ops2[:, l, :, :], in1=ln1w.unsqueeze(2).to_broadcast([128, 8, 3]), op=ALU.mult), reads=["pp", "ops2"], writes=["A2B2"])
                        S.op("dve", lambda e: e.tensor_tensor(out=B2[:], in0=ops2[:, l, :, :], in1=ln1b.unsqueeze(2).to_broadcast([128, 8, 3]), op=ALU.mult), reads=["pp", "ops2", "A2B2"], writes=["A2B2"])
                        S.op("dve", lambda e: e.tensor_tensor(out=B2[:], in0=B2[:], in1=modT[:, l, 24:32, :], op=ALU.add), reads=["modT", "A2B2"], writes=["A2B2"])
                        S.dma("pool", wB[:], winB_d[l], writes=["wB"])
                        S.dma("pool", wo[:], wout_d[l], writes=["wo"])
                        S.dma("pool", swT[:], sguw_d[l], writes=["swT"])
                        S.dma("sp", sbias[:], sgub_d[l], writes=["sbias"])
                        S.op("dve", lambda e: e.memset(padl[:], 0.0), writes=["padl"])
                        S.op("dve", lambda e: e.memset(padc[:], 0.0), writes=["padc"])
                        w1i = 0
                        w2i = 0
                        blocks = list(range(1, NBLK)) if last else list(range(NBLK))
                        stc = {"mean": tile(P4, "meanc", [128, NB]), "m2": tile(P4, "m2c", [128, NB]), "rstd": tile(P4, "rstdc", [128, NB]),
                               "keys": ("meanc", "m2c", "rstdc")}
                        w1i = [0]
                        w2i = [0]
                        h = hT[0]
                        hk = "h3T0"

                        def make_h(blk):
                            c0 = blk * NB
                            mc = mcol(blk)
                            xk = "xT%d" % blk
                            for fc in range(8):
                                S.op("act", lambda e: e.activation(out=h[:, fc, :], in_=xT[:, fc, c0:c0 + NB], func=AF.Identity,
                                                                   scale=ops1[:, l, fc, mc:mc + 1], bias=modT[:, l, fc, mc:mc + 1]),
                                     reads=[xk, "ops1", "modT"], writes=[hk])

                        def front(blk):
                            c0 = blk * NB
                            mc = mcol(blk)
                            yb = ybc[blk % 2]
                            ybk = "ybc%d" % (blk % 2)
                            xk = "xT%d" % blk

                            def fmproj(wc, pi):
                                for kc in range(8):
                                    S.op("pe", lambda e: e.matmul(ps[pi][:, 0:NB], lhsT=wB[:, kc, wc:wc + 128], rhs=h[:, kc, :], start=(kc == 0), stop=(kc == 7)),
                                         reads=["wB", hk], writes=[PK[pi]], inc=(kc == 7))
                            taps = []
                            for c in range(2):
                                fmproj(512 + c * 128, 4)
                                fmproj(768 + c * 128, 5)
                                S.op("act", lambda e: e.activation(out=sgtmp[:], in_=ps[5][:, 0:NB], func=AF.Sigmoid), reads=[PK[5]], writes=["sgtmp"])
                                if blk == 0:
                                    S.op("dve", lambda e: e.tensor_tensor(out=padc[:, c, 15:271], in0=ps[4][:, 0:NB], in1=sgtmp[:], op=ALU.mult), reads=[PK[4], "sgtmp"], writes=["padc"])
                                    srcs = [padc[:, c, k:k + 256] for k in range(31)]
                                    acc = cacc[:, c, :]
                                    pk = "padc"
                                else:
                                    S.op("dve", lambda e: e.tensor_tensor(out=padl[:, c, :, 15:79], in0=ps[4][:, 0:NB].rearrange("p (a b) -> p a b", b=64),
                                                                          in1=sgtmp[:].rearrange("p (a b) -> p a b", b=64), op=ALU.mult), reads=[PK[4], "sgtmp"], writes=["padl"])
                                    srcs = [padl[:, c, :, k:k + 64] for k in range(31)]
                                    acc = cacc[:, c, :].rearrange("p (a b) -> p a b", b=64)
                                    pk = "padl"

                                def mk(k, c=c, srcs=srcs, acc=acc, pk=pk):
                                    if k == 0:
                                        return lambda: S.op("dve", lambda e: e.tensor_scalar_mul(out=acc, in0=srcs[0], scalar1=convw[:, c, 0:1]), reads=[pk, "pp"], writes=["cacc%d" % c])
                                    if k == 31:
                                        return lambda: S.op("dve", lambda e: e.tensor_scalar_add(out=cacc[:, c, :], in0=cacc[:, c, :], scalar1=convb[:, c:c + 1]), reads=["cacc%d" % c, "pp"], writes=["cacc%d" % c])
                                    return lambda: S.op("dve", lambda e: e.scalar_tensor_tensor(out=acc, in0=srcs[k], scalar=convw[:, c, k:k + 1], in1=acc, op0=ALU.mult, op1=ALU.add),
                                                        reads=[pk, "pp", "cacc%d" % c], writes=["cacc%d" % c])
                                taps += [mk(k) for k in range(32)]
                            for c in range(2):
                                fmproj(c * 128, c)
                                S.op("act", lambda e: e.activation(out=uT[:, c, :], in_=ps[c][:, 0:NB], func=AF.Gelu_apprx_tanh), reads=[PK[c]], writes=["uT"])
                            for cc in range(2):
                                for kc in range(8):
                                    S.op("pe", lambda e: e.matmul(ps[2][:, 0:256], lhsT=h[:, kc, cc * 128:(cc + 1) * 128], rhs=wB[:, kc, 256:512], start=(kc == 0), stop=(kc == 7)),
                                         reads=["wB", hk], writes=[PK[2]], inc=(kc == 7))
                                S.op("act", lambda e: e.activation(out=gv[:], in_=ps[2][:, 0:256], func=AF.Gelu_apprx_tanh), reads=[PK[2]], writes=["gv"])
                                S.op("dve", lambda e: e.bn_stats(out=st6[:], in_=gv[:]), reads=["gv"], writes=["s_st6"])
                                S.op("dve", lambda e: e.bn_aggr(out=mv[:], in_=st6[:]), reads=["s_st6"], writes=["s_mv"])
                                S.op("act", lambda e: e.activation(out=rs1[:], in_=mv[:, 1:2], func=AF.Sqrt, bias=epsc[:, 0:1]), reads=["s_mv", "epsc"], writes=["s_rs"])
                                S.op("dve", lambda e: e.reciprocal(out=rs1[:], in_=rs1[:]), reads=["s_rs"], writes=["s_rs"])
                                S.op("dve", lambda e: e.tensor_scalar(out=gv[:], in0=gv[:], scalar1=mv[:, 0:1], scalar2=rs1[:, 0:1], op0=ALU.subtract, op1=ALU.mult),
                                     reads=["gv", "s_mv", "s_rs"], writes=["gv"])
                                S.op("dve", lambda e: e.tensor_tensor(out=gv[:], in0=gv[:], in1=sgulnw, op=ALU.mult), reads=["gv", "bcr"], writes=["gv"])
                                S.op("dve", lambda e: e.tensor_tensor(out=vtok[:], in0=gv[:], in1=sgulnb, op=ALU.add), reads=["gv", "bcr"], writes=["vtok"])
                                for pr in range(2):
                                    for g2 in range(2):
                                        g = pr * 2 + g2
                                        S.op("pe", lambda e: e.matmul(ps[3][g2 * 64:(g2 + 1) * 64, 0:128], lhsT=vtok[:, g * 64:(g + 1) * 64], rhs=swT[:, g, :], start=True, stop=True,
                                                                      tile_position=(0, g2 * 64)), reads=["vtok", "swT"], writes=[PK[3]], inc=(g2 == 1))
                                    S.op("dve", lambda e: e.tensor_tensor(out=sgtmp2[:, 0:128], in0=ps[3][:, 0:128], in1=sbias[:, pr, :], op=ALU.add), reads=[PK[3], "sbias"], writes=["sgtmp2"])
                                    S.op("dve", lambda e: e.tensor_tensor(out=yb[:, pr, cc * 128:(cc + 1) * 128], in0=sgtmp2[:, 0:128], in1=uT[:, pr, cc * 128:(cc + 1) * 128], op=ALU.mult),
                                         reads=["sgtmp2", "uT"], writes=[ybk + "s"])
                            return taps

                        def back(blk):
                            c0 = blk * NB
                            yb = ybc[blk % 2]
                            ybk = "ybc%d" % (blk % 2)
                            ln_stats(stc, lambda c: cacc[:, c, :], 2, avgC, sq8, "sq8", 6, 7, NB, ["cacc0", "cacc1"], "c")
                            S.op("dve", lambda e: e.tensor_tensor(out=tnorm[:], in0=cacc[:], in1=stc["mean"][:].unsqueeze(1).to_broadcast([128, 2, NB]), op=ALU.subtract),
                                 reads=["cacc0", "cacc1", "meanc"], writes=["cacc0", "cacc1"])
                            S.op("dve", lambda e: e.tensor_tensor(out=tnorm[:], in0=tnorm[:], in1=stc["rstd"][:].unsqueeze(1).to_broadcast([128, 2, NB]), op=ALU.mult),
                                 reads=["cacc0", "cacc1", "rstdc"], writes=["cacc0", "cacc1"])
                            for c in range(2):
                                S.op("act", lambda e: e.activation(out=yb[:, 2 + c, :], in_=tnorm[:, c, :], func=AF.Silu, scale=convlnw[:, c:c + 1], bias=convlnb[:, c:c + 1]),
                                     reads=["cacc0", "cacc1", "pp"], writes=[ybk + "c"])
                            if debug and b == 0 and l == 0:
                                S.dma("pool", dbg_mix[:, 2:6, c0:c0 + NB], yb[:], reads=[ybk + "s", ybk + "c"])

                        def ln_apply(blk, lw, lb, with_h2, nxt=None):
                            c0 = blk * NB
                            mc = mcol(blk)
                            xk = "xT%d" % blk
                            xv = xT[:, :, c0:c0 + NB]
                            ln_stats(st, lambda c: xT[:, c, c0:c0 + NB], 8, avgD, sq8, "sq8", 6, 7, NB, [xk], "r")
                            if nxt is not None:
                                make_h(nxt)
                            S.op("dve", lambda e: e.tensor_tensor(out=xv, in0=xv, in1=st["mean"][:].unsqueeze(1).to_broadcast([128, 8, NB]), op=ALU.subtract), reads=[xk, "mean"], writes=[xk])
                            S.op("dve", lambda e: e.tensor_tensor(out=xv, in0=xv, in1=st["rstd"][:].unsqueeze(1).to_broadcast([128, 8, NB]), op=ALU.mult), reads=[xk, "rstd"], writes=[xk])
                            if with_h2:
                                for fc in range(8):
                                    S.op("act", lambda e: e.activation(out=h2T[:, fc, :], in_=xT[:, fc, c0:c0 + NB], func=AF.Identity,
                                                                       scale=A2[:, fc, mc:mc + 1], bias=B2[:, fc, mc:mc + 1]),
                                         reads=[xk, "A2B2"], writes=["h2T"])
                            S.op("pool", lambda e: e.tensor_tensor(out=xv, in0=xv, in1=lw.unsqueeze(2).to_broadcast([128, 8, NB]), op=ALU.mult), reads=[xk, "pp"], writes=[xk])
                            S.op("pool", lambda e: e.tensor_tensor(out=xv, in0=xv, in1=lb.unsqueeze(2).to_broadcast([128, 8, NB]), op=ALU.add), reads=[xk, "pp"], writes=[xk])

                        def wout_ln1(blk, nxt=None):
                            c0 = blk * NB
                            mc = mcol(blk)
                            yb = ybc[blk % 2]
                            ybk = "ybc%d" % (blk % 2)
                            xk = "xT%d" % blk
                            mix = [yad[:, 0, c0:c0 + NB], yad[:, 1, c0:c0 + NB], yb[:, 0, :], yb[:, 1, :], yb[:, 2, :], yb[:, 3, :], yad[:, 2, c0:c0 + NB], yad[:, 3, c0:c0 + NB]]
                            mixk = ["yad_a%d" % (2 * blk), "yad_a%d" % (2 * blk + 1), "yad_d", ybk + "s", ybk + "c"]
                            for oc in range(8):
                                pi = oc % 4
                                for kc in range(8):
                                    S.op("pe", lambda e: e.matmul(ps[pi][:, 0:NB], lhsT=wo[:, kc, oc * 128:(oc + 1) * 128], rhs=mix[kc], start=(kc == 0), stop=(kc == 7)),
                                         reads=["wo"] + mixk, writes=[PK[pi]], inc=(kc == 7))
                                S.op("act", lambda e: e.activation(out=tmp8[:], in_=ps[pi][:, 0:NB], func=AF.Identity, scale=modT[:, l, 16 + oc, mc:mc + 1]), reads=[PK[pi], "modT"], writes=["tmp8"])
                                S.op("dve", lambda e: e.scalar_tensor_tensor(out=xT[:, oc, c0:c0 + NB], in0=xT[:, oc, c0:c0 + NB], scalar=ALPHA, in1=tmp8[:], op0=ALU.mult, op1=ALU.add),
                                     reads=[xk, "tmp8"], writes=[xk])
                            ln_apply(blk, ln1w, ln1b, True, nxt)

                        def ffn(blk, pending, after_in):
                            c0 = blk * NB
                            mc = mcol(blk)
                            xk = "xT%d" % blk
                            for j in range(NJ):
                                wt = w1p[w1i[0] % 3]
                                wk = "w1p%d" % (w1i[0] % 3)
                                w1i[0] += 1
                                S.dma("sp", wt[:].rearrange("p a b -> p (a b)"), w1s[l, j], reads=["w1s_%d_%d" % (l, j)], writes=[wk])
                                for half in range(2):
                                    pi = half * 2 + (j % 2)
                                    for kc in range(8):
                                        S.op("pe", lambda e: e.matmul(ps[pi][:, 0:NB], lhsT=wt[:, kc, half * 128:(half + 1) * 128], rhs=h2T[:, kc, :], start=(kc == 0), stop=(kc == 7)),
                                             reads=[wk, "h2T"], writes=[PK[pi]], inc=(kc == 7))
                                sg = sgf[j % 2]
                                sgk = "sgf%d" % (j % 2)
                                S.op("act", lambda e: e.activation(out=sg[:], in_=ps[j % 2][:, 0:NB], func=AF.Silu), reads=[PK[j % 2]], writes=[sgk])
                                S.op("dve", lambda e: e.tensor_tensor(out=actT[:, j, :], in0=ps[2 + j % 2][:, 0:NB], in1=sg[:], op=ALU.mult), reads=[PK[2 + j % 2], sgk], writes=["actT%d" % j])
                                for _ in range(3):
                                    if pending:
                                        pending.pop(0)()
                            while pending:
                                pending.pop(0)()
                            after_in()
                            ak = ["actT%d" % j for j in range(NJ)]
                            for oc in range(8):
                                wt = w2p[w2i[0] % 2]
                                wk = "w2p%d" % (w2i[0] % 2)
                                w2i[0] += 1
                                S.dma("sp", wt[:].rearrange("p a b -> p (a b)"), w2s[l, oc], reads=["w2s_%d_%d" % (l, oc)], writes=[wk])
                                pi = 4 + oc % 2
                                for j in range(NJ):
                                    S.op("pe", lambda e: e.matmul(ps[pi][:, 0:NB], lhsT=wt[:, j, :], rhs=actT[:, j, :], start=(j == 0), stop=(j == NJ - 1)),
                                         reads=[wk] + ak, writes=[PK[pi]], inc=(j == NJ - 1))
                                S.op("act", lambda e: e.activation(out=tmp8[:], in_=ps[pi][:, 0:NB], func=AF.Identity, scale=modT[:, l, 40 + oc, mc:mc + 1]), reads=[PK[pi], "modT"], writes=["tmp8"])
                                S.op("dve", lambda e: e.scalar_tensor_tensor(out=xT[:, oc, c0:c0 + NB], in0=xT[:, oc, c0:c0 + NB], scalar=ALPHA, in1=tmp8[:], op0=ALU.mult, op1=ALU.add),
                                     reads=[xk, "tmp8", "h2T"], writes=[xk])
                            ln_apply(blk, ln2w, ln2b, False)

                        make_h(blocks[0])
                        tp0 = front(blocks[0])
                        for t_ in tp0:
                            t_()
                        back(blocks[0])
                        for bi, blk in enumerate(blocks):
                            wout_ln1(blk, blocks[bi + 1] if bi + 1 < len(blocks) else None)
                            if bi + 1 < len(blocks):
                                nb_ = blocks[bi + 1]
                                pend = front(nb_)
                                ffn(blk, pend, lambda: back(nb_))
                            else:
                                ffn(blk, [], lambda: None)
                        if debug and b == 0 and l == 0:
                            S.dma("pool", dbg_mix[:, 0:2, :], yad[:, 0:2, :], reads=["yad_a%d" % i for i in range(NCH)])
                            S.dma("pool", dbg_mix[:, 6:8, :], yad[:, 2:4, :], reads=["yad_d"])
                            S.dma("sp", dbg_x, xT[:], reads=["xT%d" % i for i in range(NBLK)])
                        S.barrier()

            with ExitStack() as PO:
                otok = [tile(PO, "otok%d" % i, [128, D]) for i in range(2)]
                for ch in range(2, NCH):
                    ot = otok[ch % 2]
                    ok = "otok%d" % (ch % 2)
                    for half in range(2):
                        pi = (ch * 2 + half) % 4
                        for q in range(4):
                            fc = half * 4 + q
                            S.op("pe", lambda e: e.transpose(out=ps[pi][:, q * 128:(q + 1) * 128], in_=xT[:, fc, ch * 128:(ch + 1) * 128], identity=ident),
                                 reads=["xT%d" % (ch // 2), "consts"], writes=[PK[pi]], inc=(q == 3))
                        if half == 0:
                            S.op("dve", lambda e: e.tensor_copy(out=ot[:, 0:512], in_=ps[pi][:, :]), reads=[PK[pi]], writes=[ok])
                        else:
                            S.op("act", lambda e: e.activation(out=ot[:, 512:1024], in_=ps[pi][:, :], func=AF.Identity), reads=[PK[pi]], writes=[ok])
                    S.dma("sp", out_d[b, (ch - 2) * 128:(ch - 1) * 128, :], ot[:], reads=[ok])
                S.barrier()
        S.barrier()
    return nc


def _host_prep(inp):
    f = lambda a: np.ascontiguousarray(np.asarray(a, dtype=np.float32))
    w_in = f(inp["w_in"])
    sh = {}
    sh["w_mod"] = f(inp["w_mod"])
    sh["bmodT"] = f(np.asarray(inp["b_mod"]).reshape(2, 48, 128).transpose(2, 0, 1))
    colsA = np.concatenate([np.arange(0, 1040), np.arange(2064, 2320)])
    sh["w_inA"] = f(w_in[:, :, colsA].reshape(2, 8, 128, 1296).transpose(0, 2, 1, 3))
    sh["w_inB"] = f(w_in[:, :, 1040:2064].reshape(2, 8, 128, 1024).transpose(0, 2, 1, 3))
    sh["w_outh"] = f(np.asarray(inp["w_out"]).reshape(2, 8, 128, 1024).transpose(0, 2, 1, 3))
    w1 = np.asarray(inp["w_ffn_in"], dtype=np.float32).reshape(2, 8, 128, 2, NJ, 128)
    sh["w1h"] = f(w1.transpose(0, 4, 2, 1, 3, 5).reshape(2, NJ, 128, 8, 256))
    w2 = np.asarray(inp["w_ffn_out"], dtype=np.float32).reshape(2, NJ, 128, 8, 128)
    sh["w2h"] = f(w2.transpose(0, 3, 2, 1, 4))
    pp = np.zeros((2, 128, NPP), np.float32)
    r8 = lambda a: np.asarray(a).reshape(2, -1, 128).transpose(0, 2, 1)
    pp[:, :, 0:8] = r8(inp["ln1_w"]); pp[:, :, 8:16] = r8(inp["ln1_b"])
    pp[:, :, 16:24] = r8(inp["ln2_w"]); pp[:, :, 24:32] = r8(inp["ln2_b"])
    pp[:, :, 32:94] = np.asarray(inp["conv_w"]).reshape(2, 31, 2, 128).transpose(0, 3, 2, 1).reshape(2, 128, 62)
    pp[:, :, 94:96] = r8(inp["conv_b"]); pp[:, :, 96:98] = r8(inp["conv_ln_w"]); pp[:, :, 98:100] = r8(inp["conv_ln_b"])
    pp[:, :, 100:102] = r8(inp["s5_d"]); pp[:, :, 102:104] = r8(inp["s5_glu_b"])
    sh["pp"] = pp
    sh["bc"] = f(np.concatenate([np.asarray(inp["mlstm_gate_bias"]), np.asarray(inp["mlstm_norm_w"]),
                                 np.asarray(inp["sgu_ln_w"]), np.asarray(inp["sgu_ln_b"])], axis=1))
    sh["sguwT"] = f(np.asarray(inp["sgu_w"]).transpose(0, 3, 1, 2))
    sb = np.asarray(inp["sgu_b"])
    sh["sgub"] = f(np.repeat(sb.reshape(2, 2, 2, 1, 128), 64, axis=3).reshape(2, 2, 128, 128).transpose(0, 2, 1, 3))
    bre, bim = np.asarray(inp["s5_b_re"]), np.asarray(inp["s5_b_im"])
    are, aim, ldt = np.asarray(inp["s5_a_re"]), np.asarray(inp["s5_a_im"]), np.asarray(inp["s5_log_dt"])
    s5B = np.zeros((2, 128, 2, 2, 128), np.float32)
    s5A = np.zeros((2, 128, 3, 2, 2, 128), np.float32)
    for chn in range(2):
        for ppi in range(4):
            for g2 in range(2):
                g = chn * 8 + ppi * 2 + g2
                rows = slice(ppi * 32 + g2 * 16, ppi * 32 + g2 * 16 + 16)
                s5B[:, rows, 0, chn, g2 * 64:(g2 + 1) * 64] = bre[:, g].transpose(0, 2, 1)
                s5B[:, rows, 1, chn, g2 * 64:(g2 + 1) * 64] = bim[:, g].transpose(0, 2, 1)
            for g2c in range(2):
                gc = chn * 8 + ppi * 2 + g2c
                rows = slice(ppi * 32, ppi * 32 + 32)
                for d in range(2):
                    s5A[:, rows, 0, d, chn, g2c * 64:(g2c + 1) * 64] = are[:, d, gc][:, None, :]
                    s5A[:, rows, 1, d, chn, g2c * 64:(g2c + 1) * 64] = aim[:, d, gc][:, None, :]
                    s5A[:, rows, 2, d, chn, g2c * 64:(g2c + 1) * 64] = ldt[:, d, gc][:, None, None]
    sh["s5B"], sh["s5A"] = s5B, s5A
    cre, cim = np.asarray(inp["s5_c_re"]), np.asarray(inp["s5_c_im"])
    s5C = np.zeros((2, 128, 2, 8, 32), np.float32)
    s5Bn = np.zeros((2, 128, 2, 8, 32), np.float32)
    s5pp = np.zeros((2, 128, 3, 16), np.float32)
    for pair in range(8):
        for g2 in range(2):
            g = pair * 2 + g2
            s5C[:, g2 * 64:(g2 + 1) * 64, 0, pair, g2 * 16:(g2 + 1) * 16] = cre[:, g].transpose(0, 2, 1)
            s5C[:, g2 * 64:(g2 + 1) * 64, 1, pair, g2 * 16:(g2 + 1) * 16] = cim[:, g].transpose(0, 2, 1)
            s5Bn[:, g2 * 64:(g2 + 1) * 64, 0, pair, g2 * 16:(g2 + 1) * 16] = bre[:, g]
            s5Bn[:, g2 * 64:(g2 + 1) * 64, 1, pair, g2 * 16:(g2 + 1) * 16] = bim[:, g]
            for d in range(2):
                s5pp[:, g2 * 64:(g2 + 1) * 64, 0, d * 8 + pair] = are[:, d, g]
                s5pp[:, g2 * 64:(g2 + 1) * 64, 1, d * 8 + pair] = aim[:, d, g]
                s5pp[:, g2 * 64:(g2 + 1) * 64, 2, d * 8 + pair] = ldt[:, d, g][:, None]
    sh["s5C"], sh["s5pp"], sh["s5Bn"] = s5C, s5pp, s5Bn
    sh["gluh"] = f(np.asarray(inp["s5_glu_w"]).reshape(2, 2, 128, 256).transpose(0, 2, 1, 3))
    cst = np.zeros((128, 3, 128), np.float32)
    cst[:, 0] = np.eye(128)
    cst[:, 1] = np.triu(np.ones((128, 128)))
    cst[:, 2] = np.tril(np.ones((128, 128)))
    sh["consts"] = cst
    return sh


def _core_inputs(inp, shared, core):
    x, c, ctx, c_ctx = (np.asarray(inp[k], dtype=np.float32) for k in ("x", "c", "ctx", "c_ctx"))
    m = dict(shared)
    bs = [2 * core, 2 * core + 1]
    m["xin"] = np.ascontiguousarray(np.concatenate([ctx[bs], x[bs]], axis=1))
    cv = np.stack([c[bs[0]], c[bs[1]], c_ctx], axis=1)
    m["cT"] = np.ascontiguousarray(cv.reshape(8, 128, 3).transpose(1, 0, 2))
    return m


def kernel(**inputs):
    shared = _host_prep(inputs)
    nc = build_nc()
    in_maps = [_core_inputs(inputs, shared, core) for core in range(8)]
    res = run_bass_kernel_spmd(nc, in_maps, core_ids=list(range(8)))
    out = np.concatenate([np.asarray(r["out"]) for r in res.results], axis=0)
    return out.astype(np.float32)
```

```python
import math
from contextlib import ExitStack
import numpy as np
import concourse.bass as bass
import concourse.mybir as mybir
from concourse.bass_utils import run_bass_kernel_spmd

F32 = mybir.dt.float32
BF16 = mybir.dt.bfloat16
AF = mybir.ActivationFunctionType
ALU = mybir.AluOpType

D = 1024
T = 2304
NB = 256
NBLK = 9
NCH = 18
DFF = 2816
NJ = 22
ALPHA = 4.0 ** 0.25
EPS = 1e-5
PI = math.pi
NPP = 104
NBC = 784
SF = 16
SJ = T // SF


class Sync:
    NDMA = 40

    def __init__(self, nc, stack):
        self.nc = nc
        self.eng = {"pe": nc.tensor, "dve": nc.vector, "act": nc.scalar, "pool": nc.gpsimd, "sp": nc.sync}
        self.sem = {k: stack.enter_context(nc.semaphore("s_" + k)) for k in self.eng}
        self.cnt = {k: 0 for k in self.eng}
        self.seen = {k: {} for k in self.eng}
        self.dsem = [stack.enter_context(nc.semaphore("d%d" % i)) for i in range(self.NDMA)]
        self.dcnt = [0] * self.NDMA
        self.dnext = 0
        self.dnext_sw = 0
        self.lw = {}
        self.rd = {}
        self.semobj = {}
        for k in self.eng:
            self.semobj[("e", k)] = self.sem[k]
        for i in range(self.NDMA):
            self.semobj[("d", i)] = self.dsem[i]

    def _wait(self, e, sid, val):
        if self.seen[e].get(sid, 0) >= val:
            return
        if sid[0] == "e" and val > self.cnt[sid[1]]:
            if sid[1] == e:
                return
            raise RuntimeError("wait on unsignalled instruction: %s waits %s >= %d (cnt %d)" % (e, sid, val, self.cnt[sid[1]]))
        self.eng[e].wait_ge(self.semobj[sid], val)
        self.seen[e][sid] = val

    def _deps(self, e, reads, writes):
        for k in reads:
            if k in self.lw:
                self._wait(e, *self.lw[k])
        for k in writes:
            if k in self.lw:
                self._wait(e, *self.lw[k])
            for sid, val in self.rd.get(k, {}).items():
                self._wait(e, sid, val)

    def _record(self, sid, val, reads, writes):
        for k in reads:
            d = self.rd.setdefault(k, {})
            d[sid] = max(d.get(sid, 0), val)
        for k in writes:
            self.lw[k] = (sid, val)
            self.rd[k] = {}

    def op(self, e, fn, reads=(), writes=(), inc=True):
        self._deps(e, reads, writes)
        inst = fn(self.eng[e])
        if inc:
            self.cnt[e] += 1
            inst.then_inc(self.sem[e], 1)
            val = self.cnt[e]
        else:
            val = self.cnt[e] + 1
        self._record(("e", e), val, reads, writes)

    def dma(self, e, out, in_, reads=(), writes=(), **kw):
        half = self.NDMA // 2
        if e == "pool":
            s = half + self.dnext_sw
            self.dnext_sw = (self.dnext_sw + 1) % half
        else:
            s = self.dnext
            self.dnext = (self.dnext + 1) % half
        sid = ("d", s)
        if self.dcnt[s] > 0:
            self._wait(e, sid, self.dcnt[s])
        self._deps(e, reads, writes)
        self.dcnt[s] += 16
        self.eng[e].dma_start(out=out, in_=in_, **kw).then_inc(self.dsem[s], 16)
        self._record(sid, self.dcnt[s], reads, writes)

    def barrier(self):
        for e in self.eng:
            for k in self.eng:
                if k != e and self.cnt[k] > 0:
                    self._wait(e, ("e", k), self.cnt[k])
            for i in range(self.NDMA):
                if self.dcnt[i] > 0:
                    self._wait(e, ("d", i), self.dcnt[i])
        self.lw = {}
        self.rd = {}


def build_nc(nbatch=2, nlayer=2, debug=False):
    nc = bass.Bass("TRN2", target_bir_lowering=False)

    def din(name, shape):
        return nc.dram_tensor(name, list(shape), F32, kind="ExternalInput").ap()

    xin = din("xin", [2, T, D])
    cT_d = din("cT", [128, 8, 3])
    wmod_d = din("w_mod", [2, D, 6 * D])
    bmod_d = din("bmodT", [128, 2, 48])
    winA_d = din("w_inA", [2, 128, 8, 1296])
    winB_d = din("w_inB", [2, 128, 8, 1024])
    wout_d = din("w_outh", [2, 128, 8, 1024])
    w1_d = din("w1h", [2, NJ, 128, 8, 256])
    w2_d = din("w2h", [2, 8, 128, NJ, 128])
    pp_d = din("pp", [2, 128, NPP])
    bc_d = din("bc", [2, NBC])
    sguw_d = din("sguwT", [2, 128, 4, 128])
    sgub_d = din("sgub", [2, 128, 2, 128])
    s5B_d = din("s5B", [2, 128, 2, 2, 128])
    s5A_d = din("s5A", [2, 128, 3, 2, 2, 128])
    s5C_d = din("s5C", [2, 128, 2, 8, 32])
    s5Bn_d = din("s5Bn", [2, 128, 2, 8, 32])
    s5pp_d = din("s5pp", [2, 128, 3, 16])
    glu_d = din("gluh", [2, 128, 2, 256])
    consts_d = din("consts", [128, 3, 128])
    out_d = nc.dram_tensor("out", [2, 2048, D], F32, kind="ExternalOutput").ap()
    w1s = nc.dram_tensor("w1s", [2, NJ, 128, 8 * 256], BF16, kind="Internal").ap()
    w2s = nc.dram_tensor("w2s", [2, 8, 128, NJ * 128], BF16, kind="Internal").ap()
    cWE = nc.dram_tensor("cWE", [2, 128, 16 * 2 * 512], BF16, kind="Internal").ap()
    cG = nc.dram_tensor("cG", [2, 128, 2 * 2 * 16 * 128], BF16, kind="Internal").ap()
    cG0 = nc.dram_tensor("cG0", [2, 128, 256], BF16, kind="Internal").ap()
    cPW = nc.dram_tensor("cPW", [2, 128, 2 * 17 * 16], F32, kind="Internal").ap()
    cPW2 = nc.dram_tensor("cPW2", [2, 128, 2 * 9 * 16], F32, kind="Internal").ap()
    cNPW2 = nc.dram_tensor("cNPW2", [2, 128, 9 * 16], F32, kind="Internal").ap()
    if debug:
        dbg_mix = nc.dram_tensor("dbg_mix", [128, 8, T], F32, kind="ExternalOutput").ap()
        dbg_x = nc.dram_tensor("dbg_x", [128, 8, T], F32, kind="ExternalOutput").ap()
        dbg_mod = nc.dram_tensor("dbg_mod", [128, 2, 48, 3], F32, kind="ExternalOutput").ap()
        dbg_h = nc.dram_tensor("dbg_h", [128, 8, NB], F32, kind="ExternalOutput").ap()
        dbg_qk = nc.dram_tensor("dbg_qk", [128, 4, T], F32, kind="ExternalOutput").ap()
        dbg_s5u = nc.dram_tensor("dbg_s5u", [128, 2, T], F32, kind="ExternalOutput").ap()
        dbg_xT0 = nc.dram_tensor("dbg_xT0", [128, 8, T], F32, kind="ExternalOutput").ap()

    with ExitStack() as G:
        S = Sync(nc, G)

        uid = [0]

        def tile(st, name, shape, dt=F32):
            uid[0] += 1
            return st.enter_context(nc.sbuf_tensor("t%d_%s" % (uid[0], name), list(shape), dt))

        ps = [G.enter_context(nc.psum_tensor("ps%d" % i, [128, 512], F32)) for i in range(8)]
        PK = ["ps%d" % i for i in range(8)]

        xT = tile(G, "xT", [128, 8, T])
        consts = tile(G, "consts", [128, 3, 128])
        ident = consts[:, 0, :]
        tri_le = consts[:, 1, :]
        tri_ge = consts[:, 2, :]
        ones_f = tile(G, "ones_f", [128, 128])
        avgD = tile(G, "avgD", [128, 128])
        avgC = tile(G, "avgC", [128, 128])
        modT = tile(G, "modT", [128, 2, 48, 3])
        ops1 = tile(G, "ops1", [128, 2, 8, 3])
        ops2 = tile(G, "ops2", [128, 2, 8, 3])

        S.dma("sp", consts[:], consts_d, writes=["consts"])
        epsc = tile(G, "epsc", [128, 1])
        S.op("dve", lambda e: e.memset(epsc[:], EPS), writes=["epsc"])
        S.op("dve", lambda e: e.memset(ones_f[:], 1.0), writes=["ones_f"])
        S.op("dve", lambda e: e.memset(avgD[:], 1.0 / D), writes=["avgD"])
        S.op("dve", lambda e: e.memset(avgC[:], 1.0 / 256.0), writes=["avgC"])

        for l in range(nlayer):
            for j in range(NJ):
                S.dma("pool", w1s[l, j], w1_d[l, j].rearrange("p a b -> p (a b)"), writes=["w1s_%d_%d" % (l, j)])
            for oc in range(8):
                S.dma("pool", w2s[l, oc], w2_d[l, oc].rearrange("p a b -> p (a b)"), writes=["w2s_%d_%d" % (l, oc)])

        with ExitStack() as P0:
            cTt = tile(P0, "cTt", [128, 8, 3])
            scT = tile(P0, "scT", [128, 8, 3])
            bmod = tile(P0, "bmod", [128, 2, 48])
            wm = [tile(P0, "wm%d" % i, [128, 6 * D]) for i in range(2)]
            S.dma("sp", cTt[:], cT_d, writes=["cTt"])
            S.dma("sp", bmod[:], bmod_d, writes=["bmod"])
            S.op("act", lambda e: e.activation(out=scT[:], in_=cTt[:], func=AF.Silu), reads=["cTt"], writes=["scT"])
            it = 0
            for l in range(nlayer):
                for kc in range(8):
                    w = wm[it % 2]
                    wk = "wm%d" % (it % 2)
                    it += 1
                    S.dma("sp", w[:], wmod_d[l, kc * 128:(kc + 1) * 128, :], writes=[wk])
                    pz = kc % 2
                    for j in range(48):
                        S.op("pe", lambda e: e.matmul(ps[pz][:, j * 3:(j + 1) * 3], lhsT=w[:, j * 128:(j + 1) * 128],
                                                      rhs=scT[:, kc, :], start=True, stop=True),
                             reads=[wk, "scT"], writes=[PK[pz]], inc=(j == 47))
                    S.op("dve", lambda e: e.tensor_tensor(
                        out=modT[:, l, :, :], in0=ps[pz][:, 0:144].rearrange("p (a b) -> p a b", b=3),
                        in1=(bmod[:, l, :].unsqueeze(2).to_broadcast([128, 48, 3]) if kc == 0 else modT[:, l, :, :]), op=ALU.add),
                        reads=[PK[pz], "bmod", "modT"], writes=["modT"])
            S.op("dve", lambda e: e.tensor_scalar_add(out=ops1[:], in0=modT[:, :, 8:16, :], scalar1=1.0), reads=["modT"], writes=["ops1"])
            S.op("dve", lambda e: e.tensor_scalar_add(out=ops2[:], in0=modT[:, :, 32:40, :], scalar1=1.0), reads=["modT"], writes=["ops2"])
            if debug:
                S.dma("sp", dbg_mod, modT[:], reads=["modT"])
            S.barrier()

        def ln_stats(st, src_fn, nchunks, avg, sq_tile, sqk, pA, pB, width, srckeys, tag):
            mean = st["mean"]
            m2 = st["m2"]
            rstd = st["rstd"]
            kmean, km2, krstd = st.get("keys", ("mean", "m2", "rstd"))
            for c in range(nchunks):
                S.op("act", lambda e: e.activation(out=sq_tile[:, c, 0:width], in_=src_fn(c), func=AF.Square),
                     reads=srckeys, writes=[sqk])
            for c in range(nchunks):
                S.op("pe", lambda e: e.matmul(ps[pA][:, 0:width], lhsT=avg[:], rhs=src_fn(c), start=(c == 0), stop=(c == nchunks - 1)),
                     reads=srckeys, writes=[PK[pA]], inc=(c == nchunks - 1))
            for c in range(nchunks):
                S.op("pe", lambda e: e.matmul(ps[pB][:, 0:width], lhsT=avg[:], rhs=sq_tile[:, c, 0:width], start=(c == 0), stop=(c == nchunks - 1)),
                     reads=[sqk], writes=[PK[pB]], inc=(c == nchunks - 1))
            S.op("act", lambda e: e.activation(out=mean[:, 0:width], in_=ps[pA][:, 0:width], func=AF.Identity), reads=[PK[pA]], writes=[kmean])
            S.op("dve", lambda e: e.tensor_tensor(out=m2[:, 0:width], in0=mean[:, 0:width], in1=mean[:, 0:width], op=ALU.mult), reads=[kmean], writes=[km2])
            S.op("dve", lambda e: e.tensor_tensor(out=m2[:, 0:width], in0=ps[pB][:, 0:width], in1=m2[:, 0:width], op=ALU.subtract), reads=[PK[pB], km2], writes=[km2])
            S.op("act", lambda e: e.activation(out=rstd[:, 0:width], in_=m2[:, 0:width], func=AF.Ln, bias=epsc[:, 0:1]), reads=[km2, "epsc"], writes=[krstd])
            S.op("act", lambda e: e.activation(out=rstd[:, 0:width], in_=rstd[:, 0:width], func=AF.Exp, scale=-0.5), reads=[krstd], writes=[krstd])

        for b in range(nbatch):
            with ExitStack() as PL:
                xtok = [tile(PL, "xtok%d" % i, [128, D]) for i in range(2)]
                for ch in range(NCH):
                    xt = xtok[ch % 2]
                    xk = "xtok%d" % (ch % 2)
                    S.dma("sp", xt[:], xin[b, ch * 128:(ch + 1) * 128, :], writes=[xk])
                    for half in range(2):
                        pi = (ch * 2 + half) % 4
                        for q in range(4):
                            fc = half * 4 + q
                            S.op("pe", lambda e: e.transpose(out=ps[pi][:, q * 128:(q + 1) * 128], in_=xt[:, fc * 128:(fc + 1) * 128], identity=ident),
                                 reads=[xk, "consts"], writes=[PK[pi]], inc=(q == 3))
                        S.op("dve" if half == 0 else "act",
                             (lambda e: e.tensor_copy(out=xT[:, half * 4:half * 4 + 4, ch * 128:(ch + 1) * 128],
                                                      in_=ps[pi][:, :].rearrange("p (a b) -> p a b", b=128))) if half == 0 else
                             (lambda e: e.activation(out=xT[:, half * 4:half * 4 + 4, ch * 128:(ch + 1) * 128],
                                                     in_=ps[pi][:, :].rearrange("p (a b) -> p a b", b=128), func=AF.Identity)),
                             reads=[PK[pi]], writes=["xT%d" % (ch // 2)])
                S.barrier()

            for l in range(nlayer):
                last = (l == nlayer - 1) and (nlayer == 2)
                with ExitStack() as L:
                    pp = tile(L, "pp", [128, NPP])
                    bcr = tile(L, "bcr", [128, NBC])
                    yad = tile(L, "yad", [128, 4, T], BF16)
                    S.dma("sp", pp[:], pp_d[l], writes=["pp"])
                    S.dma("sp", bcr[:], bc_d[l:l + 1, :].partition_broadcast(128) if False else bc_d[l:l + 1, :].to_broadcast([128, NBC]), writes=["bcr"])
                    ln1w, ln1b, ln2w, ln2b = pp[:, 0:8], pp[:, 8:16], pp[:, 16:24], pp[:, 24:32]
                    convw = pp[:, 32:94].rearrange("p (c k) -> p c k", k=31)
                    convb, convlnw, convlnb = pp[:, 94:96], pp[:, 96:98], pp[:, 98:100]
                    s5d, glub = pp[:, 100:102], pp[:, 102:104]
                    gbias = bcr[:, 0:16]
                    normw = bcr[:, 16:272]
                    sgulnw = bcr[:, 272:528]
                    sgulnb = bcr[:, 528:784]

                    def mcol(blk):
                        return 2 if blk == 0 else b

                    with ExitStack() as P12:
                        s5u = tile(P12, "s5u", [128, 2, SF, SJ], BF16)
                        PM = ExitStack()
                        qkT = tile(PM, "qkT", [128, 4, T], BF16)
                        ktok = tile(PM, "ktok", [128, NCH, 256], BF16)
                        vaug = tile(PM, "vaug", [128, NCH, 4, 65], BF16)
                        sigo = tile(PM, "sigo", [128, NCH, 256], BF16)
                        gates = tile(PM, "gates", [128, NCH, 16])
                        with ExitStack() as P1:
                            wA = tile(P1, "wA", [128, 8, 1296], BF16)
                            hT = [tile(P1, "hT%d" % i, [128, 8, NB], BF16) for i in range(2)]
                            S.dma("pool", wA[:], winA_d[l], writes=["wA"])
                            S.op("dve", lambda e: e.memset(vaug[:], 1.0), writes=["vaug"])
                            for blk in range(NBLK):
                                c0 = blk * NB
                                mc = mcol(blk)
                                h = hT[blk % 2]
                                hk = "hT%d" % (blk % 2)
                                for fc in range(8):
                                    S.op("act", lambda e: e.activation(out=h[:, fc, :], in_=xT[:, fc, c0:c0 + NB], func=AF.Identity,
                                                                       scale=ops1[:, l, fc, mc:mc + 1], bias=modT[:, l, fc, mc:mc + 1]),
                                         reads=["xT%d" % blk, "ops1", "modT"], writes=[hk])
                                if debug and b == 0 and l == 0 and blk == 1:
                                    S.dma("pool", dbg_h, h[:], reads=[hk])
                                fm = [(0, qkT, 0, 1.0), (128, qkT, 1, 1.0), (256, qkT, 2, 0.125), (384, qkT, 3, 0.125),
                                      (1040, s5u, 0, 1.0), (1168, s5u, 1, 1.0)]
                                for fi, (wc, dst, dc, scl) in enumerate(fm):
                                    pi = fi % 2
                                    for kc in range(8):
                                        S.op("pe", lambda e: e.matmul(ps[pi][:, 0:NB], lhsT=wA[:, kc, wc:wc + 128], rhs=h[:, kc, :],
                                                                      start=(kc == 0), stop=(kc == 7)),
                                             reads=["wA", hk], writes=[PK[pi]], inc=(kc == 7))
                                    dk = ("qkT%d" % blk) if dst is qkT else ("s5u%d" % blk)
                                    if dst is s5u:
                                        S.op("act", lambda e: e.activation(out=s5u[:, dc, :, blk * 16:(blk + 1) * 16], in_=ps[pi][:, 0:NB].rearrange("p (j i) -> p i j", i=SF), func=AF.Identity),
                                             reads=[PK[pi]], writes=[dk])
                                    elif fi % 2 == 0:
                                        S.op("act", lambda e: e.activation(out=dst[:, dc, c0:c0 + NB], in_=ps[pi][:, 0:NB], func=AF.Identity, scale=scl),
                                             reads=[PK[pi]], writes=[dk])
                                    else:
                                        S.op("dve", lambda e: e.tensor_scalar_mul(out=dst[:, dc, c0:c0 + NB], in0=ps[pi][:, 0:NB], scalar1=scl),
                                             reads=[PK[pi]], writes=[dk])
                                for cc in range(2):
                                    ch = blk * 2 + cc
                                    pa, pb = 2 + cc, 4 + cc
                                    for kc in range(8):
                                        S.op("pe", lambda e: e.matmul(ps[pa][:, 0:512], lhsT=h[:, kc, cc * 128:(cc + 1) * 128], rhs=wA[:, kc, 256:768],
                                                                      start=(kc == 0), stop=(kc == 7)), reads=["wA", hk], writes=[PK[pa]], inc=(kc == 7))
                                    for kc in range(8):
                                        S.op("pe", lambda e: e.matmul(ps[pb][:, 0:272], lhsT=h[:, kc, cc * 128:(cc + 1) * 128], rhs=wA[:, kc, 768:1040],
                                                                      start=(kc == 0), stop=(kc == 7)), reads=["wA", hk], writes=[PK[pb]], inc=(kc == 7))
                                    ck = "tm%d" % ch
                                    S.op("act", lambda e: e.activation(out=ktok[:, ch, :], in_=ps[pa][:, 0:256], func=AF.Identity, scale=0.125),
                                         reads=[PK[pa]], writes=[ck])
                                    S.op("dve", lambda e: e.tensor_copy(out=vaug[:, ch, :, 0:64], in_=ps[pa][:, 256:512].rearrange("p (a b) -> p a b", b=64)),
                                         reads=[PK[pa], "vaug"], writes=[ck])
                                    S.op("act", lambda e: e.activation(out=sigo[:, ch, :], in_=ps[pb][:, 0:256], func=AF.Sigmoid),
                                         reads=[PK[pb]], writes=[ck])
                                    S.op("dve", lambda e: e.tensor_tensor(out=gates[:, ch, :], in0=ps[pb][:, 256:272], in1=gbias, op=ALU.add),
                                         reads=[PK[pb], "bcr"], writes=[ck])
                            if debug and b == 0 and l == 0:
                                S.dma("pool", dbg_qk, qkT[:], reads=["qkT%d" % i for i in range(NBLK)])
                                S.dma("sp", dbg_xT0, xT[:], reads=["xT%d" % i for i in range(NBLK)])
                            S.barrier()

                        with ExitStack() as P2:
                            gt = tile(P2, "gt", [128, NCH, 4, 8])
                            Hs = tile(P2, "Hs", [128, NCH, 256])
                            Cst = tile(P2, "Cst", [128, 4, 65])
                            Cbf = tile(P2, "Cbf", [128, 4, 65], BF16)
                            PT = [tile(P2, "PT%d" % i, [128, 128], BF16) for i in range(2)]
                            Kpp = [tile(P2, "Kpp%d" % i, [128, 64], BF16) for i in range(2)]
                            sm = [tile(P2, "sm%d" % i, [128, 4]) for i in range(2)]
                            gtmp = [tile(P2, "gtmp%d" % i, [128, 40]) for i in range(2)]
                            S.op("dve", lambda e: e.memset(Cst[:], 0.0), writes=["Cst"])
                            S.op("dve", lambda e: e.memset(Cbf[:], 0.0), writes=["Cbf"])
                            for ch in range(NCH):
                                g = gates[:, ch, :].rearrange("p (a b) -> p a b", b=4)
                                tm = gtmp[ch % 2]
                                tk = "gtmp%d" % (ch % 2)
                                sp_ = tm[:, 0:8]
                                S.op("act", lambda e: e.activation(out=sp_.rearrange("p (a b) -> p a b", b=4), in_=g[:, 1::2, :], func=AF.Exp, scale=-1.0),
                                     reads=["tm%d" % ch], writes=[tk])
                                S.op("act", lambda e: e.activation(out=sp_, in_=sp_, func=AF.Ln, bias=1.0), reads=[tk], writes=[tk])
                                S.op("pe", lambda e: e.matmul(ps[6][:, 0:4], lhsT=tri_le, rhs=sp_[:, 0:4], start=True, stop=True), reads=[tk, "consts"], writes=[PK[6]], inc=False)
                                S.op("pe", lambda e: e.matmul(ps[6][:, 4:8], lhsT=tri_ge, rhs=sp_[:, 4:8], start=True, stop=True), reads=[tk, "consts"], writes=[PK[6]], inc=False)
                                S.op("pe", lambda e: e.matmul(ps[6][:, 8:16], lhsT=ones_f[:], rhs=sp_, start=True, stop=True), reads=[tk, "ones_f"], writes=[PK[6]], inc=True)
                                gk = "gt%d" % ch
                                S.op("act", lambda e: e.activation(out=gt[:, ch, 0, :], in_=ps[6][:, 0:8], func=AF.Exp, scale=-1.0), reads=[PK[6]], writes=[gk])
                                S.op("dve", lambda e: e.tensor_tensor(out=tm[:, 8:16].rearrange("p (a b) -> p a b", b=4),
                                                                      in0=ps[6][:, 0:8].rearrange("p (a b) -> p a b", b=4), in1=g[:, 0::2, :], op=ALU.add),
                                     reads=[PK[6], "tm%d" % ch], writes=[tk])
                                S.op("act", lambda e: e.activation(out=gt[:, ch, 1, :], in_=tm[:, 8:16], func=AF.Exp), reads=[tk], writes=[gk])
                                S.op("dve", lambda e: e.tensor_tensor(out=tm[:, 16:24], in0=tm[:, 8:16], in1=ps[6][:, 8:16], op=ALU.subtract), reads=[PK[6], tk], writes=[tk])
                                S.op("act", lambda e: e.activation(out=gt[:, ch, 2, :], in_=tm[:, 16:24], func=AF.Exp), reads=[tk], writes=[gk])
                                S.op("act", lambda e: e.activation(out=gt[:, ch, 3, :], in_=ps[6][:, 8:16], func=AF.Exp, scale=-1.0), reads=[PK[6]], writes=[gk])
                            order = [list(range(NCH)), [1, 0] + list(range(NCH - 1, 1, -1))]
                            written = set()
                            PTs = [tile(P2, "PTs%d" % i, [128, 128], BF16) for i in range(2)]
                            maskb = tile(P2, "maskb", [128, 2, 128], BF16)
                            S.op("dve", lambda e: e.tensor_copy(out=maskb[:], in_=consts[:, 1:3, :]), reads=["consts"], writes=["maskb"])
                            flat = []
                            for step in range(NCH):
                                for d in range(2):
                                    for hh in range(4):
                                        it = len(flat)
                                        ch = order[d][step]
                                        flat.append(dict(it=it, step=step, d=d, hh=hh, ch=ch, hd=d * 4 + hh, qc=hh // 2, po=(hh % 2) * 64, ci=d * 2 + hh // 2,
                                                         cs=slice(ch * 128, (ch + 1) * 128), qk="qkT%d" % (ch // 2), ck="tm%d" % ch, gk="gt%d" % ch,
                                                         stk="C%d_%d" % (d * 2 + hh // 2, hh % 2), mask=(tri_le if d == 0 else tri_ge)))

                            def stage1(q):
                                it, ch, hh, hd, po, qc, cs = q["it"], q["ch"], q["hh"], q["hd"], q["po"], q["qc"], q["cs"]
                                pS = it % 2
                                ptile, kp = PT[it % 2], Kpp[it % 2]
                                ptk, kpk = "PT%d" % (it % 2), "Kpp%d" % (it % 2)
                                S.op("pe", lambda e: e.matmul(ps[pS][:, 0:128], lhsT=qkT[po:po + 64, 2 + qc, cs], rhs=qkT[po:po + 64, qc, cs], start=True, stop=True),
                                     reads=[q["qk"]], writes=[PK[pS]])
                                pts = PTs[it % 2]
                                ptsk = "PTs%d" % (it % 2)
                                S.op("act", lambda e: e.activation(out=pts[:], in_=ps[pS][:, 0:128], func=AF.Identity, scale=gt[:, ch, 1, hd:hd + 1]),
                                     reads=[PK[pS], q["gk"]], writes=[ptsk])
                                S.op("pool", lambda e: e.tensor_tensor(out=ptile[:], in0=pts[:], in1=(maskb[:, 0, :] if q["d"] == 0 else maskb[:, 1, :]), op=ALU.mult),
                                     reads=[ptsk, "maskb"], writes=[ptk])
                                if q["step"] < NCH - 1:
                                    S.op("act", lambda e: e.activation(out=kp[:], in_=ktok[:, ch, hh * 64:(hh + 1) * 64], func=AF.Identity, scale=gt[:, ch, 2, hd:hd + 1]),
                                         reads=[q["ck"], q["gk"]], writes=[kpk])

                            def stage2(q):
                                it, ch, hh, hd, po, qc, cs, ci = q["it"], q["ch"], q["hh"], q["hd"], q["po"], q["qc"], q["cs"], q["ci"]
                                pA, pC = 2 + it % 2, 4 + it % 2
                                ptile, kp, smt = PT[it % 2], Kpp[it % 2], sm[it % 2]
                                ptk, kpk, smk = "PT%d" % (it % 2), "Kpp%d" % (it % 2), "sm%d" % (it % 2)
                                qk, ck, gk, stk = q["qk"], q["ck"], q["gk"], q["stk"]
                                S.op("pe", lambda e: e.matmul(ps[pA][:, 0:65], lhsT=ptile[:], rhs=vaug[:, ch, hh, :], start=True, stop=False),
                                     reads=[ptk, ck], writes=[PK[pA]], inc=False)
                                S.op("pe", lambda e: e.matmul(ps[pA][:, 0:65], lhsT=qkT[po:po + 64, qc, cs], rhs=Cbf[po:po + 64, ci, :], start=False, stop=True),
                                     reads=[qk, stk + "b", "Cbf"], writes=[PK[pA]])
                                if q["step"] < NCH - 1:
                                    S.op("pe", lambda e: e.matmul(ps[pC][po:po + 64, 0:65], lhsT=kp[:], rhs=vaug[:, ch, hh, :], start=True, stop=True, tile_position=(0, po)),
                                         reads=[kpk, ck], writes=[PK[pC]])
                                S.op("act", lambda e: e.activation(out=smt[:, 0:1], in_=ps[pA][:, 64:65], func=AF.Abs, scale=gt[:, ch, 0, hd:hd + 1]),
                                     reads=[PK[pA], gk], writes=[smk])
                                S.op("dve", lambda e: e.tensor_scalar_max(out=smt[:, 0:1], in0=smt[:, 0:1], scalar1=1.0), reads=[smk], writes=[smk])
                                S.op("dve", lambda e: e.reciprocal(out=smt[:, 2:3], in_=smt[:, 0:1]), reads=[smk], writes=[smk])
                                S.op("dve", lambda e: e.tensor_tensor(out=smt[:, 1:2], in0=gt[:, ch, 0, hd:hd + 1], in1=smt[:, 2:3], op=ALU.mult),
                                     reads=[smk, gk], writes=[smk])
                                hk_ = "Hs%d_%d" % (ch, hh)
                                if (ch, hh) not in written:
                                    written.add((ch, hh))
                                    S.op("dve", lambda e: e.tensor_scalar_mul(out=Hs[:, ch, hh * 64:(hh + 1) * 64], in0=ps[pA][:, 0:64], scalar1=smt[:, 1:2]),
                                         reads=[PK[pA], smk], writes=[hk_])
                                else:
                                    S.op("dve", lambda e: e.scalar_tensor_tensor(out=Hs[:, ch, hh * 64:(hh + 1) * 64], in0=ps[pA][:, 0:64], scalar=smt[:, 1:2],
                                                                                 in1=Hs[:, ch, hh * 64:(hh + 1) * 64], op0=ALU.mult, op1=ALU.add),
                                         reads=[PK[pA], smk, hk_], writes=[hk_])
                                if q["step"] < NCH - 1:
                                    S.op("dve", lambda e: e.scalar_tensor_tensor(out=Cst[po:po + 64, ci, :], in0=Cst[po:po + 64, ci, :], scalar=gt[po:po + 64, ch, 3, hd:hd + 1],
                                                                                 in1=ps[pC][po:po + 64, 0:65], op0=ALU.mult, op1=ALU.add),
                                         reads=[PK[pC], gk, stk, "Cst"], writes=[stk])
                                    S.op("pool", lambda e: e.tensor_copy(out=Cbf[po:po + 64, ci, :], in_=Cst[po:po + 64, ci, :]),
                                         reads=[stk, "Cbf"], writes=[stk + "b"])

                            for i in range(len(flat) + 1):
                                if i < len(flat):
                                    stage1(flat[i])
                                if i >= 1:
                                    stage2(flat[i - 1])
                            with ExitStack() as P2r:
                                st6 = [tile(P2r, "st6_%d" % i, [128, 4, 6]) for i in range(2)]
                                mv = [tile(P2r, "mv%d" % i, [128, 4, 2]) for i in range(2)]
                                rs = [tile(P2r, "rs%d" % i, [128, 4]) for i in range(2)]
                                ya = [tile(P2r, "ya%d" % i, [128, 256]) for i in range(2)]
                                for ch in range(NCH):
                                    i2 = ch % 2
                                    hkeys = ["Hs%d_%d" % (ch, hh) for hh in range(4)]
                                    for hh in range(4):
                                        S.op("dve", lambda e: e.bn_stats(out=st6[i2][:, hh, :], in_=Hs[:, ch, hh * 64:(hh + 1) * 64]), reads=hkeys, writes=["st6_%d" % i2])
                                        S.op("dve", lambda e: e.bn_aggr(out=mv[i2][:, hh, :], in_=st6[i2][:, hh, :]), reads=["st6_%d" % i2], writes=["mv%d" % i2])
                                    S.op("act", lambda e: e.activation(out=rs[i2][:], in_=mv[i2][:, :, 1], func=AF.Sqrt, bias=epsc[:, 0:1]),
                                         reads=["mv%d" % i2, "epsc"], writes=["rs%d" % i2])
                                    S.op("dve", lambda e: e.reciprocal(out=rs[i2][:], in_=rs[i2][:]), reads=["rs%d" % i2], writes=["rs%d" % i2])
                                    for hh in range(4):
                                        S.op("dve", lambda e: e.tensor_scalar(out=ya[i2][:, hh * 64:(hh + 1) * 64], in0=Hs[:, ch, hh * 64:(hh + 1) * 64],
                                                                              scalar1=mv[i2][:, hh, 0:1], scalar2=rs[i2][:, hh:hh + 1], op0=ALU.subtract, op1=ALU.mult),
                                             reads=hkeys + ["mv%d" % i2, "rs%d" % i2], writes=["ya%d" % i2])
                                    S.op("dve", lambda e: e.tensor_tensor(out=ya[i2][:], in0=ya[i2][:], in1=normw, op=ALU.mult), reads=["ya%d" % i2, "bcr"], writes=["ya%d" % i2])
                                    S.op("dve", lambda e: e.tensor_tensor(out=ya[i2][:], in0=ya[i2][:], in1=sigo[:, ch, :], op=ALU.mult), reads=["ya%d" % i2, "tm%d" % ch], writes=["ya%d" % i2])
                                    pi = 6 + i2
                                    for j in range(2):
                                        S.op("pe", lambda e: e.transpose(out=ps[pi][:, j * 128:(j + 1) * 128], in_=ya[i2][:, j * 128:(j + 1) * 128], identity=ident),
                                             reads=["ya%d" % i2, "consts"], writes=[PK[pi]], inc=(j == 1))
                                    S.op("act", lambda e: e.activation(out=yad[:, 0:2, ch * 128:(ch + 1) * 128], in_=ps[pi][:, 0:256].rearrange("p (a b) -> p a b", b=128), func=AF.Identity),
                                         reads=[PK[pi]], writes=["yad_a%d" % ch])
                                S.barrier()
                        PM.close()

                        with ExitStack() as P3:
                            s5C = tile(P3, "s5C", [128, 2, 256])
                            s5Cb = tile(P3, "s5Cb", [128, 2, 256], BF16)
                            pw = tile(P3, "pw", [128, 2, 17, 16])
                            pw2 = tile(P3, "pw2", [128, 2, 9, 16])
                            npw2 = tile(P3, "npw2", [128, 9, 16])
                            gluw = tile(P3, "gluw", [128, 2, 256], BF16)
                            WEb = tile(P3, "WEb", [128, 16, 2, 512], BF16)
                            Gb = tile(P3, "Gb", [128, 2, 2, 16, 128], BF16)
                            G0 = tile(P3, "G0", [128, 2, 128], BF16)
                            S.dma("sp", s5C[:].rearrange("p a (b c) -> p a b c", c=32), s5C_d[l], writes=["s5C"])
                            S.dma("pool", gluw[:], glu_d[l], writes=["gluw"])
                            S.op("dve", lambda e: e.tensor_copy(out=s5Cb[:, 0, :], in_=s5C[:, 0, :]), reads=["s5C"], writes=["s5Cb"])
                            S.op("dve", lambda e: e.tensor_scalar_mul(out=s5Cb[:, 1, :], in0=s5C[:, 1, :], scalar1=-1.0), reads=["s5C", "s5Cb"], writes=["s5Cb"])
                            if b == 0:
                                PTA = ExitStack()
                                s5A = tile(PTA, "s5A", [128, 3, 512])
                                s5B = tile(PTA, "s5B", [128, 2, 256])
                                s5Bn = tile(PTA, "s5Bn", [128, 2, 256])
                                s5p = tile(PTA, "s5p", [128, 3, 16])
                                tp = [tile(PTA, "tp%d" % i, [128, 16]) for i in range(10)]
                                sT = tile(PTA, "sT", [128, 2, 16, 16])
                                Wnb = [tile(PTA, "Wnb%d" % i, [128, 2, 512], BF16) for i in range(2)]
                                tA = [tile(PTA, "tA%d" % i, [128, 512]) for i in range(8)]
                                S.dma("sp", s5A[:].rearrange("p a (b c) -> p a b c", c=128), s5A_d[l].rearrange("p a d c n -> p a (d c) n"), writes=["s5A"])
                                S.dma("sp", s5B[:].rearrange("p a (b c) -> p a b c", c=128), s5B_d[l], writes=["s5B"])
                                S.dma("sp", s5Bn[:].rearrange("p a (b c) -> p a b c", c=32), s5Bn_d[l], writes=["s5Bn"])
                                S.dma("sp", s5p[:], s5pp_d[l], writes=["s5p"])

                                def tt(out, in0, in1, op, reads, writes, en="dve"):
                                    S.op(en, lambda e: e.tensor_tensor(out=out, in0=in0, in1=in1, op=op), reads=reads, writes=writes)

                                def cplx_abar(are, aim, ldt, t, width, key, tkeys):
                                    w = width
                                    dt_, lr, li, mag, ph, kk, ar, ai = [x[:, 0:w] for x in t[0:8]]
                                    S.op("act", lambda e: e.activation(out=dt_, in_=ldt, func=AF.Exp), reads=[key], writes=[tkeys[0]])
                                    tt(lr, are, dt_, ALU.mult, [key, tkeys[0]], [tkeys[1]])
                                    tt(li, aim, dt_, ALU.mult, [key, tkeys[0]], [tkeys[2]])
                                    S.op("act", lambda e: e.activation(out=mag, in_=lr, func=AF.Exp), reads=[tkeys[1]], writes=[tkeys[3]])
                                    for (shift, dst, dk) in ((0.0, ai, tkeys[7]), (PI / 2, ar, tkeys[6])):
                                        S.op("dve", lambda e: e.tensor_scalar_add(out=ph, in0=li, scalar1=shift), reads=[tkeys[2]], writes=[tkeys[4]])
                                        S.op("dve", lambda e: e.tensor_copy(out=dst, in_=ph), reads=[tkeys[4]], writes=[dk])
                                        for m in range(6):
                                            thr = (2 * m + 1) * PI
                                            S.op("dve", lambda e: e.tensor_scalar(out=kk, in0=ph, scalar1=thr, scalar2=-2 * PI, op0=ALU.is_gt, op1=ALU.mult),
                                                 reads=[tkeys[4]], writes=[tkeys[5]])
                                            tt(dst, dst, kk, ALU.add, [tkeys[5], dk], [dk])
                                        S.op("act", lambda e: e.activation(out=dst, in_=dst, func=AF.Sin), reads=[dk], writes=[dk])
                                        tt(dst, dst, mag, ALU.mult, [dk, tkeys[3]], [dk])
                                    return ar, ai

                                def kappa(kr, ki, den, arm1, ar, ai, lr, li, t0, keys):
                                    K = keys
                                    tt(den, lr, lr, ALU.mult, [K["lam"]], [K["den"]])
                                    tt(t0, li, li, ALU.mult, [K["lam"]], [K["t0"]])
                                    tt(den, den, t0, ALU.add, [K["den"], K["t0"]], [K["den"]])
                                    S.op("dve", lambda e: e.reciprocal(out=den, in_=den), reads=[K["den"]], writes=[K["den"]])
                                    S.op("dve", lambda e: e.tensor_scalar_add(out=arm1, in0=ar, scalar1=-1.0), reads=[K["ar"]], writes=[K["arm1"]])
                                    tt(kr, arm1, lr, ALU.mult, [K["arm1"], K["lam"]], [K["kr"]])
                                    tt(t0, ai, li, ALU.mult, [K["ai"], K["lam"]], [K["t0"]])
                                    tt(kr, kr, t0, ALU.add, [K["kr"], K["t0"]], [K["kr"]])
                                    tt(kr, kr, den, ALU.mult, [K["kr"], K["den"]], [K["kr"]])
                                    tt(ki, ai, lr, ALU.mult, [K["ai"], K["lam"]], [K["ki"]])
                                    tt(t0, arm1, li, ALU.mult, [K["arm1"], K["lam"]], [K["t0"]])
                                    tt(ki, ki, t0, ALU.subtract, [K["ki"], K["t0"]], [K["ki"]])
                                    tt(ki, ki, den, ALU.mult, [K["ki"], K["den"]], [K["ki"]])

                                tAk = ["tA%d" % i for i in range(8)]
                                tpk = ["tp%d" % i for i in range(10)]
                                ar, ai = cplx_abar(s5A[:, 0, :], s5A[:, 1, :], s5A[:, 2, :], tA, 512, "s5A", tAk)
                                den, kr, ki, t0, t1, arm1 = tA[0][:, :], tA[1][:, :], tA[2][:, :], tA[3][:, :], tA[4][:, :], tA[5][:, :]
                                kappa(kr, ki, den, arm1, ar, ai, s5A[:, 0, :], s5A[:, 1, :], t0,
                                      dict(lam="s5A", den=tAk[0], kr=tAk[1], ki=tAk[2], t0=tAk[3], arm1=tAk[5], ar=tAk[6], ai=tAk[7]))
                                v3 = lambda x: x.rearrange("p (a b) -> p a b", b=256)
                                Bre = s5B[:, 0, :].unsqueeze(1).to_broadcast([128, 2, 256])
                                Bim = s5B[:, 1, :].unsqueeze(1).to_broadcast([128, 2, 256])
                                Wr, Wi, Wr2, Wi2 = tA[5][:, :], tA[0][:, :], tA[1][:, :], tA[2][:, :]
                                Wrk, Wik, Wr2k, Wi2k = tAk[5], tAk[0], tAk[1], tAk[2]
                                tt(v3(t0), v3(kr), Bre, ALU.mult, [tAk[1], "s5B"], [tAk[3]])
                                tt(v3(t1), v3(ki), Bim, ALU.mult, [tAk[2], "s5B"], [tAk[4]])
                                tt(Wr, t0, t1, ALU.subtract, [tAk[3], tAk[4], tAk[5]], [Wrk])
                                tt(v3(t0), v3(kr), Bim, ALU.mult, [tAk[1], "s5B"], [tAk[3]])
                                tt(v3(t1), v3(ki), Bre, ALU.mult, [tAk[2], "s5B"], [tAk[4]])
                                tt(Wi, t0, t1, ALU.add, [tAk[3], tAk[4], tAk[0]], [Wik])
                                for e_ in range(16):
                                    S.op("act", lambda e: e.activation(out=WEb[:, e_, 0, :], in_=Wr, func=AF.Identity), reads=[Wrk], writes=["WEb"])
                                    S.op("act", lambda e: e.activation(out=WEb[:, e_, 1, :], in_=Wi, func=AF.Identity), reads=[Wik], writes=["WEb"])
                                    if e_ == 15:
                                        break
                                    tt(t0, Wr, ar, ALU.mult, [Wrk, tAk[6]], [tAk[3]])
                                    tt(t1, Wi, ai, ALU.mult, [Wik, tAk[7]], [tAk[4]])
                                    tt(Wr2, t0, t1, ALU.subtract, [tAk[3], tAk[4], Wr2k], [Wr2k])
                                    tt(t0, Wr, ai, ALU.mult, [Wrk, tAk[7]], [tAk[3]])
                                    tt(t1, Wi, ar, ALU.mult, [Wik, tAk[6]], [tAk[4]])
                                    tt(Wi2, t0, t1, ALU.add, [tAk[3], tAk[4], Wi2k], [Wi2k])
                                    Wr, Wi, Wr2, Wi2 = Wr2, Wi2, Wr, Wi
                                    Wrk, Wik, Wr2k, Wi2k = Wr2k, Wi2k, Wrk, Wik
                                par, pai = cplx_abar(s5p[:, 0, :], s5p[:, 1, :], s5p[:, 2, :], tp, 16, "s5p", tpk)
                                S.op("dve", lambda e: e.memset(pw[:, 0, 0, :], 1.0), writes=["pw"])
                                S.op("dve", lambda e: e.memset(pw[:, 1, 0, :], 0.0), reads=["pw"], writes=["pw"])
                                S.op("dve", lambda e: e.tensor_copy(out=pw[:, 0, 1, :], in_=par), reads=[tpk[6], "pw"], writes=["pw"])
                                S.op("dve", lambda e: e.tensor_copy(out=pw[:, 1, 1, :], in_=pai), reads=[tpk[7], "pw"], writes=["pw"])
                                kappa(tp[1][:, :], tp[2][:, :], tp[0][:, :], tp[5][:, :], par, pai, s5p[:, 0, :], s5p[:, 1, :], tp[3][:, :],
                                      dict(lam="s5p", den=tpk[0], kr=tpk[1], ki=tpk[2], t0=tpk[3], arm1=tpk[5], ar=tpk[6], ai=tpk[7]))
                                S.op("dve", lambda e: e.tensor_copy(out=sT[:, 0, 0, :], in_=tp[1][:, :]), reads=[tpk[1]], writes=["sT"])
                                S.op("dve", lambda e: e.tensor_copy(out=sT[:, 1, 0, :], in_=tp[2][:, :]), reads=[tpk[2], "sT"], writes=["sT"])

                                def cmul(dr, di, xr, xi, yr, yi, rk, wk):
                                    u0, u1, u2, u3 = tp[3][:, :], tp[4][:, :], tp[8][:, :], tp[9][:, :]
                                    tt(u0, xr, yr, ALU.mult, rk, [tpk[3]])
                                    tt(u1, xi, yi, ALU.mult, rk, [tpk[4]])
                                    tt(u2, xr, yi, ALU.mult, rk, [tpk[8]])
                                    tt(u3, xi, yr, ALU.mult, rk, [tpk[9]])
                                    tt(dr, u0, u1, ALU.subtract, [tpk[3], tpk[4]] + wk, wk)
                                    tt(di, u2, u3, ALU.add, [tpk[8], tpk[9]] + wk, wk)

                                for k in range(2, 17):
                                    cmul(pw[:, 0, k, :], pw[:, 1, k, :], pw[:, 0, k - 1, :], pw[:, 1, k - 1, :], pw[:, 0, 1, :], pw[:, 1, 1, :], ["pw"], ["pw"])
                                S.op("dve", lambda e: e.tensor_copy(out=pw2[:, :, 0, :], in_=pw[:, :, 16, :]), reads=["pw"], writes=["pw2"])
                                for m in range(1, 9):
                                    cmul(pw2[:, 0, m, :], pw2[:, 1, m, :], pw2[:, 0, m - 1, :], pw2[:, 1, m - 1, :], pw2[:, 0, m - 1, :], pw2[:, 1, m - 1, :], ["pw2"], ["pw2"])
                                S.op("dve", lambda e: e.tensor_scalar_mul(out=npw2[:], in0=pw2[:, 1, :, :], scalar1=-1.0), reads=["pw2"], writes=["npw2"])
                                for tau in range(1, 16):
                                    cmul(sT[:, 0, tau, :], sT[:, 1, tau, :], pw[:, 0, tau, :], pw[:, 1, tau, :], sT[:, 0, 0, :], sT[:, 1, 0, :], ["pw", "sT"], ["sT"])
                                S.op("dve", lambda e: e.memset(Gb[:], 0.0), writes=["Gb"])
                                v4 = lambda x: x.rearrange("p (a b c) -> p a b c", a=2, b=8)
                                Bnr = s5Bn[:, 0, :].rearrange("p (b c) -> p b c", c=32).unsqueeze(1).to_broadcast([128, 2, 8, 32])
                                Bni = s5Bn[:, 1, :].rearrange("p (b c) -> p b c", c=32).unsqueeze(1).to_broadcast([128, 2, 8, 32])
                                for tau in range(16):
                                    wn = Wnb[tau % 2]
                                    wnk = "Wnb%d" % (tau % 2)
                                    sr = sT[:, 0, tau, :].rearrange("p (a b) -> p a b", b=8).unsqueeze(3).to_broadcast([128, 2, 8, 32])
                                    si = sT[:, 1, tau, :].rearrange("p (a b) -> p a b", b=8).unsqueeze(3).to_broadcast([128, 2, 8, 32])
                                    tt(v4(t0), sr, Bnr, ALU.mult, ["sT", "s5Bn"], [tAk[3]])
                                    tt(v4(t1), si, Bni, ALU.mult, ["sT", "s5Bn"], [tAk[4]])
                                    tt(wn[:, 0, :], t0, t1, ALU.subtract, [tAk[3], tAk[4]], [wnk])
                                    tt(v4(t0), sr, Bni, ALU.mult, ["sT", "s5Bn"], [tAk[3]])
                                    tt(v4(t1), si, Bnr, ALU.mult, ["sT", "s5Bn"], [tAk[4]])
                                    tt(wn[:, 1, :], t0, t1, ALU.add, [tAk[3], tAk[4], wnk], [wnk])
                                    pi = tau % 2
                                    for d in range(2):
                                        for pair in range(8):
                                            chn, ppi = pair // 4, pair % 4
                                            o = ps[pi][ppi * 32:(ppi + 1) * 32, (chn * 2 + d) * 32:(chn * 2 + d + 1) * 32]
                                            for c in range(2):
                                                S.op("pe", lambda e: e.matmul(o, lhsT=wn[:, c, (d * 8 + pair) * 32:(d * 8 + pair + 1) * 32], rhs=s5Cb[:, c, pair * 32:(pair + 1) * 32],
                                                                              start=(c == 0), stop=(c == 1), tile_position=(0, ppi * 32)),
                                                     reads=[wnk, "s5Cb"], writes=[PK[pi]], inc=(c == 1 and d == 1 and pair == 7))
                                    for pb in range(4):
                                        S.op("act", lambda e: e.activation(out=Gb[pb * 32:(pb + 1) * 32, :, :, tau, pb * 32:(pb + 1) * 32],
                                                                           in_=ps[pi][pb * 32:(pb + 1) * 32, 0:128].rearrange("p (a b c) -> p a b c", a=2, b=2), func=AF.Identity),
                                             reads=[PK[pi]], writes=["Gb"])
                                tt(G0[:], Gb[:, :, 0, 0, :], Gb[:, :, 1, 0, :], ALU.add, ["Gb"], ["G0"])
                                S.barrier()
                                PTA.close()
                                fl = lambda x: x
                                S.dma("sp", cWE[l], WEb[:].rearrange("p a b c -> p (a b c)"), reads=["WEb"])
                                S.dma("sp", cG[l], Gb[:].rearrange("p a b c d -> p (a b c d)"), reads=["Gb"])
                                S.dma("sp", cG0[l], G0[:].rearrange("p a b -> p (a b)"), reads=["G0"])
                                S.dma("sp", cPW[l], pw[:].rearrange("p a b c -> p (a b c)"), reads=["pw"])
                                S.dma("sp", cPW2[l], pw2[:].rearrange("p a b c -> p (a b c)"), reads=["pw2"])
                                S.dma("sp", cNPW2[l], npw2[:].rearrange("p a b -> p (a b)"), reads=["npw2"])
                            else:
                                S.dma("sp", WEb[:].rearrange("p a b c -> p (a b c)"), cWE[l], writes=["WEb"])
                                S.dma("sp", Gb[:].rearrange("p a b c d -> p (a b c d)"), cG[l], writes=["Gb"])
                                S.dma("sp", G0[:].rearrange("p a b -> p (a b)"), cG0[l], writes=["G0"])
                                S.dma("sp", pw[:].rearrange("p a b c -> p (a b c)"), cPW[l], writes=["pw"])
                                S.dma("sp", pw2[:].rearrange("p a b c -> p (a b c)"), cPW2[l], writes=["pw2"])
                                S.dma("sp", npw2[:].rearrange("p a b -> p (a b)"), cNPW2[l], writes=["npw2"])
                                S.barrier()
                            EA = [[[tile(P3, "EA%d%d%d" % (par, d, c), [128, SJ + 1]) for c in range(2)] for d in range(2)] for par in range(2)]
                            EB = [[[tile(P3, "EB%d%d%d" % (par, d, c), [128, SJ + 1]) for c in range(2)] for d in range(2)] for par in range(2)]
                            CWt = [tile(P3, "CW%d" % i, [128, 17, 2, 2, 32], BF16) for i in range(2)]
                            cu = [tile(P3, "cu%d" % i, [128, 16, 32]) for i in range(2)]
                            ypre = tile(P3, "ypre", [128, 2, T], BF16)
                            en = "dve"

                            def stt(out, in0, scal, in1, reads, writes):
                                S.op(en, lambda e: e.scalar_tensor_tensor(out=out, in0=in0, scalar=scal, in1=in1, op0=ALU.mult, op1=ALU.add), reads=reads, writes=writes)

                            uk = ["s5u%d" % i for i in range(NBLK)]
                            W = SJ + 1

                            def s5_head(pair):
                                chn, ppi, par = pair // 4, pair % 4, pair % 2
                                rows = slice(ppi * 32, ppi * 32 + 32)
                                cw = CWt[par]
                                cwk = "CW%d" % par
                                DD = []
                                for d in range(2):
                                    DD.append(dict(d=d, sc=d * 8 + pair, ea=EA[par][d], eb=EB[par][d],
                                                   eak=["EA%d%d0" % (par, d), "EA%d%d1" % (par, d)], ebk=["EB%d%d0" % (par, d), "EB%d%d1" % (par, d)],
                                                   lo=(1 if d == 0 else 0), zc=(0 if d == 0 else SJ)))
                                pend = []
                                Cr = s5C[:, 0, pair * 32:(pair + 1) * 32].unsqueeze(1).to_broadcast([128, 16, 32])
                                Ci = s5C[:, 1, pair * 32:(pair + 1) * 32].unsqueeze(1).to_broadcast([128, 16, 32])
                                for d in range(2):
                                    sc = d * 8 + pair
                                    pr = pw[:, 0, 1:17, sc:sc + 1].to_broadcast([128, 16, 32])
                                    pim = pw[:, 1, 1:17, sc:sc + 1].to_broadcast([128, 16, 32])
                                    pend.append(lambda pr=pr: tt(cu[0][:], Cr, pr, ALU.mult, ["s5C", "pw"], ["cu0"]))
                                    pend.append(lambda pim=pim: tt(cu[1][:], Ci, pim, ALU.mult, ["s5C", "pw"], ["cu1"]))
                                    pend.append(lambda d=d: tt(cw[:, 1:17, d, 0, :], cu[0][:], cu[1][:], ALU.subtract, ["cu0", "cu1"], [cwk]))
                                    pend.append(lambda pim=pim: tt(cu[0][:], Cr, pim, ALU.mult, ["s5C", "pw"], ["cu0"]))
                                    pend.append(lambda pr=pr: tt(cu[1][:], Ci, pr, ALU.mult, ["s5C", "pw"], ["cu1"]))
                                    pend.append(lambda d=d: S.op("dve", lambda e: e.scalar_tensor_tensor(out=cw[:, 1:17, d, 1, :], in0=cu[0][:], scalar=-1.0, in1=cu[1][:], op0=ALU.mult, op1=ALU.subtract),
                                                                 reads=["cu0", "cu1", cwk], writes=[cwk]))
                                for q in DD:
                                    d = q["d"]
                                    for c in range(2):
                                        pi = d * 2 + c
                                        for i in range(SF):
                                            lag = SF - 1 - i if d == 0 else i
                                            S.op("pe", lambda e: e.matmul(ps[pi][:, 0:SJ], lhsT=WEb[rows, lag, c, (d * 2 + chn) * 128:(d * 2 + chn + 1) * 128],
                                                                          rhs=s5u[rows, chn, i, :], start=(i == 0), stop=(i == SF - 1), tile_position=(ppi * 32, 0)),
                                                 reads=["WEb"] + uk, writes=[PK[pi]], inc=(i == SF - 1))
                                        zc = q["zc"]
                                        S.op("act", lambda e: e.activation(out=q["ea"][c][:, zc:zc + 1], in_=zcol[:, 0:1], func=AF.Identity), reads=["zcol"], writes=[q["eak"][c]])
                                        S.op("act", lambda e: e.activation(out=q["eb"][c][:, zc:zc + 1], in_=zcol[:, 0:1], func=AF.Identity), reads=["zcol"], writes=[q["ebk"][c]])
                                        if d == 0:
                                            S.op("act", lambda e: e.activation(out=q["ea"][c][:, 1:SJ + 1], in_=ps[pi][:, 0:SJ], func=AF.Identity), reads=[PK[pi]], writes=[q["eak"][c]])
                                        else:
                                            S.op("act", lambda e: e.activation(out=q["ea"][c][:, 0:128], in_=ps[pi][:, 16:SJ], func=AF.Identity), reads=[PK[pi]], writes=[q["eak"][c]])
                                            S.op("act", lambda e: e.activation(out=q["ea"][c][:, 128:SJ], in_=ps[pi][:, 0:16], func=AF.Identity), reads=[PK[pi]], writes=[q["eak"][c]])
                                    q["cur"], q["nxt"], q["curk"], q["nxtk"] = q["ea"], q["eb"], q["eak"], q["ebk"]
                                for m in range(8):
                                    dd = 1 << m
                                    for phase in range(3):
                                        for q in DD:
                                            d, sc = q["d"], q["sc"]
                                            cur, nxt, curk, nxtk = q["cur"], q["nxt"], q["curk"], q["nxtk"]
                                            p_r, p_i, np_i = pw2[:, 0, m, sc:sc + 1], pw2[:, 1, m, sc:sc + 1], npw2[:, m, sc:sc + 1]
                                            if d == 0:
                                                o_s, i_s, k_s = slice(dd, W), slice(0, W - dd), slice(0, dd)
                                            else:
                                                o_s, i_s, k_s = slice(0, W - dd), slice(dd, W), slice(W - dd, W)
                                            if phase == 0:
                                                pend.append(lambda nxt=nxt, cur=cur, o_s=o_s, i_s=i_s, p_r=p_r, curk=curk, nxtk=nxtk: stt(nxt[0][:, o_s], cur[0][:, i_s], p_r, cur[0][:, o_s], curk + ["pw2"], [nxtk[0]]))
                                                pend.append(lambda nxt=nxt, cur=cur, o_s=o_s, i_s=i_s, p_r=p_r, curk=curk, nxtk=nxtk: stt(nxt[1][:, o_s], cur[1][:, i_s], p_r, cur[1][:, o_s], curk + ["pw2"], [nxtk[1]]))
                                            elif phase == 1:
                                                pend.append(lambda nxt=nxt, cur=cur, o_s=o_s, i_s=i_s, np_i=np_i, curk=curk, nxtk=nxtk: stt(nxt[0][:, o_s], cur[1][:, i_s], np_i, nxt[0][:, o_s], curk + ["npw2", nxtk[0]], [nxtk[0]]))
                                                pend.append(lambda nxt=nxt, cur=cur, o_s=o_s, i_s=i_s, p_i=p_i, curk=curk, nxtk=nxtk: stt(nxt[1][:, o_s], cur[0][:, i_s], p_i, nxt[1][:, o_s], curk + ["pw2", nxtk[1]], [nxtk[1]]))
                                            else:
                                                for c in range(2):
                                                    pend.append(lambda nxt=nxt, cur=cur, k_s=k_s, c=c, curk=curk, nxtk=nxtk: S.op(en, lambda e: e.tensor_copy(out=nxt[c][:, k_s], in_=cur[c][:, k_s]), reads=[curk[c]], writes=[nxtk[c]]))
                                    for q in DD:
                                        q["cur"], q["nxt"], q["curk"], q["nxtk"] = q["nxt"], q["cur"], q["nxtk"], q["curk"]
                                Ef = []
                                for q in DD:
                                    for c in range(2):
                                        pend.append(lambda q=q, c=c, nxt=q["nxt"], cur=q["cur"], curk=q["curk"], nxtk=q["nxtk"]:
                                                    S.op("act", lambda e: e.activation(out=nxt[c][:, :].bitcast(BF16)[:, 0:W], in_=cur[c][:, :], func=AF.Identity),
                                                         reads=[curk[c]], writes=[nxtk[c]]))
                                    Ef.append(([q["nxt"][c][:, :].bitcast(BF16) for c in range(2)], list(q["nxtk"])))
                                return dict(pair=pair, chn=chn, ppi=ppi, rows=rows, cw=cw, cwk=cwk, Ef=Ef), pend

                            def s5_intra(chn, pend):
                                npend = (len(pend) + SF - 1) // SF if pend else 0
                                for i in range(SF):
                                    pi = 4 + i % 4
                                    o = ps[pi][:, 0:SJ]
                                    for i2 in range(SF):
                                        if i2 < i:
                                            lt = Gb[:, chn, 0, i - i2, :]
                                        elif i2 > i:
                                            lt = Gb[:, chn, 1, i2 - i, :]
                                        else:
                                            lt = G0[:, chn, :]
                                        S.op("pe", lambda e: e.matmul(o, lhsT=lt, rhs=s5u[:, chn, i2, :], start=(i2 == 0), stop=(i2 == SF - 1)),
                                             reads=["Gb", "G0"] + uk, writes=[PK[pi]], inc=(i2 == SF - 1))
                                    S.op("dve", lambda e: e.scalar_tensor_tensor(out=ypre[:, chn, i::SF], in0=s5u[:, chn, i, :], scalar=s5d[:, chn:chn + 1], in1=o,
                                                                                 op0=ALU.mult, op1=ALU.add), reads=[PK[pi], "pp"] + uk, writes=["ypI%d" % chn])
                                    for _ in range(npend):
                                        if pend:
                                            pend.pop(0)()

                            def s5_tail(ctx, pend):
                                pair, chn, ppi, rows, cw, cwk, Ef = ctx["pair"], ctx["chn"], ctx["ppi"], ctx["rows"], ctx["cw"], ctx["cwk"], ctx["Ef"]
                                npend = (len(pend) + SF - 1) // SF
                                for i in range(SF):
                                    pi = 4 + i % 4
                                    o = ps[pi][rows, 0:SJ]
                                    (ef, efk), (eb_, ebk_) = Ef
                                    for c in range(2):
                                        S.op("pe", lambda e: e.matmul(o, lhsT=cw[:, i + 1, 0, c, :], rhs=ef[c][:, 0:SJ], start=(c == 0), stop=False, tile_position=(0, ppi * 32)),
                                             reads=[cwk] + efk, writes=[PK[pi]], inc=False)
                                    for c in range(2):
                                        S.op("pe", lambda e: e.matmul(ps[pi][rows, 0:16], lhsT=cw[:, SF - i, 1, c, :], rhs=eb_[c][:, 129:145], start=False, stop=False, tile_position=(0, ppi * 32)),
                                             reads=[cwk] + ebk_, writes=[PK[pi]], inc=False)
                                        S.op("pe", lambda e: e.matmul(ps[pi][rows, 16:SJ], lhsT=cw[:, SF - i, 1, c, :], rhs=eb_[c][:, 1:129], start=False, stop=(c == 1), tile_position=(0, ppi * 32)),
                                             reads=[cwk] + ebk_, writes=[PK[pi]], inc=(c == 1))
                                    S.op("dve", lambda e: e.tensor_tensor(out=ypre[rows, chn, i::SF], in0=ypre[rows, chn, i::SF], in1=o, op=ALU.add),
                                         reads=[PK[pi], "ypI%d" % chn, "ypre%d" % pair], writes=["ypre%d" % pair])
                                    for _ in range(npend):
                                        if pend:
                                            pend.pop(0)()
                                while pend:
                                    pend.pop(0)()

                            zcol = tile(P3, "zcol", [128, 1])
                            S.op("dve", lambda e: e.memset(zcol[:], 0.0), writes=["zcol"])
                            ctx, pend0 = s5_head(0)
                            s5_intra(0, pend0)
                            s5_intra(1, pend0)
                            while pend0:
                                pend0.pop(0)()
                            for pair in range(8):
                                if pair < 7:
                                    nctx, npnd = s5_head(pair + 1)
                                else:
                                    nctx, npnd = None, []
                                s5_tail(ctx, npnd)
                                ctx = nctx
                            yg = [ypre[:, 0, :], ypre[:, 1, :]]
                            ypk = ["ypre%d" % p for p in range(8)] + ["ypI0", "ypI1"]
                            for c in range(2):
                                for s0 in range(0, T, 768):
                                    S.op("act", lambda e: e.activation(out=yg[c][:, s0:s0 + 768], in_=ypre[:, c, s0:s0 + 768], func=AF.Gelu_apprx_tanh), reads=ypk, writes=["yg"] + ypk)
                            sgt = [tile(P3, "sgt%d" % i, [128, 512]) for i in range(2)]
                            it = 0
                            for s0 in list(range(0, 2048, 512)) + [2048]:
                                n = min(512, T - s0)
                                for oc in range(2):
                                    pi = it % 2
                                    sg = sgt[it % 2]
                                    sgk = "sgt%d" % (it % 2)
                                    it += 1
                                    for kc in range(2):
                                        S.op("pe", lambda e: e.matmul(ps[pi][:, 0:n], lhsT=gluw[:, kc, oc * 128:(oc + 1) * 128], rhs=yg[kc][:, s0:s0 + n], start=(kc == 0), stop=(kc == 1)),
                                             reads=["gluw", "yg"], writes=[PK[pi]], inc=(kc == 1))
                                    S.op("act", lambda e: e.activation(out=sg[:, 0:n], in_=ps[pi][:, 0:n], func=AF.Sigmoid, bias=glub[:, oc:oc + 1]), reads=[PK[pi], "pp"], writes=[sgk])
                                    S.op("dve", lambda e: e.tensor_tensor(out=yad[:, 2 + oc, s0:s0 + n], in0=yg[oc][:, s0:s0 + n], in1=sg[:, 0:n], op=ALU.mult), reads=["yg", sgk], writes=["yad_d"])
                            S.barrier()

                    with ExitStack() as P4:
                        wB = tile(P4, "wB", [128, 8, 1024], BF16)
                        wo = tile(P4, "wo", [128, 8, 1024], BF16)
                        swT = tile(P4, "swT", [128, 4, 128], BF16)
                        sbias = tile(P4, "sbias", [128, 2, 128])
                        hT = [tile(P4, "h3T%d" % i, [128, 8, NB], BF16) for i in range(1)]
                        ybc = [tile(P4, "ybc%d" % i, [128, 4, NB], BF16) for i in range(2)]
                        uT = tile(P4, "uT", [128, 2, NB], BF16)
                        gv = tile(P4, "gv", [128, 256])
                        vtok = tile(P4, "vtok", [128, 256], BF16)
                        st6 = tile(P4, "s_st6", [128, 6])
                        mv = tile(P4, "s_mv", [128, 2])
                        rs1 = tile(P4, "s_rs", [128, 1])
                        sgtmp = tile(P4, "sgtmp", [128, NB])
                        sgtmp2 = tile(P4, "sgtmp2", [128, 128])
                        padl = tile(P4, "padl", [128, 2, 4, 94])
                        padc = tile(P4, "padc", [128, 2, 286])
                        cacc = tile(P4, "cacc", [128, 2, NB])
                        sq8 = tile(P4, "sq8", [128, 8, NB])
                        st = {"mean": tile(P4, "mean", [128, NB]), "m2": tile(P4, "m2", [128, NB]), "rstd": tile(P4, "rstd", [128, NB])}
                        tnorm = cacc
                        tmp8 = tile(P4, "tmp8", [128, NB])
                        h2T = tile(P4, "h2T", [128, 8, NB], BF16)
                        actT = tile(P4, "actT", [128, NJ, NB], BF16)
                        sgf = [tile(P4, "sgf%d" % i, [128, NB]) for i in range(2)]
                        w1p = [tile(P4, "w1p%d" % i, [128, 8, 256], BF16) for i in range(3)]
                        w2p = [tile(P4, "w2p%d" % i, [128, NJ, 128], BF16) for i in range(2)]
                        A2 = tile(P4, "A2", [128, 8, 3])
                        B2 = tile(P4, "B2", [128, 8, 3])
                        S.op("dve", lambda e: e.tensor_tensor(out=A2[:], in0=ops2[:, l, :, :], in1=ln1w.unsqueeze(2).to_broadcast([128, 8, 3]), op=ALU.mult), reads=["pp", "ops2"], writes=["A2B2"])
                        S.op("dve", lambda e: e.tensor_tensor(out=B2[:], in0=ops2[:, l, :, :], in1=ln1b.unsqueeze(2).to_broadcast([128, 8, 3]), op=ALU.mult), reads=["pp", "ops2", "A2B2"], writes=["A2B2"])
                        S.op("dve", lambda e: e.tensor_tensor(out=B2[:], in0=B2[:], in1=modT[:, l, 24:32, :], op=ALU.add), reads=["modT", "A2B2"], writes=["A2B2"])
                        S.dma("pool", wB[:], winB_d[l], writes=["wB"])
                        S.dma("pool", wo[:], wout_d[l], writes=["wo"])
                        S.dma("pool", swT[:], sguw_d[l], writes=["swT"])
                        S.dma("sp", sbias[:], sgub_d[l], writes=["sbias"])
                        S.op("dve", lambda e: e.memset(padl[:], 0.0), writes=["padl"])
                        S.op("dve", lambda e: e.memset(padc[:], 0.0), writes=["padc"])
                        w1i = 0
                        w2i = 0
                        blocks = list(range(1, NBLK)) if last else list(range(NBLK))
                        stc = {"mean": tile(P4, "meanc", [128, NB]), "m2": tile(P4, "m2c", [128, NB]), "rstd": tile(P4, "rstdc", [128, NB]),
                               "keys": ("meanc", "m2c", "rstdc")}
                        w1i = [0]
                        w2i = [0]
                        h = hT[0]
                        hk = "h3T0"

                        def make_h(blk):
                            c0 = blk * NB
                            mc = mcol(blk)
                            xk = "xT%d" % blk
                            for fc in range(8):
                                S.op("act", lambda e: e.activation(out=h[:, fc, :], in_=xT[:, fc, c0:c0 + NB], func=AF.Identity,
                                                                   scale=ops1[:, l, fc, mc:mc + 1], bias=modT[:, l, fc, mc:mc + 1]),
                                     reads=[xk, "ops1", "modT"], writes=[hk])

                        def front(blk):
                            c0 = blk * NB
                            mc = mcol(blk)
                            yb = ybc[blk % 2]
                            ybk = "ybc%d" % (blk % 2)
                            xk = "xT%d" % blk

                            def fmproj(wc, pi):
                                for kc in range(8):
                                    S.op("pe", lambda e: e.matmul(ps[pi][:, 0:NB], lhsT=wB[:, kc, wc:wc + 128], rhs=h[:, kc, :], start=(kc == 0), stop=(kc == 7)),
                                         reads=["wB", hk], writes=[PK[pi]], inc=(kc == 7))
                            taps = []
                            for c in range(2):
                                fmproj(512 + c * 128, 4)
                                fmproj(768 + c * 128, 5)
                                S.op("act", lambda e: e.activation(out=sgtmp[:], in_=ps[5][:, 0:NB], func=AF.Sigmoid), reads=[PK[5]], writes=["sgtmp"])
                                if blk == 0:
                                    S.op("dve", lambda e: e.tensor_tensor(out=padc[:, c, 15:271], in0=ps[4][:, 0:NB], in1=sgtmp[:], op=ALU.mult), reads=[PK[4], "sgtmp"], writes=["padc"])
                                    srcs = [padc[:, c, k:k + 256] for k in range(31)]
                                    acc = cacc[:, c, :]
                                    pk = "padc"
                                else:
                                    S.op("dve", lambda e: e.tensor_tensor(out=padl[:, c, :, 15:79], in0=ps[4][:, 0:NB].rearrange("p (a b) -> p a b", b=64),
                                                                          in1=sgtmp[:].rearrange("p (a b) -> p a b", b=64), op=ALU.mult), reads=[PK[4], "sgtmp"], writes=["padl"])
                                    srcs = [padl[:, c, :, k:k + 64] for k in range(31)]
                                    acc = cacc[:, c, :].rearrange("p (a b) -> p a b", b=64)
                                    pk = "padl"

                                def mk(k, c=c, srcs=srcs, acc=acc, pk=pk):
                                    if k == 0:
                                        return lambda: S.op("dve", lambda e: e.tensor_scalar_mul(out=acc, in0=srcs[0], scalar1=convw[:, c, 0:1]), reads=[pk, "pp"], writes=["cacc%d" % c])
                                    if k == 31:
                                        return lambda: S.op("dve", lambda e: e.tensor_scalar_add(out=cacc[:, c, :], in0=cacc[:, c, :], scalar1=convb[:, c:c + 1]), reads=["cacc%d" % c, "pp"], writes=["cacc%d" % c])
                                    return lambda: S.op("dve", lambda e: e.scalar_tensor_tensor(out=acc, in0=srcs[k], scalar=convw[:, c, k:k + 1], in1=acc, op0=ALU.mult, op1=ALU.add),
                                                        reads=[pk, "pp", "cacc%d" % c], writes=["cacc%d" % c])
                                taps += [mk(k) for k in range(32)]
                            for c in range(2):
                                fmproj(c * 128, c)
                                S.op("act", lambda e: e.activation(out=uT[:, c, :], in_=ps[c][:, 0:NB], func=AF.Gelu_apprx_tanh), reads=[PK[c]], writes=["uT"])
                            for cc in range(2):
                                for kc in range(8):
                                    S.op("pe", lambda e: e.matmul(ps[2][:, 0:256], lhsT=h[:, kc, cc * 128:(cc + 1) * 128], rhs=wB[:, kc, 256:512], start=(kc == 0), stop=(kc == 7)),
                                         reads=["wB", hk], writes=[PK[2]], inc=(kc == 7))
                                S.op("act", lambda e: e.activation(out=gv[:], in_=ps[2][:, 0:256], func=AF.Gelu_apprx_tanh), reads=[PK[2]], writes=["gv"])
                                S.op("dve", lambda e: e.bn_stats(out=st6[:], in_=gv[:]), reads=["gv"], writes=["s_st6"])
                                S.op("dve", lambda e: e.bn_aggr(out=mv[:], in_=st6[:]), reads=["s_st6"], writes=["s_mv"])
                                S.op("act", lambda e: e.activation(out=rs1[:], in_=mv[:, 1:2], func=AF.Sqrt, bias=epsc[:, 0:1]), reads=["s_mv", "epsc"], writes=["s_rs"])
                                S.op("dve", lambda e: e.reciprocal(out=rs1[:], in_=rs1[:]), reads=["s_rs"], writes=["s_rs"])
                                S.op("dve", lambda e: e.tensor_scalar(out=gv[:], in0=gv[:], scalar1=mv[:, 0:1], scalar2=rs1[:, 0:1], op0=ALU.subtract, op1=ALU.mult),
                                     reads=["gv", "s_mv", "s_rs"], writes=["gv"])
                                S.op("dve", lambda e: e.tensor_tensor(out=gv[:], in0=gv[:], in1=sgulnw, op=ALU.mult), reads=["gv", "bcr"], writes=["gv"])
                                S.op("dve", lambda e: e.tensor_tensor(out=vtok[:], in0=gv[:], in1=sgulnb, op=ALU.add), reads=["gv", "bcr"], writes=["vtok"])
                                for pr in range(2):
                                    for g2 in range(2):
                                        g = pr * 2 + g2
                                        S.op("pe", lambda e: e.matmul(ps[3][g2 * 64:(g2 + 1) * 64, 0:128], lhsT=vtok[:, g * 64:(g + 1) * 64], rhs=swT[:, g, :], start=True, stop=True,
                                                                      tile_position=(0, g2 * 64)), reads=["vtok", "swT"], writes=[PK[3]], inc=(g2 == 1))
                                    S.op("dve", lambda e: e.tensor_tensor(out=sgtmp2[:, 0:128], in0=ps[3][:, 0:128], in1=sbias[:, pr, :], op=ALU.add), reads=[PK[3], "sbias"], writes=["sgtmp2"])
                                    S.op("dve", lambda e: e.tensor_tensor(out=yb[:, pr, cc * 128:(cc + 1) * 128], in0=sgtmp2[:, 0:128], in1=uT[:, pr, cc * 128:(cc + 1) * 128], op=ALU.mult),
                                         reads=["sgtmp2", "uT"], writes=[ybk + "s"])
                            return taps

                        def back(blk):
                            c0 = blk * NB
                            yb = ybc[blk % 2]
                            ybk = "ybc%d" % (blk % 2)
                            ln_stats(stc, lambda c: cacc[:, c, :], 2, avgC, sq8, "sq8", 6, 7, NB, ["cacc0", "cacc1"], "c")
                            S.op("dve", lambda e: e.tensor_tensor(out=tnorm[:], in0=cacc[:], in1=stc["mean"][:].unsqueeze(1).to_broadcast([128, 2, NB]), op=ALU.subtract),
                                 reads=["cacc0", "cacc1", "meanc"], writes=["cacc0", "cacc1"])
                            S.op("dve", lambda e: e.tensor_tensor(out=tnorm[:], in0=tnorm[:], in1=stc["rstd"][:].unsqueeze(1).to_broadcast([128, 2, NB]), op=ALU.mult),
                                 reads=["cacc0", "cacc1", "rstdc"], writes=["cacc0", "cacc1"])
                            for c in range(2):
                                S.op("act", lambda e: e.activation(out=yb[:, 2 + c, :], in_=tnorm[:, c, :], func=AF.Silu, scale=convlnw[:, c:c + 1], bias=convlnb[:, c:c + 1]),
                                     reads=["cacc0", "cacc1", "pp"], writes=[ybk + "c"])
                            if debug and b == 0 and l == 0:
                                S.dma("pool", dbg_mix[:, 2:6, c0:c0 + NB], yb[:], reads=[ybk + "s", ybk + "c"])

                        def ln_apply(blk, lw, lb, with_h2, nxt=None):
                            c0 = blk * NB
                            mc = mcol(blk)
                            xk = "xT%d" % blk
                            xv = xT[:, :, c0:c0 + NB]
                            ln_stats(st, lambda c: xT[:, c, c0:c0 + NB], 8, avgD, sq8, "sq8", 6, 7, NB, [xk], "r")
                            if nxt is not None:
                                make_h(nxt)
                            S.op("dve", lambda e: e.tensor_tensor(out=xv, in0=xv, in1=st["mean"][:].unsqueeze(1).to_broadcast([128, 8, NB]), op=ALU.subtract), reads=[xk, "mean"], writes=[xk])
                            S.op("dve", lambda e: e.tensor_tensor(out=xv, in0=xv, in1=st["rstd"][:].unsqueeze(1).to_broadcast([128, 8, NB]), op=ALU.mult), reads=[xk, "rstd"], writes=[xk])
                            if with_h2:
                                for fc in range(8):
                                    S.op("act", lambda e: e.activation(out=h2T[:, fc, :], in_=xT[:, fc, c0:c0 + NB], func=AF.Identity,
                                                                       scale=A2[:, fc, mc:mc + 1], bias=B2[:, fc, mc:mc + 1]),
                                         reads=[xk, "A2B2"], writes=["h2T"])
                            S.op("pool", lambda e: e.tensor_tensor(out=xv, in0=xv, in1=lw.unsqueeze(2).to_broadcast([128, 8, NB]), op=ALU.mult), reads=[xk, "pp"], writes=[xk])
                            S.op("pool", lambda e: e.tensor_tensor(out=xv, in0=xv, in1=lb.unsqueeze(2).to_broadcast([128, 8, NB]), op=ALU.add), reads=[xk, "pp"], writes=[xk])

                        def wout_ln1(blk, nxt=None):
                            c0 = blk * NB
                            mc = mcol(blk)
                            yb = ybc[blk % 2]
                            ybk = "ybc%d" % (blk % 2)
                            xk = "xT%d" % blk
                            mix = [yad[:, 0, c0:c0 + NB], yad[:, 1, c0:c0 + NB], yb[:, 0, :], yb[:, 1, :], yb[:, 2, :], yb[:, 3, :], yad[:, 2, c0:c0 + NB], yad[:, 3, c0:c0 + NB]]
                            mixk = ["yad_a%d" % (2 * blk), "yad_a%d" % (2 * blk + 1), "yad_d", ybk + "s", ybk + "c"]
                            for oc in range(8):
                                pi = oc % 4
                                for kc in range(8):
                                    S.op("pe", lambda e: e.matmul(ps[pi][:, 0:NB], lhsT=wo[:, kc, oc * 128:(oc + 1) * 128], rhs=mix[kc], start=(kc == 0), stop=(kc == 7)),
                                         reads=["wo"] + mixk, writes=[PK[pi]], inc=(kc == 7))
                                S.op("act", lambda e: e.activation(out=tmp8[:], in_=ps[pi][:, 0:NB], func=AF.Identity, scale=modT[:, l, 16 + oc, mc:mc + 1]), reads=[PK[pi], "modT"], writes=["tmp8"])
                                S.op("dve", lambda e: e.scalar_tensor_tensor(out=xT[:, oc, c0:c0 + NB], in0=xT[:, oc, c0:c0 + NB], scalar=ALPHA, in1=tmp8[:], op0=ALU.mult, op1=ALU.add),
                                     reads=[xk, "tmp8"], writes=[xk])
                            ln_apply(blk, ln1w, ln1b, True, nxt)

                        def ffn(blk, pending, after_in):
                            c0 = blk * NB
                            mc = mcol(blk)
                            xk = "xT%d" % blk
                            for j in range(NJ):
                                wt = w1p[w1i[0] % 3]
                                wk = "w1p%d" % (w1i[0] % 3)
                                w1i[0] += 1
                                S.dma("sp", wt[:].rearrange("p a b -> p (a b)"), w1s[l, j], reads=["w1s_%d_%d" % (l, j)], writes=[wk])
                                for half in range(2):
                                    pi = half * 2 + (j % 2)
                                    for kc in range(8):
                                        S.op("pe", lambda e: e.matmul(ps[pi][:, 0:NB], lhsT=wt[:, kc, half * 128:(half + 1) * 128], rhs=h2T[:, kc, :], start=(kc == 0), stop=(kc == 7)),
                                             reads=[wk, "h2T"], writes=[PK[pi]], inc=(kc == 7))
                                sg = sgf[j % 2]
                                sgk = "sgf%d" % (j % 2)
                                S.op("act", lambda e: e.activation(out=sg[:], in_=ps[j % 2][:, 0:NB], func=AF.Silu), reads=[PK[j % 2]], writes=[sgk])
                                S.op("dve", lambda e: e.tensor_tensor(out=actT[:, j, :], in0=ps[2 + j % 2][:, 0:NB], in1=sg[:], op=ALU.mult), reads=[PK[2 + j % 2], sgk], writes=["actT%d" % j])
                                for _ in range(3):
                                    if pending:
                                        pending.pop(0)()
                            while pending:
                                pending.pop(0)()
                            after_in()
                            ak = ["actT%d" % j for j in range(NJ)]
                            for oc in range(8):
                                wt = w2p[w2i[0] % 2]
                                wk = "w2p%d" % (w2i[0] % 2)
                                w2i[0] += 1
                                S.dma("sp", wt[:].rearrange("p a b -> p (a b)"), w2s[l, oc], reads=["w2s_%d_%d" % (l, oc)], writes=[wk])
                                pi = 4 + oc % 2
                                for j in range(NJ):
                                    S.op("pe", lambda e: e.matmul(ps[pi][:, 0:NB], lhsT=wt[:, j, :], rhs=actT[:, j, :], start=(j == 0), stop=(j == NJ - 1)),
                                         reads=[wk] + ak, writes=[PK[pi]], inc=(j == NJ - 1))
                                S.op("act", lambda e: e.activation(out=tmp8[:], in_=ps[pi][:, 0:NB], func=AF.Identity, scale=modT[:, l, 40 + oc, mc:mc + 1]), reads=[PK[pi], "modT"], writes=["tmp8"])
                                S.op("dve", lambda e: e.scalar_tensor_tensor(out=xT[:, oc, c0:c0 + NB], in0=xT[:, oc, c0:c0 + NB], scalar=ALPHA, in1=tmp8[:], op0=ALU.mult, op1=ALU.add),
                                     reads=[xk, "tmp8", "h2T"], writes=[xk])
                            ln_apply(blk, ln2w, ln2b, False)

                        make_h(blocks[0])
                        tp0 = front(blocks[0])
                        for t_ in tp0:
                            t_()
                        back(blocks[0])
                        for bi, blk in enumerate(blocks):
                            wout_ln1(blk, blocks[bi + 1] if bi + 1 < len(blocks) else None)
                            if bi + 1 < len(blocks):
                                nb_ = blocks[bi + 1]
                                pend = front(nb_)
                                ffn(blk, pend, lambda: back(nb_))
                            else:
                                ffn(blk, [], lambda: None)
                        if debug and b == 0 and l == 0:
                            S.dma("pool", dbg_mix[:, 0:2, :], yad[:, 0:2, :], reads=["yad_a%d" % i for i in range(NCH)])
                            S.dma("pool", dbg_mix[:, 6:8, :], yad[:, 2:4, :], reads=["yad_d"])
                            S.dma("sp", dbg_x, xT[:], reads=["xT%d" % i for i in range(NBLK)])
                        S.barrier()

            with ExitStack() as PO:
                otok = [tile(PO, "otok%d" % i, [128, D]) for i in range(2)]
                for ch in range(2, NCH):
                    ot = otok[ch % 2]
                    ok = "otok%d" % (ch % 2)
                    for half in range(2):
                        pi = (ch * 2 + half) % 4
                        for q in range(4):
                            fc = half * 4 + q
                            S.op("pe", lambda e: e.transpose(out=ps[pi][:, q * 128:(q + 1) * 128], in_=xT[:, fc, ch * 128:(ch + 1) * 128], identity=ident),
                                 reads=["xT%d" % (ch // 2), "consts"], writes=[PK[pi]], inc=(q == 3))
                        if half == 0:
                            S.op("dve", lambda e: e.tensor_copy(out=ot[:, 0:512], in_=ps[pi][:, :]), reads=[PK[pi]], writes=[ok])
                        else:
                            S.op("act", lambda e: e.activation(out=ot[:, 512:1024], in_=ps[pi][:, :], func=AF.Identity), reads=[PK[pi]], writes=[ok])
                    S.dma("sp", out_d[b, (ch - 2) * 128:(ch - 1) * 128, :], ot[:], reads=[ok])
                S.barrier()
        S.barrier()
    return nc


def _host_prep(inp):
    f = lambda a: np.ascontiguousarray(np.asarray(a, dtype=np.float32))
    w_in = f(inp["w_in"])
    sh = {}
    sh["w_mod"] = f(inp["w_mod"])
    sh["bmodT"] = f(np.asarray(inp["b_mod"]).reshape(2, 48, 128).transpose(2, 0, 1))
    colsA = np.concatenate([np.arange(0, 1040), np.arange(2064, 2320)])
    sh["w_inA"] = f(w_in[:, :, colsA].reshape(2, 8, 128, 1296).transpose(0, 2, 1, 3))
    sh["w_inB"] = f(w_in[:, :, 1040:2064].reshape(2, 8, 128, 1024).transpose(0, 2, 1, 3))
    sh["w_outh"] = f(np.asarray(inp["w_out"]).reshape(2, 8, 128, 1024).transpose(0, 2, 1, 3))
    w1 = np.asarray(inp["w_ffn_in"], dtype=np.float32).reshape(2, 8, 128, 2, NJ, 128)
    sh["w1h"] = f(w1.transpose(0, 4, 2, 1, 3, 5).reshape(2, NJ, 128, 8, 256))
    w2 = np.asarray(inp["w_ffn_out"], dtype=np.float32).reshape(2, NJ, 128, 8, 128)
    sh["w2h"] = f(w2.transpose(0, 3, 2, 1, 4))
    pp = np.zeros((2, 128, NPP), np.float32)
    r8 = lambda a: np.asarray(a).reshape(2, -1, 128).transpose(0, 2, 1)
    pp[:, :, 0:8] = r8(inp["ln1_w"]); pp[:, :, 8:16] = r8(inp["ln1_b"])
    pp[:, :, 16:24] = r8(inp["ln2_w"]); pp[:, :, 24:32] = r8(inp["ln2_b"])
    pp[:, :, 32:94] = np.asarray(inp["conv_w"]).reshape(2, 31, 2, 128).transpose(0, 3, 2, 1).reshape(2, 128, 62)
    pp[:, :, 94:96] = r8(inp["conv_b"]); pp[:, :, 96:98] = r8(inp["conv_ln_w"]); pp[:, :, 98:100] = r8(inp["conv_ln_b"])
    pp[:, :, 100:102] = r8(inp["s5_d"]); pp[:, :, 102:104] = r8(inp["s5_glu_b"])
    sh["pp"] = pp
    sh["bc"] = f(np.concatenate([np.asarray(inp["mlstm_gate_bias"]), np.asarray(inp["mlstm_norm_w"]),
                                 np.asarray(inp["sgu_ln_w"]), np.asarray(inp["sgu_ln_b"])], axis=1))
    sh["sguwT"] = f(np.asarray(inp["sgu_w"]).transpose(0, 3, 1, 2))
    sb = np.asarray(inp["sgu_b"])
    sh["sgub"] = f(np.repeat(sb.reshape(2, 2, 2, 1, 128), 64, axis=3).reshape(2, 2, 128, 128).transpose(0, 2, 1, 3))
    bre, bim = np.asarray(inp["s5_b_re"]), np.asarray(inp["s5_b_im"])
    are, aim, ldt = np.asarray(inp["s5_a_re"]), np.asarray(inp["s5_a_im"]), np.asarray(inp["s5_log_dt"])
    s5B = np.zeros((2, 128, 2, 2, 128), np.float32)
    s5A = np.zeros((2, 128, 3, 2, 2, 128), np.float32)
    for chn in range(2):
        for ppi in range(4):
            for g2 in range(2):
                g = chn * 8 + ppi * 2 + g2
                rows = slice(ppi * 32 + g2 * 16, ppi * 32 + g2 * 16 + 16)
                s5B[:, rows, 0, chn, g2 * 64:(g2 + 1) * 64] = bre[:, g].transpose(0, 2, 1)
                s5B[:, rows, 1, chn, g2 * 64:(g2 + 1) * 64] = bim[:, g].transpose(0, 2, 1)
            for g2c in range(2):
                gc = chn * 8 + ppi * 2 + g2c
                rows = slice(ppi * 32, ppi * 32 + 32)
                for d in range(2):
                    s5A[:, rows, 0, d, chn, g2c * 64:(g2c + 1) * 64] = are[:, d, gc][:, None, :]
                    s5A[:, rows, 1, d, chn, g2c * 64:(g2c + 1) * 64] = aim[:, d, gc][:, None, :]
                    s5A[:, rows, 2, d, chn, g2c * 64:(g2c + 1) * 64] = ldt[:, d, gc][:, None, None]
    sh["s5B"], sh["s5A"] = s5B, s5A
    cre, cim = np.asarray(inp["s5_c_re"]), np.asarray(inp["s5_c_im"])
    s5C = np.zeros((2, 128, 2, 8, 32), np.float32)
    s5Bn = np.zeros((2, 128, 2, 8, 32), np.float32)
    s5pp = np.zeros((2, 128, 3, 16), np.float32)
    for pair in range(8):
        for g2 in range(2):
            g = pair * 2 + g2
            s5C[:, g2 * 64:(g2 + 1) * 64, 0, pair, g2 * 16:(g2 + 1) * 16] = cre[:, g].transpose(0, 2, 1)
            s5C[:, g2 * 64:(g2 + 1) * 64, 1, pair, g2 * 16:(g2 + 1) * 16] = cim[:, g].transpose(0, 2, 1)
            s5Bn[:, g2 * 64:(g2 + 1) * 64, 0, pair, g2 * 16:(g2 + 1) * 16] = bre[:, g]
            s5Bn[:, g2 * 64:(g2 + 1) * 64, 1, pair, g2 * 16:(g2 + 1) * 16] = bim[:, g]
            for d in range(2):
                s5pp[:, g2 * 64:(g2 + 1) * 64, 0, d * 8 + pair] = are[:, d, g]
                s5pp[:, g2 * 64:(g2 + 1) * 64, 1, d * 8 + pair] = aim[:, d, g]
                s5pp[:, g2 * 64:(g2 + 1) * 64, 2, d * 8 + pair] = ldt[:, d, g][:, None]
    sh["s5C"], sh["s5pp"], sh["s5Bn"] = s5C, s5pp, s5Bn
    sh["gluh"] = f(np.asarray(inp["s5_glu_w"]).reshape(2, 2, 128, 256).transpose(0, 2, 1, 3))
    cst = np.zeros((128, 3, 128), np.float32)
    cst[:, 0] = np.eye(128)
    cst[:, 1] = np.triu(np.ones((128, 128)))
    cst[:, 2] = np.tril(np.ones((128, 128)))
    sh["consts"] = cst
    return sh


def _core_inputs(inp, shared, core):
    x, c, ctx, c_ctx = (np.asarray(inp[k], dtype=np.float32) for k in ("x", "c", "ctx", "c_ctx"))
    m = dict(shared)
    bs = [2 * core, 2 * core + 1]
    m["xin"] = np.ascontiguousarray(np.concatenate([ctx[bs], x[bs]], axis=1))
    cv = np.stack([c[bs[0]], c[bs[1]], c_ctx], axis=1)
    m["cT"] = np.ascontiguousarray(cv.reshape(8, 128, 3).transpose(1, 0, 2))
    return m


def kernel(**inputs):
    shared = _host_prep(inputs)
    nc = build_nc()
    in_maps = [_core_inputs(inputs, shared, core) for core in range(8)]
    res = run_bass_kernel_spmd(nc, in_maps, core_ids=list(range(8)))
    out = np.concatenate([np.asarray(r["out"]) for r in res.results], axis=0)
    return out.astype(np.float32)
```

```python
import math
from contextlib import ExitStack
import numpy as np
import concourse.bass as bass
import concourse.mybir as mybir
from concourse.bass_utils import run_bass_kernel_spmd

F32 = mybir.dt.float32
BF16 = mybir.dt.bfloat16
AF = mybir.ActivationFunctionType
ALU = mybir.AluOpType

D = 1024
T = 2304
NB = 256
NBLK = 9
NCH = 18
DFF = 2816
NJ = 22
ALPHA = 4.0 ** 0.25
EPS = 1e-5
PI = math.pi
NPP = 104
NBC = 784
SF = 16
SJ = T // SF


class Sync:
    NDMA = 40

    def __init__(self, nc, stack):
        self.nc = nc
        self.eng = {"pe": nc.tensor, "dve": nc.vector, "act": nc.scalar, "pool": nc.gpsimd, "sp": nc.sync}
        self.sem = {k: stack.enter_context(nc.semaphore("s_" + k)) for k in self.eng}
        self.cnt = {k: 0 for k in self.eng}
        self.seen = {k: {} for k in self.eng}
        self.dsem = [stack.enter_context(nc.semaphore("d%d" % i)) for i in range(self.NDMA)]
        self.dcnt = [0] * self.NDMA
        self.dnext = 0
        self.dnext_sw = 0
        self.lw = {}
        self.rd = {}
        self.semobj = {}
        for k in self.eng:
            self.semobj[("e", k)] = self.sem[k]
        for i in range(self.NDMA):
            self.semobj[("d", i)] = self.dsem[i]

    def _wait(self, e, sid, val):
        if self.seen[e].get(sid, 0) >= val:
            return
        if sid[0] == "e" and val > self.cnt[sid[1]]:
            if sid[1] == e:
                return
            raise RuntimeError("wait on unsignalled instruction: %s waits %s >= %d (cnt %d)" % (e, sid, val, self.cnt[sid[1]]))
        self.eng[e].wait_ge(self.semobj[sid], val)
        self.seen[e][sid] = val

    def _deps(self, e, reads, writes):
        for k in reads:
            if k in self.lw:
                self._wait(e, *self.lw[k])
        for k in writes:
            if k in self.lw:
                self._wait(e, *self.lw[k])
            for sid, val in self.rd.get(k, {}).items():
                self._wait(e, sid, val)

    def _record(self, sid, val, reads, writes):
        for k in reads:
            d = self.rd.setdefault(k, {})
            d[sid] = max(d.get(sid, 0), val)
        for k in writes:
            self.lw[k] = (sid, val)
            self.rd[k] = {}

    def op(self, e, fn, reads=(), writes=(), inc=True):
        self._deps(e, reads, writes)
        inst = fn(self.eng[e])
        if inc:
            self.cnt[e] += 1
            inst.then_inc(self.sem[e], 1)
            val = self.cnt[e]
        else:
            val = self.cnt[e] + 1
        self._record(("e", e), val, reads, writes)

    def dma(self, e, out, in_, reads=(), writes=(), **kw):
        half = self.NDMA // 2
        if e == "pool":
            s = half + self.dnext_sw
            self.dnext_sw = (self.dnext_sw + 1) % half
        else:
            s = self.dnext
            self.dnext = (self.dnext + 1) % half
        sid = ("d", s)
        if self.dcnt[s] > 0:
            self._wait(e, sid, self.dcnt[s])
        self._deps(e, reads, writes)
        self.dcnt[s] += 16
        self.eng[e].dma_start(out=out, in_=in_, **kw).then_inc(self.dsem[s], 16)
        self._record(sid, self.dcnt[s], reads, writes)

    def barrier(self):
        for e in self.eng:
            for k in self.eng:
                if k != e and self.cnt[k] > 0:
                    self._wait(e, ("e", k), self.cnt[k])
            for i in range(self.NDMA):
                if self.dcnt[i] > 0:
                    self._wait(e, ("d", i), self.dcnt[i])
        self.lw = {}
        self.rd = {}


def build_nc(nbatch=2, nlayer=2, debug=False):
    nc = bass.Bass("TRN2", target_bir_lowering=False)

    def din(name, shape):
        return nc.dram_tensor(name, list(shape), F32, kind="ExternalInput").ap()

    xin = din("xin", [2, T, D])
    cT_d = din("cT", [128, 8, 3])
    wmod_d = din("w_mod", [2, D, 6 * D])
    bmod_d = din("bmodT", [128, 2, 48])
    winA_d = din("w_inA", [2, 128, 8, 1296])
    winB_d = din("w_inB", [2, 128, 8, 1024])
    wout_d = din("w_outh", [2, 128, 8, 1024])
    w1_d = din("w1h", [2, NJ, 128, 8, 256])
    w2_d = din("w2h", [2, 8, 128, NJ, 128])
    pp_d = din("pp", [2, 128, NPP])
    bc_d = din("bc", [2, NBC])
    sguw_d = din("sguwT", [2, 128, 4, 128])
    sgub_d = din("sgub", [2, 128, 2, 128])
    s5B_d = din("s5B", [2, 128, 2, 2, 128])
    s5A_d = din("s5A", [2, 128, 3, 2, 2, 128])
    s5C_d = din("s5C", [2, 128, 2, 8, 32])
    s5Bn_d = din("s5Bn", [2, 128, 2, 8, 32])
    s5pp_d = din("s5pp", [2, 128, 3, 16])
    glu_d = din("gluh", [2, 128, 2, 256])
    consts_d = din("consts", [128, 3, 128])
    out_d = nc.dram_tensor("out", [2, 2048, D], F32, kind="ExternalOutput").ap()
    w1s = nc.dram_tensor("w1s", [2, NJ, 128, 8 * 256], BF16, kind="Internal").ap()
    w2s = nc.dram_tensor("w2s", [2, 8, 128, NJ * 128], BF16, kind="Internal").ap()
    cWE = nc.dram_tensor("cWE", [2, 128, 16 * 2 * 512], BF16, kind="Internal").ap()
    cG = nc.dram_tensor("cG", [2, 128, 2 * 2 * 16 * 128], BF16, kind="Internal").ap()
    cG0 = nc.dram_tensor("cG0", [2, 128, 256], BF16, kind="Internal").ap()
    cPW = nc.dram_tensor("cPW", [2, 128, 2 * 17 * 16], F32, kind="Internal").ap()
    cPW2 = nc.dram_tensor("cPW2", [2, 128, 2 * 9 * 16], F32, kind="Internal").ap()
    cNPW2 = nc.dram_tensor("cNPW2", [2, 128, 9 * 16], F32, kind="Internal").ap()
    if debug:
        dbg_mix = nc.dram_tensor("dbg_mix", [128, 8, T], F32, kind="ExternalOutput").ap()
        dbg_x = nc.dram_tensor("dbg_x", [128, 8, T], F32, kind="ExternalOutput").ap()
        dbg_mod = nc.dram_tensor("dbg_mod", [128, 2, 48, 3], F32, kind="ExternalOutput").ap()
        dbg_h = nc.dram_tensor("dbg_h", [128, 8, NB], F32, kind="ExternalOutput").ap()
        dbg_qk = nc.dram_tensor("dbg_qk", [128, 4, T], F32, kind="ExternalOutput").ap()
        dbg_s5u = nc.dram_tensor("dbg_s5u", [128, 2, T], F32, kind="ExternalOutput").ap()
        dbg_xT0 = nc.dram_tensor("dbg_xT0", [128, 8, T], F32, kind="ExternalOutput").ap()

    with ExitStack() as G:
        S = Sync(nc, G)

        uid = [0]

        def tile(st, name, shape, dt=F32):
            uid[0] += 1
            return st.enter_context(nc.sbuf_tensor("t%d_%s" % (uid[0], name), list(shape), dt))

        ps = [G.enter_context(nc.psum_tensor("ps%d" % i, [128, 512], F32)) for i in range(8)]
        PK = ["ps%d" % i for i in range(8)]

        xT = tile(G, "xT", [128, 8, T])
        consts = tile(G, "consts", [128, 3, 128])
        ident = consts[:, 0, :]
        tri_le = consts[:, 1, :]
        tri_ge = consts[:, 2, :]
        ones_f = tile(G, "ones_f", [128, 128])
        avgD = tile(G, "avgD", [128, 128])
        avgC = tile(G, "avgC", [128, 128])
        modT = tile(G, "modT", [128, 2, 48, 3])
        ops1 = tile(G, "ops1", [128, 2, 8, 3])
        ops2 = tile(G, "ops2", [128, 2, 8, 3])

        S.dma("sp", consts[:], consts_d, writes=["consts"])
        epsc = tile(G, "epsc", [128, 1])
        S.op("dve", lambda e: e.memset(epsc[:], EPS), writes=["epsc"])
        S.op("dve", lambda e: e.memset(ones_f[:], 1.0), writes=["ones_f"])
        S.op("dve", lambda e: e.memset(avgD[:], 1.0 / D), writes=["avgD"])
        S.op("dve", lambda e: e.memset(avgC[:], 1.0 / 256.0), writes=["avgC"])

        for l in range(nlayer):
            for j in range(NJ):
                S.dma("pool", w1s[l, j], w1_d[l, j].rearrange("p a b -> p (a b)"), writes=["w1s_%d_%d" % (l, j)])
            for oc in range(8):
                S.dma("pool", w2s[l, oc], w2_d[l, oc].rearrange("p a b -> p (a b)"), writes=["w2s_%d_%d" % (l, oc)])

        with ExitStack() as P0:
            cTt = tile(P0, "cTt", [128, 8, 3])
            scT = tile(P0, "scT", [128, 8, 3])
            bmod = tile(P0, "bmod", [128, 2, 48])
            wm = [tile(P0, "wm%d" % i, [128, 6 * D]) for i in range(2)]
            S.dma("sp", cTt[:], cT_d, writes=["cTt"])
            S.dma("sp", bmod[:], bmod_d, writes=["bmod"])
            S.op("act", lambda e: e.activation(out=scT[:], in_=cTt[:], func=AF.Silu), reads=["cTt"], writes=["scT"])
            it = 0
            for l in range(nlayer):
                for kc in range(8):
                    w = wm[it % 2]
                    wk = "wm%d" % (it % 2)
                    it += 1
                    S.dma("sp", w[:], wmod_d[l, kc * 128:(kc + 1) * 128, :], writes=[wk])
                    pz = kc % 2
                    for j in range(48):
                        S.op("pe", lambda e: e.matmul(ps[pz][:, j * 3:(j + 1) * 3], lhsT=w[:, j * 128:(j + 1) * 128],
                                                      rhs=scT[:, kc, :], start=True, stop=True),
                             reads=[wk, "scT"], writes=[PK[pz]], inc=(j == 47))
                    S.op("dve", lambda e: e.tensor_tensor(
                        out=modT[:, l, :, :], in0=ps[pz][:, 0:144].rearrange("p (a b) -> p a b", b=3),
                        in1=(bmod[:, l, :].unsqueeze(2).to_broadcast([128, 48, 3]) if kc == 0 else modT[:, l, :, :]), op=ALU.add),
                        reads=[PK[pz], "bmod", "modT"], writes=["modT"])
            S.op("dve", lambda e: e.tensor_scalar_add(out=ops1[:], in0=modT[:, :, 8:16, :], scalar1=1.0), reads=["modT"], writes=["ops1"])
            S.op("dve", lambda e: e.tensor_scalar_add(out=ops2[:], in0=modT[:, :, 32:40, :], scalar1=1.0), reads=["modT"], writes=["ops2"])
            if debug:
                S.dma("sp", dbg_mod, modT[:], reads=["modT"])
            S.barrier()

        def ln_stats(st, src_fn, nchunks, avg, sq_tile, sqk, pA, pB, width, srckeys, tag):
            mean = st["mean"]
            m2 = st["m2"]
            rstd = st["rstd"]
            kmean, km2, krstd = st.get("keys", ("mean", "m2", "rstd"))
            for c in range(nchunks):
                S.op("act", lambda e: e.activation(out=sq_tile[:, c, 0:width], in_=src_fn(c), func=AF.Square),
                     reads=srckeys, writes=[sqk])
            for c in range(nchunks):
                S.op("pe", lambda e: e.matmul(ps[pA][:, 0:width], lhsT=avg[:], rhs=src_fn(c), start=(c == 0), stop=(c == nchunks - 1)),
                     reads=srckeys, writes=[PK[pA]], inc=(c == nchunks - 1))
            for c in range(nchunks):
                S.op("pe", lambda e: e.matmul(ps[pB][:, 0:width], lhsT=avg[:], rhs=sq_tile[:, c, 0:width], start=(c == 0), stop=(c == nchunks - 1)),
                     reads=[sqk], writes=[PK[pB]], inc=(c == nchunks - 1))
            S.op("act", lambda e: e.activation(out=mean[:, 0:width], in_=ps[pA][:, 0:width], func=AF.Identity), reads=[PK[pA]], writes=[kmean])
            S.op("dve", lambda e: e.tensor_tensor(out=m2[:, 0:width], in0=mean[:, 0:width], in1=mean[:, 0:width], op=ALU.mult), reads=[kmean], writes=[km2])
            S.op("dve", lambda e: e.tensor_tensor(out=m2[:, 0:width], in0=ps[pB][:, 0:width], in1=m2[:, 0:width], op=ALU.subtract), reads=[PK[pB], km2], writes=[km2])
            S.op("act", lambda e: e.activation(out=rstd[:, 0:width], in_=m2[:, 0:width], func=AF.Ln, bias=epsc[:, 0:1]), reads=[km2, "epsc"], writes=[krstd])
            S.op("act", lambda e: e.activation(out=rstd[:, 0:width], in_=rstd[:, 0:width], func=AF.Exp, scale=-0.5), reads=[krstd], writes=[krstd])

        for b in range(nbatch):
            with ExitStack() as PL:
                xtok = [tile(PL, "xtok%d" % i, [128, D]) for i in range(2)]
                for ch in range(NCH):
                    xt = xtok[ch % 2]
                    xk = "xtok%d" % (ch % 2)
                    S.dma("sp", xt[:], xin[b, ch * 128:(ch + 1) * 128, :], writes=[xk])
                    for half in range(2):
                        pi = (ch * 2 + half) % 4
                        for q in range(4):
                            fc = half * 4 + q
                            S.op("pe", lambda e: e.transpose(out=ps[pi][:, q * 128:(q + 1) * 128], in_=xt[:, fc * 128:(fc + 1) * 128], identity=ident),
                                 reads=[xk, "consts"], writes=[PK[pi]], inc=(q == 3))
                        S.op("dve" if half == 0 else "act",
                             (lambda e: e.tensor_copy(out=xT[:, half * 4:half * 4 + 4, ch * 128:(ch + 1) * 128],
                                                      in_=ps[pi][:, :].rearrange("p (a b) -> p a b", b=128))) if half == 0 else
                             (lambda e: e.activation(out=xT[:, half * 4:half * 4 + 4, ch * 128:(ch + 1) * 128],
                                                     in_=ps[pi][:, :].rearrange("p (a b) -> p a b", b=128), func=AF.Identity)),
                             reads=[PK[pi]], writes=["xT%d" % (ch // 2)])
                S.barrier()

            for l in range(nlayer):
                last = (l == nlayer - 1) and (nlayer == 2)
                with ExitStack() as L:
                    pp = tile(L, "pp", [128, NPP])
                    bcr = tile(L, "bcr", [128, NBC])
                    yad = tile(L, "yad", [128, 4, T], BF16)
                    S.dma("sp", pp[:], pp_d[l], writes=["pp"])
                    S.dma("sp", bcr[:], bc_d[l:l + 1, :].partition_broadcast(128) if False else bc_d[l:l + 1, :].to_broadcast([128, NBC]), writes=["bcr"])
                    ln1w, ln1b, ln2w, ln2b = pp[:, 0:8], pp[:, 8:16], pp[:, 16:24], pp[:, 24:32]
                    convw = pp[:, 32:94].rearrange("p (c k) -> p c k", k=31)
                    convb, convlnw, convlnb = pp[:, 94:96], pp[:, 96:98], pp[:, 98:100]
                    s5d, glub = pp[:, 100:102], pp[:, 102:104]
                    gbias = bcr[:, 0:16]
                    normw = bcr[:, 16:272]
                    sgulnw = bcr[:, 272:528]
                    sgulnb = bcr[:, 528:784]

                    def mcol(blk):
                        return 2 if blk == 0 else b

                    with ExitStack() as P12:
                        s5u = tile(P12, "s5u", [128, 2, SF, SJ], BF16)
                        PM = ExitStack()
                        qkT = tile(PM, "qkT", [128, 4, T], BF16)
                        ktok = tile(PM, "ktok", [128, NCH, 256], BF16)
                        vaug = tile(PM, "vaug", [128, NCH, 4, 65], BF16)
                        sigo = tile(PM, "sigo", [128, NCH, 256], BF16)
                        gates = tile(PM, "gates", [128, NCH, 16])
                        with ExitStack() as P1:
                            wA = tile(P1, "wA", [128, 8, 1296], BF16)
                            hT = [tile(P1, "hT%d" % i, [128, 8, NB], BF16) for i in range(2)]
                            S.dma("pool", wA[:], winA_d[l], writes=["wA"])
                            S.op("dve", lambda e: e.memset(vaug[:], 1.0), writes=["vaug"])
                            for blk in range(NBLK):
                                c0 = blk * NB
                                mc = mcol(blk)
                                h = hT[blk % 2]
                                hk = "hT%d" % (blk % 2)
                                for fc in range(8):
                                    S.op("act", lambda e: e.activation(out=h[:, fc, :], in_=xT[:, fc, c0:c0 + NB], func=AF.Identity,
                                                                       scale=ops1[:, l, fc, mc:mc + 1], bias=modT[:, l, fc, mc:mc + 1]),
                                         reads=["xT%d" % blk, "ops1", "modT"], writes=[hk])
                                if debug and b == 0 and l == 0 and blk == 1:
                                    S.dma("pool", dbg_h, h[:], reads=[hk])
                                fm = [(0, qkT, 0, 1.0), (128, qkT, 1, 1.0), (256, qkT, 2, 0.125), (384, qkT, 3, 0.125),
                                      (1040, s5u, 0, 1.0), (1168, s5u, 1, 1.0)]
                                for fi, (wc, dst, dc, scl) in enumerate(fm):
                                    pi = fi % 2
                                    for kc in range(8):
                                        S.op("pe", lambda e: e.matmul(ps[pi][:, 0:NB], lhsT=wA[:, kc, wc:wc + 128], rhs=h[:, kc, :],
                                                                      start=(kc == 0), stop=(kc == 7)),
                                             reads=["wA", hk], writes=[PK[pi]], inc=(kc == 7))
                                    dk = ("qkT%d" % blk) if dst is qkT else ("s5u%d" % blk)
                                    if dst is s5u:
                                        S.op("act", lambda e: e.activation(out=s5u[:, dc, :, blk * 16:(blk + 1) * 16], in_=ps[pi][:, 0:NB].rearrange("p (j i) -> p i j", i=SF), func=AF.Identity),
                                             reads=[PK[pi]], writes=[dk])
                                    elif fi % 2 == 0:
                                        S.op("act", lambda e: e.activation(out=dst[:, dc, c0:c0 + NB], in_=ps[pi][:, 0:NB], func=AF.Identity, scale=scl),
                                             reads=[PK[pi]], writes=[dk])
                                    else:
                                        S.op("dve", lambda e: e.tensor_scalar_mul(out=dst[:, dc, c0:c0 + NB], in0=ps[pi][:, 0:NB], scalar1=scl),
                                             reads=[PK[pi]], writes=[dk])
                                for cc in range(2):
                                    ch = blk * 2 + cc
                                    pa, pb = 2 + cc, 4 + cc
                                    for kc in range(8):
                                        S.op("pe", lambda e: e.matmul(ps[pa][:, 0:512], lhsT=h[:, kc, cc * 128:(cc + 1) * 128], rhs=wA[:, kc, 256:768],
                                                                      start=(kc == 0), stop=(kc == 7)), reads=["wA", hk], writes=[PK[pa]], inc=(kc == 7))
                                    for kc in range(8):
                                        S.op("pe", lambda e: e.matmul(ps[pb][:, 0:272], lhsT=h[:, kc, cc * 128:(cc + 1) * 128], rhs=wA[:, kc, 768:1040],
                                                                      start=(kc == 0), stop=(kc == 7)), reads=["wA", hk], writes=[PK[pb]], inc=(kc == 7))
                                    ck = "tm%d" % ch
                                    S.op("act", lambda e: e.activation(out=ktok[:, ch, :], in_=ps[pa][:, 0:256], func=AF.Identity, scale=0.125),
                                         reads=[PK[pa]], writes=[ck])
                                    S.op("dve", lambda e: e.tensor_copy(out=vaug[:, ch, :, 0:64], in_=ps[pa][:, 256:512].rearrange("p (a b) -> p a b", b=64)),
                                         reads=[PK[pa], "vaug"], writes=[ck])
                                    S.op("act", lambda e: e.activation(out=sigo[:, ch, :], in_=ps[pb][:, 0:256], func=AF.Sigmoid),
                                         reads=[PK[pb]], writes=[ck])
                                    S.op("dve", lambda e: e.tensor_tensor(out=gates[:, ch, :], in0=ps[pb][:, 256:272], in1=gbias, op=ALU.add),
                                         reads=[PK[pb], "bcr"], writes=[ck])
                            if debug and b == 0 and l == 0:
                                S.dma("pool", dbg_qk, qkT[:], reads=["qkT%d" % i for i in range(NBLK)])
                                S.dma("sp", dbg_xT0, xT[:], reads=["xT%d" % i for i in range(NBLK)])
                            S.barrier()

                        with ExitStack() as P2:
                            gt = tile(P2, "gt", [128, NCH, 4, 8])
                            Hs = tile(P2, "Hs", [128, NCH, 256])
                            Cst = tile(P2, "Cst", [128, 4, 65])
                            Cbf = tile(P2, "Cbf", [128, 4, 65], BF16)
                            PT = [tile(P2, "PT%d" % i, [128, 128], BF16) for i in range(2)]
                            Kpp = [tile(P2, "Kpp%d" % i, [128, 64], BF16) for i in range(2)]
                            sm = [tile(P2, "sm%d" % i, [128, 4]) for i in range(2)]
                            gtmp = [tile(P2, "gtmp%d" % i, [128, 40]) for i in range(2)]
                            S.op("dve", lambda e: e.memset(Cst[:], 0.0), writes=["Cst"])
                            S.op("dve", lambda e: e.memset(Cbf[:], 0.0), writes=["Cbf"])
                            for ch in range(NCH):
                                g = gates[:, ch, :].rearrange("p (a b) -> p a b", b=4)
                                tm = gtmp[ch % 2]
                                tk = "gtmp%d" % (ch % 2)
                                sp_ = tm[:, 0:8]
                                S.op("act", lambda e: e.activation(out=sp_.rearrange("p (a b) -> p a b", b=4), in_=g[:, 1::2, :], func=AF.Exp, scale=-1.0),
                                     reads=["tm%d" % ch], writes=[tk])
                                S.op("act", lambda e: e.activation(out=sp_, in_=sp_, func=AF.Ln, bias=1.0), reads=[tk], writes=[tk])
                                S.op("pe", lambda e: e.matmul(ps[6][:, 0:4], lhsT=tri_le, rhs=sp_[:, 0:4], start=True, stop=True), reads=[tk, "consts"], writes=[PK[6]], inc=False)
                                S.op("pe", lambda e: e.matmul(ps[6][:, 4:8], lhsT=tri_ge, rhs=sp_[:, 4:8], start=True, stop=True), reads=[tk, "consts"], writes=[PK[6]], inc=False)
                                S.op("pe", lambda e: e.matmul(ps[6][:, 8:16], lhsT=ones_f[:], rhs=sp_, start=True, stop=True), reads=[tk, "ones_f"], writes=[PK[6]], inc=True)
                                gk = "gt%d" % ch
                                S.op("act", lambda e: e.activation(out=gt[:, ch, 0, :], in_=ps[6][:, 0:8], func=AF.Exp, scale=-1.0), reads=[PK[6]], writes=[gk])
                                S.op("dve", lambda e: e.tensor_tensor(out=tm[:, 8:16].rearrange("p (a b) -> p a b", b=4),
                                                                      in0=ps[6][:, 0:8].rearrange("p (a b) -> p a b", b=4), in1=g[:, 0::2, :], op=ALU.add),
                                     reads=[PK[6], "tm%d" % ch], writes=[tk])
                                S.op("act", lambda e: e.activation(out=gt[:, ch, 1, :], in_=tm[:, 8:16], func=AF.Exp), reads=[tk], writes=[gk])
                                S.op("dve", lambda e: e.tensor_tensor(out=tm[:, 16:24], in0=tm[:, 8:16], in1=ps[6][:, 8:16], op=ALU.subtract), reads=[PK[6], tk], writes=[tk])
                                S.op("act", lambda e: e.activation(out=gt[:, ch, 2, :], in_=tm[:, 16:24], func=AF.Exp), reads=[tk], writes=[gk])
                                S.op("act", lambda e: e.activation(out=gt[:, ch, 3, :], in_=ps[6][:, 8:16], func=AF.Exp, scale=-1.0), reads=[PK[6]], writes=[gk])
                            order = [list(range(NCH)), [1, 0] + list(range(NCH - 1, 1, -1))]
                            written = set()
                            PTs = [tile(P2, "PTs%d" % i, [128, 128], BF16) for i in range(2)]
                            maskb = tile(P2, "maskb", [128, 2, 128], BF16)
                            S.op("dve", lambda e: e.tensor_copy(out=maskb[:], in_=consts[:, 1:3, :]), reads=["consts"], writes=["maskb"])
                            flat = []
                            for step in range(NCH):
                                for d in range(2):
                                    for hh in range(4):
                                        it = len(flat)
                                        ch = order[d][step]
                                        flat.append(dict(it=it, step=step, d=d, hh=hh, ch=ch, hd=d * 4 + hh, qc=hh // 2, po=(hh % 2) * 64, ci=d * 2 + hh // 2,
                                                         cs=slice(ch * 128, (ch + 1) * 128), qk="qkT%d" % (ch // 2), ck="tm%d" % ch, gk="gt%d" % ch,
                                                         stk="C%d_%d" % (d * 2 + hh // 2, hh % 2), mask=(tri_le if d == 0 else tri_ge)))

                            def stage1(q):
                                it, ch, hh, hd, po, qc, cs = q["it"], q["ch"], q["hh"], q["hd"], q["po"], q["qc"], q["cs"]
                                pS = it % 2
                                ptile, kp = PT[it % 2], Kpp[it % 2]
                                ptk, kpk = "PT%d" % (it % 2), "Kpp%d" % (it % 2)
                                S.op("pe", lambda e: e.matmul(ps[pS][:, 0:128], lhsT=qkT[po:po + 64, 2 + qc, cs], rhs=qkT[po:po + 64, qc, cs], start=True, stop=True),
                                     reads=[q["qk"]], writes=[PK[pS]])
                                pts = PTs[it % 2]
                                ptsk = "PTs%d" % (it % 2)
                                S.op("act", lambda e: e.activation(out=pts[:], in_=ps[pS][:, 0:128], func=AF.Identity, scale=gt[:, ch, 1, hd:hd + 1]),
                                     reads=[PK[pS], q["gk"]], writes=[ptsk])
                                S.op("pool", lambda e: e.tensor_tensor(out=ptile[:], in0=pts[:], in1=(maskb[:, 0, :] if q["d"] == 0 else maskb[:, 1, :]), op=ALU.mult),
                                     reads=[ptsk, "maskb"], writes=[ptk])
                                if q["step"] < NCH - 1:
                                    S.op("act", lambda e: e.activation(out=kp[:], in_=ktok[:, ch, hh * 64:(hh + 1) * 64], func=AF.Identity, scale=gt[:, ch, 2, hd:hd + 1]),
                                         reads=[q["ck"], q["gk"]], writes=[kpk])

                            def stage2(q):
                                it, ch, hh, hd, po, qc, cs, ci = q["it"], q["ch"], q["hh"], q["hd"], q["po"], q["qc"], q["cs"], q["ci"]
                                pA, pC = 2 + it % 2, 4 + it % 2
                                ptile, kp, smt = PT[it % 2], Kpp[it % 2], sm[it % 2]
                                ptk, kpk, smk = "PT%d" % (it % 2), "Kpp%d" % (it % 2), "sm%d" % (it % 2)
                                qk, ck, gk, stk = q["qk"], q["ck"], q["gk"], q["stk"]
                                S.op("pe", lambda e: e.matmul(ps[pA][:, 0:65], lhsT=ptile[:], rhs=vaug[:, ch, hh, :], start=True, stop=False),
                                     reads=[ptk, ck], writes=[PK[pA]], inc=False)
                                S.op("pe", lambda e: e.matmul(ps[pA][:, 0:65], lhsT=qkT[po:po + 64, qc, cs], rhs=Cbf[po:po + 64, ci, :], start=False, stop=True),
                                     reads=[qk, stk + "b", "Cbf"], writes=[PK[pA]])
                                if q["step"] < NCH - 1:
                                    S.op("pe", lambda e: e.matmul(ps[pC][po:po + 64, 0:65], lhsT=kp[:], rhs=vaug[:, ch, hh, :], start=True, stop=True, tile_position=(0, po)),
                                         reads=[kpk, ck], writes=[PK[pC]])
                                S.op("act", lambda e: e.activation(out=smt[:, 0:1], in_=ps[pA][:, 64:65], func=AF.Abs, scale=gt[:, ch, 0, hd:hd + 1]),
                                     reads=[PK[pA], gk], writes=[smk])
                                S.op("dve", lambda e: e.tensor_scalar_max(out=smt[:, 0:1], in0=smt[:, 0:1], scalar1=1.0), reads=[smk], writes=[smk])
                                S.op("dve", lambda e: e.reciprocal(out=smt[:, 2:3], in_=smt[:, 0:1]), reads=[smk], writes=[smk])
                                S.op("dve", lambda e: e.tensor_tensor(out=smt[:, 1:2], in0=gt[:, ch, 0, hd:hd + 1], in1=smt[:, 2:3], op=ALU.mult),
                                     reads=[smk, gk], writes=[smk])
                                hk_ = "Hs%d_%d" % (ch, hh)
                                if (ch, hh) not in written:
                                    written.add((ch, hh))
                                    S.op("dve", lambda e: e.tensor_scalar_mul(out=Hs[:, ch, hh * 64:(hh + 1) * 64], in0=ps[pA][:, 0:64], scalar1=smt[:, 1:2]),
                                         reads=[PK[pA], smk], writes=[hk_])
                                else:
                                    S.op("dve", lambda e: e.scalar_tensor_tensor(out=Hs[:, ch, hh * 64:(hh + 1) * 64], in0=ps[pA][:, 0:64], scalar=smt[:, 1:2],
                                                                                 in1=Hs[:, ch, hh * 64:(hh + 1) * 64], op0=ALU.mult, op1=ALU.add),
                                         reads=[PK[pA], smk, hk_], writes=[hk_])
                                if q["step"] < NCH - 1:
                                    S.op("dve", lambda e: e.scalar_tensor_tensor(out=Cst[po:po + 64, ci, :], in0=Cst[po:po + 64, ci, :], scalar=gt[po:po + 64, ch, 3, hd:hd + 1],
                                                                                 in1=ps[pC][po:po + 64, 0:65], op0=ALU.mult, op1=ALU.add),
                                         reads=[PK[pC], gk, stk, "Cst"], writes=[stk])
                                    S.op("pool", lambda e: e.tensor_copy(out=Cbf[po:po + 64, ci, :], in_=Cst[po:po + 64, ci, :]),
                                         reads=[stk, "Cbf"], writes=[stk + "b"])

                            for i in range(len(flat) + 1):
                                if i < len(flat):
                                    stage1(flat[i])
                                if i >= 1:
                                    stage2(flat[i - 1])
                            with ExitStack() as P2r:
                                st6 = [tile(P2r, "st6_%d" % i, [128, 4, 6]) for i in range(2)]
                                mv = [tile(P2r, "mv%d" % i, [128, 4, 2]) for i in range(2)]
                                rs = [tile(P2r, "rs%d" % i, [128, 4]) for i in range(2)]
                                ya = [tile(P2r, "ya%d" % i, [128, 256]) for i in range(2)]
                                for ch in range(NCH):
                                    i2 = ch % 2
                                    hkeys = ["Hs%d_%d" % (ch, hh) for hh in range(4)]
                                    for hh in range(4):
                                        S.op("dve", lambda e: e.bn_stats(out=st6[i2][:, hh, :], in_=Hs[:, ch, hh * 64:(hh + 1) * 64]), reads=hkeys, writes=["st6_%d" % i2])
                                        S.op("dve", lambda e: e.bn_aggr(out=mv[i2][:, hh, :], in_=st6[i2][:, hh, :]), reads=["st6_%d" % i2], writes=["mv%d" % i2])
                                    S.op("act", lambda e: e.activation(out=rs[i2][:], in_=mv[i2][:, :, 1], func=AF.Sqrt, bias=epsc[:, 0:1]),
                                         reads=["mv%d" % i2, "epsc"], writes=["rs%d" % i2])
                                    S.op("dve", lambda e: e.reciprocal(out=rs[i2][:], in_=rs[i2][:]), reads=["rs%d" % i2], writes=["rs%d" % i2])
                                    for hh in range(4):
                                        S.op("dve", lambda e: e.tensor_scalar(out=ya[i2][:, hh * 64:(hh + 1) * 64], in0=Hs[:, ch, hh * 64:(hh + 1) * 64],
                                                                              scalar1=mv[i2][:, hh, 0:1], scalar2=rs[i2][:, hh:hh + 1], op0=ALU.subtract, op1=ALU.mult),
                                             reads=hkeys + ["mv%d" % i2, "rs%d" % i2], writes=["ya%d" % i2])
                                    S.op("dve", lambda e: e.tensor_tensor(out=ya[i2][:], in0=ya[i2][:], in1=normw, op=ALU.mult), reads=["ya%d" % i2, "bcr"], writes=["ya%d" % i2])
                                    S.op("dve", lambda e: e.tensor_tensor(out=ya[i2][:], in0=ya[i2][:], in1=sigo[:, ch, :], op=ALU.mult), reads=["ya%d" % i2, "tm%d" % ch], writes=["ya%d" % i2])
                                    pi = 6 + i2
                                    for j in range(2):
                                        S.op("pe", lambda e: e.transpose(out=ps[pi][:, j * 128:(j + 1) * 128], in_=ya[i2][:, j * 128:(j + 1) * 128], identity=ident),
                                             reads=["ya%d" % i2, "consts"], writes=[PK[pi]], inc=(j == 1))
                                    S.op("act", lambda e: e.activation(out=yad[:, 0:2, ch * 128:(ch + 1) * 128], in_=ps[pi][:, 0:256].rearrange("p (a b) -> p a b", b=128), func=AF.Identity),
                                         reads=[PK[pi]], writes=["yad_a%d" % ch])
                                S.barrier()
                        PM.close()

                        with ExitStack() as P3:
                            s5C = tile(P3, "s5C", [128, 2, 256])
                            s5Cb = tile(P3, "s5Cb", [128, 2, 256], BF16)
                            pw = tile(P3, "pw", [128, 2, 17, 16])
                            pw2 = tile(P3, "pw2", [128, 2, 9, 16])
                            npw2 = tile(P3, "npw2", [128, 9, 16])
                            gluw = tile(P3, "gluw", [128, 2, 256], BF16)
                            WEb = tile(P3, "WEb", [128, 16, 2, 512], BF16)
                            Gb = tile(P3, "Gb", [128, 2, 2, 16, 128], BF16)
                            G0 = tile(P3, "G0", [128, 2, 128], BF16)
                            S.dma("sp", s5C[:].rearrange("p a (b c) -> p a b c", c=32), s5C_d[l], writes=["s5C"])
                            S.dma("pool", gluw[:], glu_d[l], writes=["gluw"])
                            S.op("dve", lambda e: e.tensor_copy(out=s5Cb[:, 0, :], in_=s5C[:, 0, :]), reads=["s5C"], writes=["s5Cb"])
                            S.op("dve", lambda e: e.tensor_scalar_mul(out=s5Cb[:, 1, :], in0=s5C[:, 1, :], scalar1=-1.0), reads=["s5C", "s5Cb"], writes=["s5Cb"])
                            if b == 0:
                                PTA = ExitStack()
                                s5A = tile(PTA, "s5A", [128, 3, 512])
                                s5B = tile(PTA, "s5B", [128, 2, 256])
                                s5Bn = tile(PTA, "s5Bn", [128, 2, 256])
                                s5p = tile(PTA, "s5p", [128, 3, 16])
                                tp = [tile(PTA, "tp%d" % i, [128, 16]) for i in range(10)]
                                sT = tile(PTA, "sT", [128, 2, 16, 16])
                                Wnb = [tile(PTA, "Wnb%d" % i, [128, 2, 512], BF16) for i in range(2)]
                                tA = [tile(PTA, "tA%d" % i, [128, 512]) for i in range(8)]
                                S.dma("sp", s5A[:].rearrange("p a (b c) -> p a b c", c=128), s5A_d[l].rearrange("p a d c n -> p a (d c) n"), writes=["s5A"])
                                S.dma("sp", s5B[:].rearrange("p a (b c) -> p a b c", c=128), s5B_d[l], writes=["s5B"])
                                S.dma("sp", s5Bn[:].rearrange("p a (b c) -> p a b c", c=32), s5Bn_d[l], writes=["s5Bn"])
                                S.dma("sp", s5p[:], s5pp_d[l], writes=["s5p"])

                                def tt(out, in0, in1, op, reads, writes, en="dve"):
                                    S.op(en, lambda e: e.tensor_tensor(out=out, in0=in0, in1=in1, op=op), reads=reads, writes=writes)

                                def cplx_abar(are, aim, ldt, t, width, key, tkeys):
                                    w = width
                                    dt_, lr, li, mag, ph, kk, ar, ai = [x[:, 0:w] for x in t[0:8]]
                                    S.op("act", lambda e: e.activation(out=dt_, in_=ldt, func=AF.Exp), reads=[key], writes=[tkeys[0]])
                                    tt(lr, are, dt_, ALU.mult, [key, tkeys[0]], [tkeys[1]])
                                    tt(li, aim, dt_, ALU.mult, [key, tkeys[0]], [tkeys[2]])
                                    S.op("act", lambda e: e.activation(out=mag, in_=lr, func=AF.Exp), reads=[tkeys[1]], writes=[tkeys[3]])
                                    for (shift, dst, dk) in ((0.0, ai, tkeys[7]), (PI / 2, ar, tkeys[6])):
                                        S.op("dve", lambda e: e.tensor_scalar_add(out=ph, in0=li, scalar1=shift), reads=[tkeys[2]], writes=[tkeys[4]])
                                        S.op("dve", lambda e: e.tensor_copy(out=dst, in_=ph), reads=[tkeys[4]], writes=[dk])
                                        for m in range(6):
                                            thr = (2 * m + 1) * PI
                                            S.op("dve", lambda e: e.tensor_scalar(out=kk, in0=ph, scalar1=thr, scalar2=-2 * PI, op0=ALU.is_gt, op1=ALU.mult),
                                                 reads=[tkeys[4]], writes=[tkeys[5]])
                                            tt(dst, dst, kk, ALU.add, [tkeys[5], dk], [dk])
                                        S.op("act", lambda e: e.activation(out=dst, in_=dst, func=AF.Sin), reads=[dk], writes=[dk])
                                        tt(dst, dst, mag, ALU.mult, [dk, tkeys[3]], [dk])
                                    return ar, ai

                                def kappa(kr, ki, den, arm1, ar, ai, lr, li, t0, keys):
                                    K = keys
                                    tt(den, lr, lr, ALU.mult, [K["lam"]], [K["den"]])
                                    tt(t0, li, li, ALU.mult, [K["lam"]], [K["t0"]])
                                    tt(den, den, t0, ALU.add, [K["den"], K["t0"]], [K["den"]])
                                    S.op("dve", lambda e: e.reciprocal(out=den, in_=den), reads=[K["den"]], writes=[K["den"]])
                                    S.op("dve", lambda e: e.tensor_scalar_add(out=arm1, in0=ar, scalar1=-1.0), reads=[K["ar"]], writes=[K["arm1"]])
                                    tt(kr, arm1, lr, ALU.mult, [K["arm1"], K["lam"]], [K["kr"]])
                                    tt(t0, ai, li, ALU.mult, [K["ai"], K["lam"]], [K["t0"]])
                                    tt(kr, kr, t0, ALU.add, [K["kr"], K["t0"]], [K["kr"]])
                                    tt(kr, kr, den, ALU.mult, [K["kr"], K["den"]], [K["kr"]])
                                    tt(ki, ai, lr, ALU.mult, [K["ai"], K["lam"]], [K["ki"]])
                                    tt(t0, arm1, li, ALU.mult, [K["arm1"], K["lam"]], [K["t0"]])
                                    tt(ki, ki, t0, ALU.subtract, [K["ki"], K["t0"]], [K["ki"]])
                                    tt(ki, ki, den, ALU.mult, [K["ki"], K["den"]], [K["ki"]])

                                tAk = ["tA%d" % i for i in range(8)]
                                tpk = ["tp%d" % i for i in range(10)]
                                ar, ai = cplx_abar(s5A[:, 0, :], s5A[:, 1, :], s5A[:, 2, :], tA, 512, "s5A", tAk)
                                den, kr, ki, t0, t1, arm1 = tA[0][:, :], tA[1][:, :], tA[2][:, :], tA[3][:, :], tA[4][:, :], tA[5][:, :]
                                kappa(kr, ki, den, arm1, ar, ai, s5A[:, 0, :], s5A[:, 1, :], t0,
                                      dict(lam="s5A", den=tAk[0], kr=tAk[1], ki=tAk[2], t0=tAk[3], arm1=tAk[5], ar=tAk[6], ai=tAk[7]))
                                v3 = lambda x: x.rearrange("p (a b) -> p a b", b=256)
                                Bre = s5B[:, 0, :].unsqueeze(1).to_broadcast([128, 2, 256])
                                Bim = s5B[:, 1, :].unsqueeze(1).to_broadcast([128, 2, 256])
                                Wr, Wi, Wr2, Wi2 = tA[5][:, :], tA[0][:, :], tA[1][:, :], tA[2][:, :]
                                Wrk, Wik, Wr2k, Wi2k = tAk[5], tAk[0], tAk[1], tAk[2]
                                tt(v3(t0), v3(kr), Bre, ALU.mult, [tAk[1], "s5B"], [tAk[3]])
                                tt(v3(t1), v3(ki), Bim, ALU.mult, [tAk[2], "s5B"], [tAk[4]])
                                tt(Wr, t0, t1, ALU.subtract, [tAk[3], tAk[4], tAk[5]], [Wrk])
                                tt(v3(t0), v3(kr), Bim, ALU.mult, [tAk[1], "s5B"], [tAk[3]])
                                tt(v3(t1), v3(ki), Bre, ALU.mult, [tAk[2], "s5B"], [tAk[4]])
                                tt(Wi, t0, t1, ALU.add, [tAk[3], tAk[4], tAk[0]], [Wik])
                                for e_ in range(16):
                                    S.op("act", lambda e: e.activation(out=WEb[:, e_, 0, :], in_=Wr, func=AF.Identity), reads=[Wrk], writes=["WEb"])
                                    S.op("act", lambda e: e.activation(out=WEb[:, e_, 1, :], in_=Wi, func=AF.Identity), reads=[Wik], writes=["WEb"])
                                    if e_ == 15:
                                        break
                                    tt(t0, Wr, ar, ALU.mult, [Wrk, tAk[6]], [tAk[3]])
                                    tt(t1, Wi, ai, ALU.mult, [Wik, tAk[7]], [tAk[4]])
                                    tt(Wr2, t0, t1, ALU.subtract, [tAk[3], tAk[4], Wr2k], [Wr2k])
                                    tt(t0, Wr, ai, ALU.mult, [Wrk, tAk[7]], [tAk[3]])
                                    tt(t1, Wi, ar, ALU.mult, [Wik, tAk[6]], [tAk[4]])
                                    tt(Wi2, t0, t1, ALU.add, [tAk[3], tAk[4], Wi2k], [Wi2k])
                                    Wr, Wi, Wr2, Wi2 = Wr2, Wi2, Wr, Wi
                                    Wrk, Wik, Wr2k, Wi2k = Wr2k, Wi2k, Wrk, Wik
                                par, pai = cplx_abar(s5p[:, 0, :], s5p[:, 1, :], s5p[:, 2, :], tp, 16, "s5p", tpk)
                                S.op("dve", lambda e: e.memset(pw[:, 0, 0, :], 1.0), writes=["pw"])
                                S.op("dve", lambda e: e.memset(pw[:, 1, 0, :], 0.0), reads=["pw"], writes=["pw"])
                                S.op("dve", lambda e: e.tensor_copy(out=pw[:, 0, 1, :], in_=par), reads=[tpk[6], "pw"], writes=["pw"])
                                S.op("dve", lambda e: e.tensor_copy(out=pw[:, 1, 1, :], in_=pai), reads=[tpk[7], "pw"], writes=["pw"])
                                kappa(tp[1][:, :], tp[2][:, :], tp[0][:, :], tp[5][:, :], par, pai, s5p[:, 0, :], s5p[:, 1, :], tp[3][:, :],
                                      dict(lam="s5p", den=tpk[0], kr=tpk[1], ki=tpk[2], t0=tpk[3], arm1=tpk[5], ar=tpk[6], ai=tpk[7]))
                                S.op("dve", lambda e: e.tensor_copy(out=sT[:, 0, 0, :], in_=tp[1][:, :]), reads=[tpk[1]], writes=["sT"])
                                S.op("dve", lambda e: e.tensor_copy(out=sT[:, 1, 0, :], in_=tp[2][:, :]), reads=[tpk[2], "sT"], writes=["sT"])

                                def cmul(dr, di, xr, xi, yr, yi, rk, wk):
                                    u0, u1, u2, u3 = tp[3][:, :], tp[4][:, :], tp[8][:, :], tp[9][:, :]
                                    tt(u0, xr, yr, ALU.mult, rk, [tpk[3]])
                                    tt(u1, xi, yi, ALU.mult, rk, [tpk[4]])
                                    tt(u2, xr, yi, ALU.mult, rk, [tpk[8]])
                                    tt(u3, xi, yr, ALU.mult, rk, [tpk[9]])
                                    tt(dr, u0, u1, ALU.subtract, [tpk[3], tpk[4]] + wk, wk)
                                    tt(di, u2, u3, ALU.add, [tpk[8], tpk[9]] + wk, wk)

                                for k in range(2, 17):
                                    cmul(pw[:, 0, k, :], pw[:, 1, k, :], pw[:, 0, k - 1, :], pw[:, 1, k - 1, :], pw[:, 0, 1, :], pw[:, 1, 1, :], ["pw"], ["pw"])
                                S.op("dve", lambda e: e.tensor_copy(out=pw2[:, :, 0, :], in_=pw[:, :, 16, :]), reads=["pw"], writes=["pw2"])
                                for m in range(1, 9):
                                    cmul(pw2[:, 0, m, :], pw2[:, 1, m, :], pw2[:, 0, m - 1, :], pw2[:, 1, m - 1, :], pw2[:, 0, m - 1, :], pw2[:, 1, m - 1, :], ["pw2"], ["pw2"])
                                S.op("dve", lambda e: e.tensor_scalar_mul(out=npw2[:], in0=pw2[:, 1, :, :], scalar1=-1.0), reads=["pw2"], writes=["npw2"])
                                for tau in range(1, 16):
                                    cmul(sT[:, 0, tau, :], sT[:, 1, tau, :], pw[:, 0, tau, :], pw[:, 1, tau, :], sT[:, 0, 0, :], sT[:, 1, 0, :], ["pw", "sT"], ["sT"])
                                S.op("dve", lambda e: e.memset(Gb[:], 0.0), writes=["Gb"])
                                v4 = lambda x: x.rearrange("p (a b c) -> p a b c", a=2, b=8)
                                Bnr = s5Bn[:, 0, :].rearrange("p (b c) -> p b c", c=32).unsqueeze(1).to_broadcast([128, 2, 8, 32])
                                Bni = s5Bn[:, 1, :].rearrange("p (b c) -> p b c", c=32).unsqueeze(1).to_broadcast([128, 2, 8, 32])
                                for tau in range(16):
                                    wn = Wnb[tau % 2]
                                    wnk = "Wnb%d" % (tau % 2)
                                    sr = sT[:, 0, tau, :].rearrange("p (a b) -> p a b", b=8).unsqueeze(3).to_broadcast([128, 2, 8, 32])
                                    si = sT[:, 1, tau, :].rearrange("p (a b) -> p a b", b=8).unsqueeze(3).to_broadcast([128, 2, 8, 32])
                                    tt(v4(t0), sr, Bnr, ALU.mult, ["sT", "s5Bn"], [tAk[3]])
                                    tt(v4(t1), si, Bni, ALU.mult, ["sT", "s5Bn"], [tAk[4]])
                                    tt(wn[:, 0, :], t0, t1, ALU.subtract, [tAk[3], tAk[4]], [wnk])
                                    tt(v4(t0), sr, Bni, ALU.mult, ["sT", "s5Bn"], [tAk[3]])
                                    tt(v4(t1), si, Bnr, ALU.mult, ["sT", "s5Bn"], [tAk[4]])
                                    tt(wn[:, 1, :], t0, t1, ALU.add, [tAk[3], tAk[4], wnk], [wnk])
                                    pi = tau % 2
                                    for d in range(2):
                                        for pair in range(8):
                                            chn, ppi = pair // 4, pair % 4
                                            o = ps[pi][ppi * 32:(ppi + 1) * 32, (chn * 2 + d) * 32:(chn * 2 + d + 1) * 32]
                                            for c in range(2):
                                                S.op("pe", lambda e: e.matmul(o, lhsT=wn[:, c, (d * 8 + pair) * 32:(d * 8 + pair + 1) * 32], rhs=s5Cb[:, c, pair * 32:(pair + 1) * 32],
                                                                              start=(c == 0), stop=(c == 1), tile_position=(0, ppi * 32)),
                                                     reads=[wnk, "s5Cb"], writes=[PK[pi]], inc=(c == 1 and d == 1 and pair == 7))
                                    for pb in range(4):
                                        S.op("act", lambda e: e.activation(out=Gb[pb * 32:(pb + 1) * 32, :, :, tau, pb * 32:(pb + 1) * 32],
                                                                           in_=ps[pi][pb * 32:(pb + 1) * 32, 0:128].rearrange("p (a b c) -> p a b c", a=2, b=2), func=AF.Identity),
                                             reads=[PK[pi]], writes=["Gb"])
                                tt(G0[:], Gb[:, :, 0, 0, :], Gb[:, :, 1, 0, :], ALU.add, ["Gb"], ["G0"])
                                S.barrier()
                                PTA.close()
                                fl = lambda x: x
                                S.dma("sp", cWE[l], WEb[:].rearrange("p a b c -> p (a b c)"), reads=["WEb"])
                                S.dma("sp", cG[l], Gb[:].rearrange("p a b c d -> p (a b c d)"), reads=["Gb"])
                                S.dma("sp", cG0[l], G0[:].rearrange("p a b -> p (a b)"), reads=["G0"])
                                S.dma("sp", cPW[l], pw[:].rearrange("p a b c -> p (a b c)"), reads=["pw"])
                                S.dma("sp", cPW2[l], pw2[:].rearrange("p a b c -> p (a b c)"), reads=["pw2"])
                                S.dma("sp", cNPW2[l], npw2[:].rearrange("p a b -> p (a b)"), reads=["npw2"])
                            else:
                                S.dma("sp", WEb[:].rearrange("p a b c -> p (a b c)"), cWE[l], writes=["WEb"])
                                S.dma("sp", Gb[:].rearrange("p a b c d -> p (a b c d)"), cG[l], writes=["Gb"])
                                S.dma("sp", G0[:].rearrange("p a b -> p (a b)"), cG0[l], writes=["G0"])
                                S.dma("sp", pw[:].rearrange("p a b c -> p (a b c)"), cPW[l], writes=["pw"])
                                S.dma("sp", pw2[:].rearrange("p a b c -> p (a b c)"), cPW2[l], writes=["pw2"])
                                S.dma("sp", npw2[:].rearrange("p a b -> p (a b)"), cNPW2[l], writes=["npw2"])
                                S.barrier()
                            EA = [[[tile(P3, "EA%d%d%d" % (par, d, c), [128, SJ + 1]) for c in range(2)] for d in range(2)] for par in range(2)]
                            EB = [[[tile(P3, "EB%d%d%d" % (par, d, c), [128, SJ + 1]) for c in range(2)] for d in range(2)] for par in range(2)]
                            CWt = [tile(P3, "CW%d" % i, [128, 17, 2, 2, 32], BF16) for i in range(2)]
                            cu = [tile(P3, "cu%d" % i, [128, 16, 32]) for i in range(2)]
                            ypre = tile(P3, "ypre", [128, 2, T], BF16)
                            en = "dve"

                            def stt(out, in0, scal, in1, reads, writes):
                                S.op(en, lambda e: e.scalar_tensor_tensor(out=out, in0=in0, scalar=scal, in1=in1, op0=ALU.mult, op1=ALU.add), reads=reads, writes=writes)

                            uk = ["s5u%d" % i for i in range(NBLK)]
                            W = SJ + 1

                            def s5_head(pair):
                                chn, ppi, par = pair // 4, pair % 4, pair % 2
                                rows = slice(ppi * 32, ppi * 32 + 32)
                                cw = CWt[par]
                                cwk = "CW%d" % par
                                DD = []
                                for d in range(2):
                                    DD.append(dict(d=d, sc=d * 8 + pair, ea=EA[par][d], eb=EB[par][d],
                                                   eak=["EA%d%d0" % (par, d), "EA%d%d1" % (par, d)], ebk=["EB%d%d0" % (par, d), "EB%d%d1" % (par, d)],
                                                   lo=(1 if d == 0 else 0), zc=(0 if d == 0 else SJ)))
                                pend = []
                                Cr = s5C[:, 0, pair * 32:(pair + 1) * 32].unsqueeze(1).to_broadcast([128, 16, 32])
                                Ci = s5C[:, 1, pair * 32:(pair + 1) * 32].unsqueeze(1).to_broadcast([128, 16, 32])
                                for d in range(2):
                                    sc = d * 8 + pair
                                    pr = pw[:, 0, 1:17, sc:sc + 1].to_broadcast([128, 16, 32])
                                    pim = pw[:, 1, 1:17, sc:sc + 1].to_broadcast([128, 16, 32])
                                    pend.append(lambda pr=pr: tt(cu[0][:], Cr, pr, ALU.mult, ["s5C", "pw"], ["cu0"]))
                                    pend.append(lambda pim=pim: tt(cu[1][:], Ci, pim, ALU.mult, ["s5C", "pw"], ["cu1"]))
                                    pend.append(lambda d=d: tt(cw[:, 1:17, d, 0, :], cu[0][:], cu[1][:], ALU.subtract, ["cu0", "cu1"], [cwk]))
                                    pend.append(lambda pim=pim: tt(cu[0][:], Cr, pim, ALU.mult, ["s5C", "pw"], ["cu0"]))
                                    pend.append(lambda pr=pr: tt(cu[1][:], Ci, pr, ALU.mult, ["s5C", "pw"], ["cu1"]))
                                    pend.append(lambda d=d: S.op("dve", lambda e: e.scalar_tensor_tensor(out=cw[:, 1:17, d, 1, :], in0=cu[0][:], scalar=-1.0, in1=cu[1][:], op0=ALU.mult, op1=ALU.subtract),
                                                                 reads=["cu0", "cu1", cwk], writes=[cwk]))
                                for q in DD:
                                    d = q["d"]
                                    for c in range(2):
                                        pi = d * 2 + c
                                        for i in range(SF):
                                            lag = SF - 1 - i if d == 0 else i
                                            S.op("pe", lambda e: e.matmul(ps[pi][:, 0:SJ], lhsT=WEb[rows, lag, c, (d * 2 + chn) * 128:(d * 2 + chn + 1) * 128],
                                                                          rhs=s5u[rows, chn, i, :], start=(i == 0), stop=(i == SF - 1), tile_position=(ppi * 32, 0)),
                                                 reads=["WEb"] + uk, writes=[PK[pi]], inc=(i == SF - 1))
                                        zc = q["zc"]
                                        S.op("act", lambda e: e.activation(out=q["ea"][c][:, zc:zc + 1], in_=zcol[:, 0:1], func=AF.Identity), reads=["zcol"], writes=[q["eak"][c]])
                                        S.op("act", lambda e: e.activation(out=q["eb"][c][:, zc:zc + 1], in_=zcol[:, 0:1], func=AF.Identity), reads=["zcol"], writes=[q["ebk"][c]])
                                        if d == 0:
                                            S.op("act", lambda e: e.activation(out=q["ea"][c][:, 1:SJ + 1], in_=ps[pi][:, 0:SJ], func=AF.Identity), reads=[PK[pi]], writes=[q["eak"][c]])
                                        else:
                                            S.op("act", lambda e: e.activation(out=q["ea"][c][:, 0:128], in_=ps[pi][:, 16:SJ], func=AF.Identity), reads=[PK[pi]], writes=[q["eak"][c]])
                                            S.op("act", lambda e: e.activation(out=q["ea"][c][:, 128:SJ], in_=ps[pi][:, 0:16], func=AF.Identity), reads=[PK[pi]], writes=[q["eak"][c]])
                                    q["cur"], q["nxt"], q["curk"], q["nxtk"] = q["ea"], q["eb"], q["eak"], q["ebk"]
                                for m in range(8):
                                    dd = 1 << m
                                    for phase in range(3):
                                        for q in DD:
                                            d, sc = q["d"], q["sc"]
                                            cur, nxt, curk, nxtk = q["cur"], q["nxt"], q["curk"], q["nxtk"]
                                            p_r, p_i, np_i = pw2[:, 0, m, sc:sc + 1], pw2[:, 1, m, sc:sc + 1], npw2[:, m, sc:sc + 1]
                                            if d == 0:
                                                o_s, i_s, k_s = slice(dd, W), slice(0, W - dd), slice(0, dd)
                                            else:
                                                o_s, i_s, k_s = slice(0, W - dd), slice(dd, W), slice(W - dd, W)
                                            if phase == 0:
                                                pend.append(lambda nxt=nxt, cur=cur, o_s=o_s, i_s=i_s, p_r=p_r, curk=curk, nxtk=nxtk: stt(nxt[0][:, o_s], cur[0][:, i_s], p_r, cur[0][:, o_s], curk + ["pw2"], [nxtk[0]]))
                                                pend.append(lambda nxt=nxt, cur=cur, o_s=o_s, i_s=i_s, p_r=p_r, curk=curk, nxtk=nxtk: stt(nxt[1][:, o_s], cur[1][:, i_s], p_r, cur[1][:, o_s], curk + ["pw2"], [nxtk[1]]))
                                            elif phase == 1:
                                                pend.append(lambda nxt=nxt, cur=cur, o_s=o_s, i_s=i_s, np_i=np_i, curk=curk, nxtk=nxtk: stt(nxt[0][:, o_s], cur[1][:, i_s], np_i, nxt[0][:, o_s], curk + ["npw2", nxtk[0]], [nxtk[0]]))
                                                pend.append(lambda nxt=nxt, cur=cur, o_s=o_s, i_s=i_s, p_i=p_i, curk=curk, nxtk=nxtk: stt(nxt[1][:, o_s], cur[0][:, i_s], p_i, nxt[1][:, o_s], curk + ["pw2", nxtk[1]], [nxtk[1]]))
                                            else:
                                                for c in range(2):
                                                    pend.append(lambda nxt=nxt, cur=cur, k_s=k_s, c=c, curk=curk, nxtk=nxtk: S.op(en, lambda e: e.tensor_copy(out=nxt[c][:, k_s], in_=cur[c][:, k_s]), reads=[curk[c]], writes=[nxtk[c]]))
                                    for q in DD:
                                        q["cur"], q["nxt"], q["curk"], q["nxtk"] = q["nxt"], q["cur"], q["nxtk"], q["curk"]
                                Ef = []
                                for q in DD:
                                    for c in range(2):
                                        pend.append(lambda q=q, c=c, nxt=q["nxt"], cur=q["cur"], curk=q["curk"], nxtk=q["nxtk"]:
                                                    S.op("act", lambda e: e.activation(out=nxt[c][:, :].bitcast(BF16)[:, 0:W], in_=cur[c][:, :], func=AF.Identity),
                                                         reads=[curk[c]], writes=[nxtk[c]]))
                                    Ef.append(([q["nxt"][c][:, :].bitcast(BF16) for c in range(2)], list(q["nxtk"])))
                                return dict(pair=pair, chn=chn, ppi=ppi, rows=rows, cw=cw, cwk=cwk, Ef=Ef), pend

                            def s5_intra(chn, pend):
                                npend = (len(pend) + SF - 1) // SF if pend else 0
                                for i in range(SF):
                                    pi = 4 + i % 4
                                    o = ps[pi][:, 0:SJ]
                                    for i2 in range(SF):
                                        if i2 < i:
                                            lt = Gb[:, chn, 0, i - i2, :]
                                        elif i2 > i:
                                            lt = Gb[:, chn, 1, i2 - i, :]
                                        else:
                                            lt = G0[:, chn, :]
                                        S.op("pe", lambda e: e.matmul(o, lhsT=lt, rhs=s5u[:, chn, i2, :], start=(i2 == 0), stop=(i2 == SF - 1)),
                                             reads=["Gb", "G0"] + uk, writes=[PK[pi]], inc=(i2 == SF - 1))
                                    S.op("dve", lambda e: e.scalar_tensor_tensor(out=ypre[:, chn, i::SF], in0=s5u[:, chn, i, :], scalar=s5d[:, chn:chn + 1], in1=o,
                                                                                 op0=ALU.mult, op1=ALU.add), reads=[PK[pi], "pp"] + uk, writes=["ypI%d" % chn])
                                    for _ in range(npend):
                                        if pend:
                                            pend.pop(0)()

                            def s5_tail(ctx, pend):
                                pair, chn, ppi, rows, cw, cwk, Ef = ctx["pair"], ctx["chn"], ctx["ppi"], ctx["rows"], ctx["cw"], ctx["cwk"], ctx["Ef"]
                                npend = (len(pend) + SF - 1) // SF
                                for i in range(SF):
                                    pi = 4 + i % 4
                                    o = ps[pi][rows, 0:SJ]
                                    (ef, efk), (eb_, ebk_) = Ef
                                    for c in range(2):
                                        S.op("pe", lambda e: e.matmul(o, lhsT=cw[:, i + 1, 0, c, :], rhs=ef[c][:, 0:SJ], start=(c == 0), stop=False, tile_position=(0, ppi * 32)),
                                             reads=[cwk] + efk, writes=[PK[pi]], inc=False)
                                    for c in range(2):
                                        S.op("pe", lambda e: e.matmul(ps[pi][rows, 0:16], lhsT=cw[:, SF - i, 1, c, :], rhs=eb_[c][:, 129:145], start=False, stop=False, tile_position=(0, ppi * 32)),
                                             reads=[cwk] + ebk_, writes=[PK[pi]], inc=False)
                                        S.op("pe", lambda e: e.matmul(ps[pi][rows, 16:SJ], lhsT=cw[:, SF - i, 1, c, :], rhs=eb_[c][:, 1:129], start=False, stop=(c == 1), tile_position=(0, ppi * 32)),
                                             reads=[cwk] + ebk_, writes=[PK[pi]], inc=(c == 1))
                                    S.op("dve", lambda e: e.tensor_tensor(out=ypre[rows, chn, i::SF], in0=ypre[rows, chn, i::SF], in1=o, op=ALU.add),
                                         reads=[PK[pi], "ypI%d" % chn, "ypre%d" % pair], writes=["ypre%d" % pair])
                                    for _ in range(npend):
                                        if pend:
                                            pend.pop(0)()
                                while pend:
                                    pend.pop(0)()

                            zcol = tile(P3, "zcol", [128, 1])
                            S.op("dve", lambda e: e.memset(zcol[:], 0.0), writes=["zcol"])
                            ctx, pend0 = s5_head(0)
                            s5_intra(0, pend0)
                            s5_intra(1, pend0)
                            while pend0:
                                pend0.pop(0)()
                            for pair in range(8):
                                if pair < 7:
                                    nctx, npnd = s5_head(pair + 1)
                                else:
                                    nctx, npnd = None, []
                                s5_tail(ctx, npnd)
                                ctx = nctx
                            yg = [ypre[:, 0, :], ypre[:, 1, :]]
                            ypk = ["ypre%d" % p for p in range(8)] + ["ypI0", "ypI1"]
                            for c in range(2):
                                for s0 in range(0, T, 768):
                                    S.op("act", lambda e: e.activation(out=yg[c][:, s0:s0 + 768], in_=ypre[:, c, s0:s0 + 768], func=AF.Gelu_apprx_tanh), reads=ypk, writes=["yg"] + ypk)
                            sgt = [tile(P3, "sgt%d" % i, [128, 512]) for i in range(2)]
                            it = 0
                            for s0 in list(range(0, 2048, 512)) + [2048]:
                                n = min(512, T - s0)
                                for oc in range(2):
                                    pi = it % 2
                                    sg = sgt[it % 2]
                                    sgk = "sgt%d" % (it % 2)
                                    it += 1
                                    for kc in range(2):
                                        S.op("pe", lambda e: e.matmul(ps[pi][:, 0:n], lhsT=gluw[:, kc, oc * 128:(oc + 1) * 128], rhs=yg[kc][:, s0:s0 + n], start=(kc == 0), stop=(kc == 1)),
                                             reads=["gluw", "yg"], writes=[PK[pi]], inc=(kc == 1))
                                    S.op("act", lambda e: e.activation(out=sg[:, 0:n], in_=ps[pi][:, 0:n], func=AF.Sigmoid, bias=glub[:, oc:oc + 1]), reads=[PK[pi], "pp"], writes=[sgk])
                                    S.op("dve", lambda e: e.tensor_tensor(out=yad[:, 2 + oc, s0:s0 + n], in0=yg[oc][:, s0:s0 + n], in1=sg[:, 0:n], op=ALU.mult), reads=["yg", sgk], writes=["yad_d"])
                            S.barrier()

                    with ExitStack() as P4:
                        wB = tile(P4, "wB", [128, 8, 1024], BF16)
                        wo = tile(P4, "wo", [128, 8, 1024], BF16)
                        swT = tile(P4, "swT", [128, 4, 128], BF16)
                        sbias = tile(P4, "sbias", [128, 2, 128])
                        hT = [tile(P4, "h3T%d" % i, [128, 8, NB], BF16) for i in range(1)]
                        ybc = [tile(P4, "ybc%d" % i, [128, 4, NB], BF16) for i in range(2)]
                        uT = tile(P4, "uT", [128, 2, NB], BF16)
                        gv2 = [tile(P4, "gv%d" % i, [128, 256]) for i in range(2)]
                        vtok2 = [tile(P4, "vtok%d" % i, [128, 256], BF16) for i in range(2)]
                        st6_2 = [tile(P4, "s_st6%d" % i, [128, 6]) for i in range(2)]
                        mv_2 = [tile(P4, "s_mv%d" % i, [128, 2]) for i in range(2)]
                        rs_2 = [tile(P4, "s_rs%d" % i, [128, 1]) for i in range(2)]
                        sgtmp = tile(P4, "sgtmp", [128, NB])
                        sgtmp2 = tile(P4, "sgtmp2", [128, 128])
                        padl = tile(P4, "padl", [128, 2, 4, 94])
                        padc = tile(P4, "padc", [128, 2, 286])
                        cacc = tile(P4, "cacc", [128, 2, NB])
                        sq8 = tile(P4, "sq8", [128, 8, NB])
                        st = {"mean": tile(P4, "mean", [128, NB]), "m2": tile(P4, "m2", [128, NB]), "rstd": tile(P4, "rstd", [128, NB])}
                        tnorm = cacc
                        tmp8 = tile(P4, "tmp8", [128, NB])
                        h2T = tile(P4, "h2T", [128, 8, NB], BF16)
                        actT = tile(P4, "actT", [128, NJ, NB], BF16)
                        sgf = [tile(P4, "sgf%d" % i, [128, NB], BF16) for i in range(2)]
                        w1p = [tile(P4, "w1p%d" % i, [128, 8, 256], BF16) for i in range(3)]
                        w2p = [tile(P4, "w2p%d" % i, [128, NJ, 128], BF16) for i in range(2)]
                        A2 = tile(P4, "A2", [128, 8, 3])
                        B2 = tile(P4, "B2", [128, 8, 3])
                        S.op("dve", lambda e: e.tensor_tensor(out=A2[:], in0=ops2[:, l, :, :], in1=ln1w.unsqueeze(2).to_broadcast([128, 8, 3]), op=ALU.mult), reads=["pp", "ops2"], writes=["A2B2"])
                        S.op("dve", lambda e: e.tensor_tensor(out=B2[:], in0=ops2[:, l, :, :], in1=ln1b.unsqueeze(2).to_broadcast([128, 8, 3]), op=ALU.mult), reads=["pp", "ops2", "A2B2"], writes=["A2B2"])
                        S.op("dve", lambda e: e.tensor_tensor(out=B2[:], in0=B2[:], in1=modT[:, l, 24:32, :], op=ALU.add), reads=["modT", "A2B2"], writes=["A2B2"])
                        S.dma("pool", wB[:], winB_d[l], writes=["wB"])
                        S.dma("pool", wo[:], wout_d[l], writes=["wo"])
                        S.dma("pool", swT[:], sguw_d[l], writes=["swT"])
                        S.dma("sp", sbias[:], sgub_d[l], writes=["sbias"])
                        S.op("dve", lambda e: e.memset(padl[:], 0.0), writes=["padl"])
                        S.op("dve", lambda e: e.memset(padc[:], 0.0), writes=["padc"])
                        w1i = 0
                        w2i = 0
                        blocks = list(range(1, NBLK)) if last else list(range(NBLK))
                        stc = {"mean": tile(P4, "meanc", [128, NB]), "m2": tile(P4, "m2c", [128, NB]), "rstd": tile(P4, "rstdc", [128, NB]),
                               "keys": ("meanc", "m2c", "rstdc")}
                        w1i = [0]
                        w2i = [0]
                        h = hT[0]
                        hk = "h3T0"

                        def make_h(blk):
                            c0 = blk * NB
                            mc = mcol(blk)
                            xk = "xT%d" % blk
                            for fc in range(8):
                                S.op("act", lambda e: e.activation(out=h[:, fc, :], in_=xT[:, fc, c0:c0 + NB], func=AF.Identity,
                                                                   scale=ops1[:, l, fc, mc:mc + 1], bias=modT[:, l, fc, mc:mc + 1]),
                                     reads=[xk, "ops1", "modT"], writes=[hk])

                        def front(blk):
                            c0 = blk * NB
                            mc = mcol(blk)
                            yb = ybc[blk % 2]
                            ybk = "ybc%d" % (blk % 2)
                            xk = "xT%d" % blk

                            def fmproj(wc, pi):
                                for kc in range(8):
                                    S.op("pe", lambda e: e.matmul(ps[pi][:, 0:NB], lhsT=wB[:, kc, wc:wc + 128], rhs=h[:, kc, :], start=(kc == 0), stop=(kc == 7)),
                                         reads=["wB", hk], writes=[PK[pi]], inc=(kc == 7))
                            for cc in range(2):
                                for kc in range(8):
                                    S.op("pe", lambda e: e.matmul(ps[2 + cc][:, 0:256], lhsT=h[:, kc, cc * 128:(cc + 1) * 128], rhs=wB[:, kc, 256:512], start=(kc == 0), stop=(kc == 7)),
                                         reads=["wB", hk], writes=[PK[2 + cc]], inc=(kc == 7))
                            for cc in range(2):
                                gvc, vk, gk_ = gv2[cc], "vtok%d" % cc, "gv%d" % cc
                                S.op("act", lambda e: e.activation(out=gvc[:], in_=ps[2 + cc][:, 0:256], func=AF.Gelu_apprx_tanh), reads=[PK[2 + cc]], writes=[gk_])
                            for cc in range(2):
                                gvc, vk, gk_ = gv2[cc], "vtok%d" % cc, "gv%d" % cc
                                st6c, mvc, rsc = st6_2[cc], mv_2[cc], rs_2[cc]
                                sk = "sst%d" % cc
                                S.op("dve", lambda e: e.bn_stats(out=st6c[:], in_=gvc[:]), reads=[gk_], writes=[sk])
                                S.op("dve", lambda e: e.bn_aggr(out=mvc[:], in_=st6c[:]), reads=[sk], writes=[sk])
                                S.op("act", lambda e: e.activation(out=rsc[:], in_=mvc[:, 1:2], func=AF.Sqrt, bias=epsc[:, 0:1]), reads=[sk, "epsc"], writes=[sk + "r"])
                                S.op("dve", lambda e: e.reciprocal(out=rsc[:], in_=rsc[:]), reads=[sk + "r"], writes=[sk + "r"])
                                S.op("dve", lambda e: e.tensor_scalar(out=gvc[:], in0=gvc[:], scalar1=mvc[:, 0:1], scalar2=rsc[:, 0:1], op0=ALU.subtract, op1=ALU.mult),
                                     reads=[gk_, sk, sk + "r"], writes=[gk_])
                                S.op("dve", lambda e: e.tensor_tensor(out=gvc[:], in0=gvc[:], in1=sgulnw, op=ALU.mult), reads=[gk_, "bcr"], writes=[gk_])
                                S.op("dve", lambda e: e.tensor_tensor(out=vtok2[cc][:], in0=gvc[:], in1=sgulnb, op=ALU.add), reads=[gk_, "bcr"], writes=[vk])
                            taps = []
                            for c in range(2):
                                fmproj(512 + c * 128, 4)
                                fmproj(768 + c * 128, 5)
                                S.op("act", lambda e: e.activation(out=sgtmp[:], in_=ps[5][:, 0:NB], func=AF.Sigmoid), reads=[PK[5]], writes=["sgtmp"])
                                if blk == 0:
                                    S.op("dve", lambda e: e.tensor_tensor(out=padc[:, c, 15:271], in0=ps[4][:, 0:NB], in1=sgtmp[:], op=ALU.mult), reads=[PK[4], "sgtmp"], writes=["padc"])
                                    srcs = [padc[:, c, k:k + 256] for k in range(31)]
                                    acc = cacc[:, c, :]
                                    pk = "padc"
                                else:
                                    S.op("dve", lambda e: e.tensor_tensor(out=padl[:, c, :, 15:79], in0=ps[4][:, 0:NB].rearrange("p (a b) -> p a b", b=64),
                                                                          in1=sgtmp[:].rearrange("p (a b) -> p a b", b=64), op=ALU.mult), reads=[PK[4], "sgtmp"], writes=["padl"])
                                    srcs = [padl[:, c, :, k:k + 64] for k in range(31)]
                                    acc = cacc[:, c, :].rearrange("p (a b) -> p a b", b=64)
                                    pk = "padl"

                                def mk(k, c=c, srcs=srcs, acc=acc, pk=pk):
                                    if k == 0:
                                        return lambda: S.op("dve", lambda e: e.tensor_scalar_mul(out=acc, in0=srcs[0], scalar1=convw[:, c, 0:1]), reads=[pk, "pp"], writes=["cacc%d" % c])
                                    if k == 31:
                                        return lambda: S.op("dve", lambda e: e.tensor_scalar_add(out=cacc[:, c, :], in0=cacc[:, c, :], scalar1=convb[:, c:c + 1]), reads=["cacc%d" % c, "pp"], writes=["cacc%d" % c])
                                    return lambda: S.op("dve", lambda e: e.scalar_tensor_tensor(out=acc, in0=srcs[k], scalar=convw[:, c, k:k + 1], in1=acc, op0=ALU.mult, op1=ALU.add),
                                                        reads=[pk, "pp", "cacc%d" % c], writes=["cacc%d" % c])
                                taps += [mk(k) for k in range(32)]
                            for c in range(2):
                                fmproj(c * 128, c)
                                S.op("act", lambda e: e.activation(out=uT[:, c, :], in_=ps[c][:, 0:NB], func=AF.Gelu_apprx_tanh), reads=[PK[c]], writes=["uT"])
                            for cc in range(2):
                                for pr in range(2):
                                    for g2 in range(2):
                                        g = pr * 2 + g2
                                        S.op("pe", lambda e: e.matmul(ps[2 + cc][g2 * 64:(g2 + 1) * 64, 0:128], lhsT=vtok2[cc][:, g * 64:(g + 1) * 64], rhs=swT[:, g, :], start=True, stop=True,
                                                                      tile_position=(0, g2 * 64)), reads=["vtok%d" % cc, "swT"], writes=[PK[2 + cc]], inc=(g2 == 1))
                                    S.op("dve", lambda e: e.tensor_tensor(out=sgtmp2[:, 0:128], in0=ps[2 + cc][:, 0:128], in1=sbias[:, pr, :], op=ALU.add), reads=[PK[2 + cc], "sbias"], writes=["sgtmp2"])
                                    S.op("dve", lambda e: e.tensor_tensor(out=yb[:, pr, cc * 128:(cc + 1) * 128], in0=sgtmp2[:, 0:128], in1=uT[:, pr, cc * 128:(cc + 1) * 128], op=ALU.mult),
                                         reads=["sgtmp2", "uT"], writes=[ybk + "s"])
                            return taps

                        def back(blk):
                            c0 = blk * NB
                            yb = ybc[blk % 2]
                            ybk = "ybc%d" % (blk % 2)
                            ln_stats(stc, lambda c: cacc[:, c, :], 2, avgC, sq8, "sq8", 6, 7, NB, ["cacc0", "cacc1"], "c")
                            S.op("dve", lambda e: e.tensor_tensor(out=tnorm[:], in0=cacc[:], in1=stc["mean"][:].unsqueeze(1).to_broadcast([128, 2, NB]), op=ALU.subtract),
                                 reads=["cacc0", "cacc1", "meanc"], writes=["cacc0", "cacc1"])
                            S.op("dve", lambda e: e.tensor_tensor(out=tnorm[:], in0=tnorm[:], in1=stc["rstd"][:].unsqueeze(1).to_broadcast([128, 2, NB]), op=ALU.mult),
                                 reads=["cacc0", "cacc1", "rstdc"], writes=["cacc0", "cacc1"])
                            for c in range(2):
                                S.op("act", lambda e: e.activation(out=yb[:, 2 + c, :], in_=tnorm[:, c, :], func=AF.Silu, scale=convlnw[:, c:c + 1], bias=convlnb[:, c:c + 1]),
                                     reads=["cacc0", "cacc1", "pp"], writes=[ybk + "c"])
                            if debug and b == 0 and l == 0:
                                S.dma("pool", dbg_mix[:, 2:6, c0:c0 + NB], yb[:], reads=[ybk + "s", ybk + "c"])

                        def ln_apply(blk, lw, lb, with_h2, nxt=None):
                            c0 = blk * NB
                            mc = mcol(blk)
                            xk = "xT%d" % blk
                            xv = xT[:, :, c0:c0 + NB]
                            ln_stats(st, lambda c: xT[:, c, c0:c0 + NB], 8, avgD, sq8, "sq8", 6, 7, NB, [xk], "r")
                            if nxt is not None:
                                make_h(nxt)
                            S.op("dve", lambda e: e.tensor_tensor(out=xv, in0=xv, in1=st["mean"][:].unsqueeze(1).to_broadcast([128, 8, NB]), op=ALU.subtract), reads=[xk, "mean"], writes=[xk])
                            S.op("dve", lambda e: e.tensor_tensor(out=xv, in0=xv, in1=st["rstd"][:].unsqueeze(1).to_broadcast([128, 8, NB]), op=ALU.mult), reads=[xk, "rstd"], writes=[xk])
                            if with_h2:
                                for fc in range(8):
                                    S.op("act", lambda e: e.activation(out=h2T[:, fc, :], in_=xT[:, fc, c0:c0 + NB], func=AF.Identity,
                                                                       scale=A2[:, fc, mc:mc + 1], bias=B2[:, fc, mc:mc + 1]),
                                         reads=[xk, "A2B2"], writes=["h2T"])
                            S.op("pool", lambda e: e.tensor_tensor(out=xv, in0=xv, in1=lw.unsqueeze(2).to_broadcast([128, 8, NB]), op=ALU.mult), reads=[xk, "pp"], writes=[xk])
                            S.op("pool", lambda e: e.tensor_tensor(out=xv, in0=xv, in1=lb.unsqueeze(2).to_broadcast([128, 8, NB]), op=ALU.add), reads=[xk, "pp"], writes=[xk])

                        def wout_ln1(blk, nxt=None):
                            c0 = blk * NB
                            mc = mcol(blk)
                            yb = ybc[blk % 2]
                            ybk = "ybc%d" % (blk % 2)
                            xk = "xT%d" % blk
                            mix = [yad[:, 0, c0:c0 + NB], yad[:, 1, c0:c0 + NB], yb[:, 0, :], yb[:, 1, :], yb[:, 2, :], yb[:, 3, :], yad[:, 2, c0:c0 + NB], yad[:, 3, c0:c0 + NB]]
                            mixk = ["yad_a%d" % (2 * blk), "yad_a%d" % (2 * blk + 1), "yad_d", ybk + "s", ybk + "c"]
                            for oc in range(8):
                                pi = oc % 4
                                for kc in range(8):
                                    S.op("pe", lambda e: e.matmul(ps[pi][:, 0:NB], lhsT=wo[:, kc, oc * 128:(oc + 1) * 128], rhs=mix[kc], start=(kc == 0), stop=(kc == 7)),
                                         reads=["wo"] + mixk, writes=[PK[pi]], inc=(kc == 7))
                                S.op("act", lambda e: e.activation(out=tmp8[:], in_=ps[pi][:, 0:NB], func=AF.Identity, scale=modT[:, l, 16 + oc, mc:mc + 1]), reads=[PK[pi], "modT"], writes=["tmp8"])
                                S.op("dve", lambda e: e.scalar_tensor_tensor(out=xT[:, oc, c0:c0 + NB], in0=xT[:, oc, c0:c0 + NB], scalar=ALPHA, in1=tmp8[:], op0=ALU.mult, op1=ALU.add),
                                     reads=[xk, "tmp8"], writes=[xk])
                            ln_apply(blk, ln1w, ln1b, True, nxt)

                        def ffn(blk, pending, after_in):
                            c0 = blk * NB
                            mc = mcol(blk)
                            xk = "xT%d" % blk
                            for j in range(NJ):
                                wt = w1p[w1i[0] % 3]
                                wk = "w1p%d" % (w1i[0] % 3)
                                w1i[0] += 1
                                S.dma("sp", wt[:].rearrange("p a b -> p (a b)"), w1s[l, j], reads=["w1s_%d_%d" % (l, j)], writes=[wk])
                                for half in range(2):
                                    pi = half * 2 + (j % 2)
                                    for kc in range(8):
                                        S.op("pe", lambda e: e.matmul(ps[pi][:, 0:NB], lhsT=wt[:, kc, half * 128:(half + 1) * 128], rhs=h2T[:, kc, :], start=(kc == 0), stop=(kc == 7)),
                                             reads=[wk, "h2T"], writes=[PK[pi]], inc=(kc == 7))
                                sg = sgf[j % 2]
                                sgk = "sgf%d" % (j % 2)
                                S.op("act", lambda e: e.activation(out=sg[:], in_=ps[j % 2][:, 0:NB], func=AF.Silu), reads=[PK[j % 2]], writes=[sgk])
                                S.op("dve", lambda e: e.tensor_tensor(out=actT[:, j, :], in0=ps[2 + j % 2][:, 0:NB], in1=sg[:], op=ALU.mult), reads=[PK[2 + j % 2], sgk], writes=["actT%d" % j])
                                for _ in range(3):
                                    if pending:
                                        pending.pop(0)()
                            while pending:
                                pending.pop(0)()
                            after_in()
                            ak = ["actT%d" % j for j in range(NJ)]
                            for oc in range(8):
                                wt = w2p[w2i[0] % 2]
                                wk = "w2p%d" % (w2i[0] % 2)
                                w2i[0] += 1
                                S.dma("sp", wt[:].rearrange("p a b -> p (a b)"), w2s[l, oc], reads=["w2s_%d_%d" % (l, oc)], writes=[wk])
                                pi = 4 + oc % 2
                                for j in range(NJ):
                                    S.op("pe", lambda e: e.matmul(ps[pi][:, 0:NB], lhsT=wt[:, j, :], rhs=actT[:, j, :], start=(j == 0), stop=(j == NJ - 1)),
                                         reads=[wk] + ak, writes=[PK[pi]], inc=(j == NJ - 1))
                                S.op("act", lambda e: e.activation(out=tmp8[:], in_=ps[pi][:, 0:NB], func=AF.Identity, scale=modT[:, l, 40 + oc, mc:mc + 1]), reads=[PK[pi], "modT"], writes=["tmp8"])
                                S.op("dve", lambda e: e.scalar_tensor_tensor(out=xT[:, oc, c0:c0 + NB], in0=xT[:, oc, c0:c0 + NB], scalar=ALPHA, in1=tmp8[:], op0=ALU.mult, op1=ALU.add),
                                     reads=[xk, "tmp8", "h2T"], writes=[xk])
                            ln_apply(blk, ln2w, ln2b, False)

                        make_h(blocks[0])
                        tp0 = front(blocks[0])
                        for t_ in tp0:
                            t_()
                        back(blocks[0])
                        for bi, blk in enumerate(blocks):
                            wout_ln1(blk, blocks[bi + 1] if bi + 1 < len(blocks) else None)
                            if bi + 1 < len(blocks):
                                nb_ = blocks[bi + 1]
                                pend = front(nb_)
                                ffn(blk, pend, lambda: back(nb_))
                            else:
                                ffn(blk, [], lambda: None)
                        if debug and b == 0 and l == 0:
                            S.dma("pool", dbg_mix[:, 0:2, :], yad[:, 0:2, :], reads=["yad_a%d" % i for i in range(NCH)])
                            S.dma("pool", dbg_mix[:, 6:8, :], yad[:, 2:4, :], reads=["yad_d"])
                            S.dma("sp", dbg_x, xT[:], reads=["xT%d" % i for i in range(NBLK)])
                        S.barrier()

            with ExitStack() as PO:
                otok = [tile(PO, "otok%d" % i, [128, D]) for i in range(2)]
                for ch in range(2, NCH):
                    ot = otok[ch % 2]
                    ok = "otok%d" % (ch % 2)
                    for half in range(2):
                        pi = (ch * 2 + half) % 4
                        for q in range(4):
                            fc = half * 4 + q
                            S.op("pe", lambda e: e.transpose(out=ps[pi][:, q * 128:(q + 1) * 128], in_=xT[:, fc, ch * 128:(ch + 1) * 128], identity=ident),
                                 reads=["xT%d" % (ch // 2), "consts"], writes=[PK[pi]], inc=(q == 3))
                        if half == 0:
                            S.op("dve", lambda e: e.tensor_copy(out=ot[:, 0:512], in_=ps[pi][:, :]), reads=[PK[pi]], writes=[ok])
                        else:
                            S.op("act", lambda e: e.activation(out=ot[:, 512:1024], in_=ps[pi][:, :], func=AF.Identity), reads=[PK[pi]], writes=[ok])
                    S.dma("sp", out_d[b, (ch - 2) * 128:(ch - 1) * 128, :], ot[:], reads=[ok])
                S.barrier()
        S.barrier()
    return nc


def _host_prep(inp):
    f = lambda a: np.ascontiguousarray(np.asarray(a, dtype=np.float32))
    w_in = f(inp["w_in"])
    sh = {}
    sh["w_mod"] = f(inp["w_mod"])
    sh["bmodT"] = f(np.asarray(inp["b_mod"]).reshape(2, 48, 128).transpose(2, 0, 1))
    colsA = np.concatenate([np.arange(0, 1040), np.arange(2064, 2320)])
    sh["w_inA"] = f(w_in[:, :, colsA].reshape(2, 8, 128, 1296).transpose(0, 2, 1, 3))
    sh["w_inB"] = f(w_in[:, :, 1040:2064].reshape(2, 8, 128, 1024).transpose(0, 2, 1, 3))
    sh["w_outh"] = f(np.asarray(inp["w_out"]).reshape(2, 8, 128, 1024).transpose(0, 2, 1, 3))
    w1 = np.asarray(inp["w_ffn_in"], dtype=np.float32).reshape(2, 8, 128, 2, NJ, 128)
    sh["w1h"] = f(w1.transpose(0, 4, 2, 1, 3, 5).reshape(2, NJ, 128, 8, 256))
    w2 = np.asarray(inp["w_ffn_out"], dtype=np.float32).reshape(2, NJ, 128, 8, 128)
    sh["w2h"] = f(w2.transpose(0, 3, 2, 1, 4))
    pp = np.zeros((2, 128, NPP), np.float32)
    r8 = lambda a: np.asarray(a).reshape(2, -1, 128).transpose(0, 2, 1)
    pp[:, :, 0:8] = r8(inp["ln1_w"]); pp[:, :, 8:16] = r8(inp["ln1_b"])
    pp[:, :, 16:24] = r8(inp["ln2_w"]); pp[:, :, 24:32] = r8(inp["ln2_b"])
    pp[:, :, 32:94] = np.asarray(inp["conv_w"]).reshape(2, 31, 2, 128).transpose(0, 3, 2, 1).reshape(2, 128, 62)
    pp[:, :, 94:96] = r8(inp["conv_b"]); pp[:, :, 96:98] = r8(inp["conv_ln_w"]); pp[:, :, 98:100] = r8(inp["conv_ln_b"])
    pp[:, :, 100:102] = r8(inp["s5_d"]); pp[:, :, 102:104] = r8(inp["s5_glu_b"])
    sh["pp"] = pp
    sh["bc"] = f(np.concatenate([np.asarray(inp["mlstm_gate_bias"]), np.asarray(inp["mlstm_norm_w"]),
                                 np.asarray(inp["sgu_ln_w"]), np.asarray(inp["sgu_ln_b"])], axis=1))
    sh["sguwT"] = f(np.asarray(inp["sgu_w"]).transpose(0, 3, 1, 2))
    sb = np.asarray(inp["sgu_b"])
    sh["sgub"] = f(np.repeat(sb.reshape(2, 2, 2, 1, 128), 64, axis=3).reshape(2, 2, 128, 128).transpose(0, 2, 1, 3))
    bre, bim = np.asarray(inp["s5_b_re"]), np.asarray(inp["s5_b_im"])
    are, aim, ldt = np.asarray(inp["s5_a_re"]), np.asarray(inp["s5_a_im"]), np.asarray(inp["s5_log_dt"])
    s5B = np.zeros((2, 128, 2, 2, 128), np.float32)
    s5A = np.zeros((2, 128, 3, 2, 2, 128), np.float32)
    for chn in range(2):
        for ppi in range(4):
            for g2 in range(2):
                g = chn * 8 + ppi * 2 + g2
                rows = slice(ppi * 32 + g2 * 16, ppi * 32 + g2 * 16 + 16)
                s5B[:, rows, 0, chn, g2 * 64:(g2 + 1) * 64] = bre[:, g].transpose(0, 2, 1)
                s5B[:, rows, 1, chn, g2 * 64:(g2 + 1) * 64] = bim[:, g].transpose(0, 2, 1)
            for g2c in range(2):
                gc = chn * 8 + ppi * 2 + g2c
                rows = slice(ppi * 32, ppi * 32 + 32)
                for d in range(2):
                    s5A[:, rows, 0, d, chn, g2c * 64:(g2c + 1) * 64] = are[:, d, gc][:, None, :]
                    s5A[:, rows, 1, d, chn, g2c * 64:(g2c + 1) * 64] = aim[:, d, gc][:, None, :]
                    s5A[:, rows, 2, d, chn, g2c * 64:(g2c + 1) * 64] = ldt[:, d, gc][:, None, None]
    sh["s5B"], sh["s5A"] = s5B, s5A
    cre, cim = np.asarray(inp["s5_c_re"]), np.asarray(inp["s5_c_im"])
    s5C = np.zeros((2, 128, 2, 8, 32), np.float32)
    s5Bn = np.zeros((2, 128, 2, 8, 32), np.float32)
    s5pp = np.zeros((2, 128, 3, 16), np.float32)
    for pair in range(8):
        for g2 in range(2):
            g = pair * 2 + g2
            s5C[:, g2 * 64:(g2 + 1) * 64, 0, pair, g2 * 16:(g2 + 1) * 16] = cre[:, g].transpose(0, 2, 1)
            s5C[:, g2 * 64:(g2 + 1) * 64, 1, pair, g2 * 16:(g2 + 1) * 16] = cim[:, g].transpose(0, 2, 1)
            s5Bn[:, g2 * 64:(g2 + 1) * 64, 0, pair, g2 * 16:(g2 + 1) * 16] = bre[:, g]
            s5Bn[:, g2 * 64:(g2 + 1) * 64, 1, pair, g2 * 16:(g2 + 1) * 16] = bim[:, g]
            for d in range(2):
                s5pp[:, g2 * 64:(g2 + 1) * 64, 0, d * 8 + pair] = are[:, d, g]
                s5pp[:, g2 * 64:(g2 + 1) * 64, 1, d * 8 + pair] = aim[:, d, g]
                s5pp[:, g2 * 64:(g2 + 1) * 64, 2, d * 8 + pair] = ldt[:, d, g][:, None]
    sh["s5C"], sh["s5pp"], sh["s5Bn"] = s5C, s5pp, s5Bn
    sh["gluh"] = f(np.asarray(inp["s5_glu_w"]).reshape(2, 2, 128, 256).transpose(0, 2, 1, 3))
    cst = np.zeros((128, 3, 128), np.float32)
    cst[:, 0] = np.eye(128)
    cst[:, 1] = np.triu(np.ones((128, 128)))
    cst[:, 2] = np.tril(np.ones((128, 128)))
    sh["consts"] = cst
    return sh


def _core_inputs(inp, shared, core):
    x, c, ctx, c_ctx = (np.asarray(inp[k], dtype=np.float32) for k in ("x", "c", "ctx", "c_ctx"))
    m = dict(shared)
    bs = [2 * core, 2 * core + 1]
    m["xin"] = np.ascontiguousarray(np.concatenate([ctx[bs], x[bs]], axis=1))
    cv = np.stack([c[bs[0]], c[bs[1]], c_ctx], axis=1)
    m["cT"] = np.ascontiguousarray(cv.reshape(8, 128, 3).transpose(1, 0, 2))
    return m


def kernel(**inputs):
    shared = _host_prep(inputs)
    nc = build_nc()
    in_maps = [_core_inputs(inputs, shared, core) for core in range(8)]
    res = run_bass_kernel_spmd(nc, in_maps, core_ids=list(range(8)))
    out = np.concatenate([np.asarray(r["out"]) for r in res.results], axis=0)
    return out.astype(np.float32)
```
